# Optimizing a Trainium2 kernel written in Bass

```python
import math
import jax, jax.numpy as jnp
from jax import lax
import numpy as np

D_MODEL = 1024
BATCH = 2
SEQ = 16384
DEPTH = 1
DEC_BATCH = 2
DEC_SEQ = 8192
PAST_LEN = 128

GRID_W = 64
SSM_EXPAND = 2
D_INNER = SSM_EXPAND * D_MODEL
SSM_HEAD_DIM = 64
SSM_HEADS = D_INNER // SSM_HEAD_DIM
SSM_GROUPS = 4
SSM_HEADS_PER_GROUP = SSM_HEADS // SSM_GROUPS
D_STATE = 128
D_CONV = 5
CHUNK = 128
CONV_DIM = D_INNER + 2 * SSM_GROUPS * D_STATE
ATTN_HEAD_DIM = 128
N_Q_HEADS = 8
N_KV_HEADS = 2
Q_PER_KV = N_Q_HEADS // N_KV_HEADS
ATTN_WIDTH = N_Q_HEADS * ATTN_HEAD_DIM
KV_WIDTH = N_KV_HEADS * ATTN_HEAD_DIM
Q_BLOCK = 128
ROPE_THETA = 10000.0
D_FF = 2816
N_BRANCH = 2
EPS = 1e-6
IN_SIZES = (D_INNER, CONV_DIM, SSM_HEADS, SSM_HEADS, ATTN_WIDTH, KV_WIDTH, KV_WIDTH, N_BRANCH * D_MODEL)
IN_PROJ_DIM = sum(IN_SIZES)
IN_SPLITS = tuple(int(v) for v in np.cumsum(IN_SIZES)[:-1])

kernel_name = 'hybrid_ssd_axial_gqa_macaron_encoder'


def rmsnorm(x, g):
    xf = x.astype(jnp.float32)
    y = xf * lax.rsqrt(jnp.mean(xf * xf, axis=-1, keepdims=True) + EPS)
    return (y * g.astype(jnp.float32)).astype(x.dtype)


def swiglu(x, w_gu, w_down):
    g, u = jnp.split(x @ w_gu, 2, axis=-1)
    return (jax.nn.silu(g) * u) @ w_down


def depthwise_conv_centred(x, w, b):
    c = x.shape[-1]
    pad = D_CONV // 2
    out = lax.conv_general_dilated(x, w[:, None, :].astype(x.dtype), window_strides=(1,),
                                   padding=[(pad, pad)], dimension_numbers=('NWC', 'WIO', 'NWC'),
                                   feature_group_count=c)
    return out + b.astype(x.dtype)


def ssd_scan(x, dt, A, B, C):
    f32 = jnp.float32
    b, L, H, P = x.shape
    nc = L // CHUNK
    G, E = SSM_GROUPS, SSM_HEADS_PER_GROUP
    xc = (x.astype(f32) * dt[..., None]).reshape(b, nc, CHUNK, G, E, P)
    Bc = B.astype(f32).reshape(b, nc, CHUNK, G, D_STATE)
    Cc = C.astype(f32).reshape(b, nc, CHUNK, G, D_STATE)
    a = (dt * A).reshape(b, nc, CHUNK, G, E).transpose(0, 1, 3, 4, 2)
    acum = jnp.cumsum(a, axis=-1)
    tril = jnp.tril(jnp.ones((CHUNK, CHUNK), dtype=bool))
    seg = acum[..., :, None] - acum[..., None, :]
    Lmat = jnp.exp(jnp.where(tril, seg, -jnp.inf))
    CB = jnp.einsum('bclgn,bcsgn->bcgls', Cc, Bc)
    y_diag = jnp.einsum('bcgls,bcgels,bcsgep->bclgep', CB, Lmat, xc)
    decay_states = jnp.exp(acum[..., -1:] - acum)
    states = jnp.einsum('bclgn,bcgel,bclgep->bcgepn', Bc, decay_states, xc)
    chunk_decay = jnp.exp(acum[..., -1])

    def step(h, inp):
        s_c, d_c = inp
        return h * d_c[..., None, None] + s_c, h

    h0 = jnp.zeros((b, G, E, P, D_STATE), f32)
    _, prev = lax.scan(step, h0, (states.transpose(1, 0, 2, 3, 4, 5), chunk_decay.transpose(1, 0, 2, 3)))
    prev = prev.transpose(1, 0, 2, 3, 4, 5)
    y_off = jnp.einsum('bclgn,bcgepn,bcgel->bclgep', Cc, prev, jnp.exp(acum))
    return (y_diag + y_off).reshape(b, L, H, P)


def ssd_branch(z, xBC, dt_f, dt_b, conv_w, conv_b, dt_bias_f, dt_bias_b, A_log_f, A_log_b, D_skip, ssm_norm):
    f32 = jnp.float32
    b, L, _ = xBC.shape
    xBC = jax.nn.silu(depthwise_conv_centred(xBC, conv_w, conv_b))
    xs, Bm, Cm = jnp.split(xBC, [D_INNER, D_INNER + SSM_GROUPS * D_STATE], axis=-1)
    xs = xs.reshape(b, L, SSM_HEADS, SSM_HEAD_DIM)
    Bm = Bm.reshape(b, L, SSM_GROUPS, D_STATE)
    Cm = Cm.reshape(b, L, SSM_GROUPS, D_STATE)
    dtf = jax.nn.softplus(dt_f.astype(f32) + dt_bias_f.astype(f32))
    dtb = jax.nn.softplus(dt_b.astype(f32) + dt_bias_b.astype(f32))
    Af = -jnp.exp(A_log_f.astype(f32))
    Ab = -jnp.exp(A_log_b.astype(f32))
    flip = lambda t: jnp.flip(t, axis=1)
    y_fwd = ssd_scan(xs, dtf, Af, Bm, Cm)
    y_bwd = flip(ssd_scan(flip(xs), flip(dtb), Ab, flip(Bm), flip(Cm)))
    y = y_fwd + y_bwd + D_skip.astype(f32)[:, None] * xs.astype(f32)
    y = y.reshape(b, L, D_INNER) * jax.nn.silu(z.astype(f32))
    y = y.reshape(b, L, SSM_GROUPS, D_INNER // SSM_GROUPS)
    y = y * lax.rsqrt(jnp.mean(y * y, axis=-1, keepdims=True) + EPS)
    return (y.reshape(b, L, D_INNER) * ssm_norm.astype(f32)).astype(z.dtype)


def axial_rope_tables(L):
    rows = L // GRID_W
    row = jnp.repeat(jnp.arange(rows, dtype=jnp.float32), GRID_W)
    col = jnp.tile(jnp.arange(GRID_W, dtype=jnp.float32), rows)
    axis_dim = ATTN_HEAD_DIM // 2
    inv_freq = ROPE_THETA ** (-jnp.arange(0, axis_dim, 2, dtype=jnp.float32) / axis_dim)
    ang = jnp.concatenate([row[:, None] * inv_freq, col[:, None] * inv_freq], axis=-1)
    return jnp.cos(ang), jnp.sin(ang)


def apply_rope(x, cos, sin):
    xf = x.astype(jnp.float32).reshape(*x.shape[:-1], ATTN_HEAD_DIM // 2, 2)
    x0, x1 = xf[..., 0], xf[..., 1]
    c, s = cos[:, None, :], sin[:, None, :]
    out = jnp.stack([x0 * c - x1 * s, x0 * s + x1 * c], axis=-1)
    return out.reshape(x.shape).astype(x.dtype)


def block_attention(q, k, v):
    b, L, _, dh = q.shape
    nblk = L // Q_BLOCK
    qb = q.reshape(b, nblk, Q_BLOCK, N_KV_HEADS, Q_PER_KV, dh).transpose(1, 0, 2, 3, 4, 5)
    scale = dh ** -0.5

    def one_block(qblk):
        s = jnp.einsum('bqhgd,bshd->bhgqs', qblk, k, preferred_element_type=jnp.float32) * scale
        p = jax.nn.softmax(s, axis=-1)
        return jnp.einsum('bhgqs,bshd->bqhgd', p.astype(v.dtype), v)

    o = lax.map(one_block, qb)
    return o.transpose(1, 0, 2, 3, 4, 5).reshape(b, L, ATTN_WIDTH)


def encoder_layer(x, norm_ffn1, w_ffn1_gu, w_ffn1_down, norm_mix, w_in, conv_w, conv_b,
                  dt_bias_f, dt_bias_b, A_log_f, A_log_b, D_skip, ssm_norm, q_norm, k_norm,
                  w_ssm_branch, w_attn_branch, b_gate, w_out, norm_ffn2, w_ffn2_gu, w_ffn2_down):
    b, L, _ = x.shape
    h = x + 0.5 * swiglu(rmsnorm(x, norm_ffn1), w_ffn1_gu, w_ffn1_down)
    u = rmsnorm(h, norm_mix)
    z, xBC, dt_f, dt_b, q, k, v, gate_logits = jnp.split(u @ w_in, IN_SPLITS, axis=-1)
    s = ssd_branch(z, xBC, dt_f, dt_b, conv_w, conv_b, dt_bias_f, dt_bias_b, A_log_f, A_log_b, D_skip, ssm_norm)
    q = rmsnorm(q.reshape(b, L, N_Q_HEADS, ATTN_HEAD_DIM), q_norm)
    k = rmsnorm(k.reshape(b, L, N_KV_HEADS, ATTN_HEAD_DIM), k_norm)
    v = v.reshape(b, L, N_KV_HEADS, ATTN_HEAD_DIM)
    cos, sin = axial_rope_tables(L)
    a = block_attention(apply_rope(q, cos, sin), apply_rope(k, cos, sin), v)
    g = jax.nn.sigmoid((gate_logits + b_gate).astype(jnp.float32)).astype(x.dtype)
    g_s, g_a = jnp.split(g, N_BRANCH, axis=-1)
    m = g_s * (s @ w_ssm_branch) + g_a * (a @ w_attn_branch)
    h = h + m @ w_out
    return h + 0.5 * swiglu(rmsnorm(h, norm_ffn2), w_ffn2_gu, w_ffn2_down)


def run_trunk(x, norm_ffn1, w_ffn1_gu, w_ffn1_down, norm_mix, w_in, conv_w, conv_b,
              dt_bias_f, dt_bias_b, A_log_f, A_log_b, D_skip, ssm_norm, q_norm, k_norm,
              w_ssm_branch, w_attn_branch, b_gate, w_out, norm_ffn2, w_ffn2_gu, w_ffn2_down, norm_final):
    h = x
    for i in range(DEPTH):
        h = encoder_layer(h, norm_ffn1[i], w_ffn1_gu[i], w_ffn1_down[i], norm_mix[i], w_in[i],
                          conv_w[i], conv_b[i], dt_bias_f[i], dt_bias_b[i], A_log_f[i], A_log_b[i],
                          D_skip[i], ssm_norm[i], q_norm[i], k_norm[i], w_ssm_branch[i],
                          w_attn_branch[i], b_gate[i], w_out[i], norm_ffn2[i], w_ffn2_gu[i], w_ffn2_down[i])
    return rmsnorm(h, norm_final)


def setup_inputs(seed: int = 0) -> dict:
    key = jax.random.key(seed)
    ks = jax.random.split(key, 32)
    f32 = jnp.float32
    Ld = DEPTH

    def nrm(k, shape, fan_in):
        return jax.random.normal(k, shape, f32) * (fan_in ** -0.5)

    def gain(k, shape):
        return 1.0 + 0.02 * jax.random.normal(k, shape, f32)

    def dt_bias_init(k):
        dt0 = jnp.exp(jax.random.uniform(k, (Ld, SSM_HEADS), f32, minval=math.log(1e-3), maxval=math.log(1e-1)))
        return dt0 + jnp.log(-jnp.expm1(-dt0))

    def a_log_init(k):
        return jnp.log(jax.random.uniform(k, (Ld, SSM_HEADS), f32, minval=1.0, maxval=16.0))

    return {
        'x_prompt': jax.random.normal(ks[0], (BATCH, SEQ, D_MODEL), f32),
        'x_sample': jax.random.normal(ks[1], (DEC_BATCH, DEC_SEQ, D_MODEL), f32),
        'norm_ffn1': gain(ks[2], (Ld, D_MODEL)),
        'w_ffn1_gu': nrm(ks[3], (Ld, D_MODEL, 2 * D_FF), D_MODEL),
        'w_ffn1_down': nrm(ks[4], (Ld, D_FF, D_MODEL), D_FF),
        'norm_mix': gain(ks[5], (Ld, D_MODEL)),
        'w_in': nrm(ks[6], (Ld, D_MODEL, IN_PROJ_DIM), D_MODEL),
        'conv_w': nrm(ks[7], (Ld, D_CONV, CONV_DIM), D_CONV),
        'conv_b': 0.01 * jax.random.normal(ks[8], (Ld, CONV_DIM), f32),
        'dt_bias_f': dt_bias_init(ks[9]),
        'dt_bias_b': dt_bias_init(ks[10]),
        'A_log_f': a_log_init(ks[11]),
        'A_log_b': a_log_init(ks[12]),
        'D_skip': gain(ks[13], (Ld, SSM_HEADS)),
        'ssm_norm': gain(ks[14], (Ld, D_INNER)),
        'q_norm': gain(ks[15], (Ld, ATTN_HEAD_DIM)),
        'k_norm': gain(ks[16], (Ld, ATTN_HEAD_DIM)),
        'w_ssm_branch': nrm(ks[17], (Ld, D_INNER, D_MODEL), D_INNER),
        'w_attn_branch': nrm(ks[18], (Ld, ATTN_WIDTH, D_MODEL), ATTN_WIDTH),
        'b_gate': 0.01 * jax.random.normal(ks[19], (Ld, N_BRANCH * D_MODEL), f32),
        'w_out': nrm(ks[20], (Ld, D_MODEL, D_MODEL), D_MODEL),
        'norm_ffn2': gain(ks[21], (Ld, D_MODEL)),
        'w_ffn2_gu': nrm(ks[22], (Ld, D_MODEL, 2 * D_FF), D_MODEL),
        'w_ffn2_down': nrm(ks[23], (Ld, D_FF, D_MODEL), D_FF),
        'norm_final': gain(ks[24], (D_MODEL,)),
    }


def reference(x_prompt, x_sample, norm_ffn1, w_ffn1_gu, w_ffn1_down, norm_mix, w_in, conv_w, conv_b,
              dt_bias_f, dt_bias_b, A_log_f, A_log_b, D_skip, ssm_norm, q_norm, k_norm,
              w_ssm_branch, w_attn_branch, b_gate, w_out, norm_ffn2, w_ffn2_gu, w_ffn2_down, norm_final):
    y_prompt = run_trunk(x_prompt, norm_ffn1, w_ffn1_gu, w_ffn1_down, norm_mix, w_in, conv_w, conv_b,
                         dt_bias_f, dt_bias_b, A_log_f, A_log_b, D_skip, ssm_norm, q_norm, k_norm,
                         w_ssm_branch, w_attn_branch, b_gate, w_out, norm_ffn2, w_ffn2_gu, w_ffn2_down, norm_final)
    y_sample = run_trunk(x_sample, norm_ffn1, w_ffn1_gu, w_ffn1_down, norm_mix, w_in, conv_w, conv_b,
                         dt_bias_f, dt_bias_b, A_log_f, A_log_b, D_skip, ssm_norm, q_norm, k_norm,
                         w_ssm_branch, w_attn_branch, b_gate, w_out, norm_ffn2, w_ffn2_gu, w_ffn2_down, norm_final)
    return (y_prompt, y_sample)
```

```python
import contextlib
import math
import numpy as np
import concourse.bass as bass
import concourse.mybir as mybir
from concourse.bass_utils import run_bass_kernel_spmd

F32 = mybir.dt.float32
BF16 = mybir.dt.bfloat16
AF = mybir.ActivationFunctionType
ALU = mybir.AluOpType
AX = mybir.AxisListType

D = 1024
DFF = 2816
NFT = DFF // 128
EPS = 1e-6
D_INNER = 2048
NH = 32
HP = 64
NG = 4
DS = 128
CONV_DIM = 3072
GRID_W = 64
IN_SIZES = (2048, 3072, 32, 32, 1024, 256, 256, 2048)
IN_OFF = [0]
for _s in IN_SIZES:
    IN_OFF.append(IN_OFF[-1] + _s)
OFF_Z, OFF_XBC, OFF_DTF, OFF_DTB, OFF_Q, OFF_K, OFF_V, OFF_G = IN_OFF[:8]
NEG = -30000.0


class Buf:
    __slots__ = ("w", "rc", "rd")

    def __init__(self, w=None):
        self.w = w
        self.rc = {}
        self.rd = []


class Op:
    __slots__ = ("eng", "fn", "dma", "deps", "flag", "sem", "val", "idx", "waits", "cc")


ENGS = ("pe", "act", "dve", "pool", "sp")


class KB:
    def __init__(self, nc, es):
        self.nc = nc
        self.ops = {e: [] for e in ENGS}
        self.csem = {e: es.enter_context(nc.semaphore("c_" + e)) for e in ENGS}
        self.npool = {"sp": 10, "pool": 6, "act": 4}
        self.dsem = {q: [es.enter_context(nc.semaphore(f"d_{q}{i}")) for i in range(n)]
                     for q, n in self.npool.items()}
        self.ccsems = [es.enter_context(nc.semaphore(f"cc{i}")) for i in range(16)]
        self.dcnt = {q: 0 for q in self.npool}
        self.dlast = {q: [None] * n for q, n in self.npool.items()}
        self.barrier_op = None
        self.bufs = []

    def buf(self):
        b = Buf(self.barrier_op)
        self.bufs.append(b)
        return b

    def op(self, eng, fn, reads=(), writes=(), dma=False):
        o = Op()
        o.eng, o.fn, o.dma, o.flag, o.deps = eng, fn, dma, False, []
        o.sem = None
        o.val = 0
        o.cc = False

        def dep(p, raw):
            if p is None:
                return
            if (not p.dma) and (not dma) and p.eng == eng:
                if not (raw and eng != "pe"):
                    return
            o.deps.append(p)

        for b in reads:
            dep(b.w, True)
        for b in writes:
            dep(b.w, False)
            for r in b.rc.values():
                dep(r, False)
            for r in b.rd:
                dep(r, False)
        for b in reads:
            if dma:
                b.rd.append(o)
            else:
                b.rc[eng] = o
        for b in writes:
            b.w = o
            b.rc = {}
            b.rd = []
        self.ops[eng].append(o)
        o.idx = len(self.ops[eng])
        if dma:
            n = self.dcnt[eng]
            kq = self.npool[eng]
            slot = n % kq
            o.sem = self.dsem[eng][slot]
            o.val = 16 * (n // kq + 1)
            prev = self.dlast[eng][slot]
            if prev is not None:
                o.deps.append(prev)
            self.dlast[eng][slot] = o
            self.dcnt[eng] = n + 1
        return o

    def cc(self, in_h, out_h, R, W):
        if not hasattr(self, "ccops"):
            self.ccops = []
        i = len(self.ccops)
        sem = self.ccsems[i]
        groups = [[0, 1, 2, 3], [4, 5, 6, 7]]
        o = self.op("pool", lambda e: e.collective_compute("AllGather", ALU.bypass, replica_groups=groups,
                                                           ins=[in_h.ap().opt()], outs=[out_h.ap().opt()]), R, W)
        o.dma = True
        o.cc = True
        o.sem = sem
        o.val = 1
        for b in R:
            if b.rc.get("pool") is o:
                del b.rc["pool"]
                b.rd.append(o)
        self.ccops.append(o)
        return o

    def barrier(self):
        allb = self.bufs
        sc = self._bar_scratch
        o = self.op("dve", lambda e: e.memset(sc[:, 0:1], 0.0), reads=(), writes=allb)
        for q in self.dlast:
            for p in self.dlast[q]:
                if p is not None:
                    o.deps.append(p)
        for p in getattr(self, "ccops", []):
            o.deps.append(p)
        for e in ENGS:
            if e != "dve" and self.ops[e]:
                for last in reversed(self.ops[e]):
                    if not last.dma:
                        o.deps.append(last)
                        break
        self.barrier_op = o
        self.bufs = []
        return o

    def finalize(self):
        for e in ENGS:
            seen = {}
            for o in self.ops[e]:
                waits = []
                for p in o.deps:
                    if p.dma:
                        key, val = id(p.sem), p.val
                    else:
                        key, val = p.eng, p.idx
                    if seen.get(key, 0) >= val:
                        continue
                    seen[key] = val
                    waits.append(p)
                    p.flag = True
                o.waits = waits
        for e in ENGS:
            c = 0
            for o in self.ops[e]:
                if not o.dma and o.flag:
                    c += 1
                    o.val = c
                    o.sem = self.csem[e]

    def emit(self, eng, h):
        for o in self.ops[eng]:
            for p in o.waits:
                h.wait_ge(p.sem, p.val)
            ins = o.fn(h)
            if o.cc:
                ins.then_inc(o.sem)
            elif o.dma:
                ins.then_inc(o.sem, 16)
            elif o.flag:
                ins.then_inc(o.sem, 1)
        if eng == "sp":
            for q in self.dlast:
                for p in self.dlast[q]:
                    if p is not None:
                        h.wait_ge(p.sem, p.val)

    def mm(self, out, lhsT, rhs, start, stop, R, W):
        return self.op("pe", lambda e: e.matmul(out, lhsT=lhsT, rhs=rhs, start=start, stop=stop), R, W)

    def tr(self, out, in_, ident, R, W):
        return self.op("pe", lambda e: e.transpose(out, in_, ident), R, W)

    def act(self, out, in_, func, R, W, bias=None, scale=None, eng="act"):
        kw = {}
        if bias is not None:
            kw["bias"] = bias
        if scale is not None:
            kw["scale"] = scale
        return self.op(eng, lambda e: e.activation(out=out, in_=in_, func=func, **kw), R, W)

    def tt(self, eng, out, in0, in1, op, R, W):
        return self.op(eng, lambda e: e.tensor_tensor(out=out, in0=in0, in1=in1, op=op), R, W)

    def ts(self, eng, out, in0, s1, s2, op0, op1, R, W):
        if op1 is None:
            return self.op(eng, lambda e: e.tensor_scalar(out=out, in0=in0, scalar1=s1, scalar2=None, op0=op0), R, W)
        return self.op(eng, lambda e: e.tensor_scalar(out=out, in0=in0, scalar1=s1, scalar2=s2, op0=op0, op1=op1), R, W)

    def stt(self, out, in0, scalar, in1, op0, op1, R, W):
        return self.op("dve", lambda e: e.scalar_tensor_tensor(out=out, in0=in0, scalar=scalar, in1=in1,
                                                                op0=op0, op1=op1), R, W)

    def copy(self, eng, out, in_, R, W):
        if eng == "act":
            return self.op("act", lambda e: e.activation(out=out, in_=in_, func=AF.Copy), R, W)
        return self.op(eng, lambda e: e.tensor_copy(out=out, in_=in_), R, W)

    def memset(self, eng, ap, val, W):
        return self.op(eng, lambda e: e.memset(ap, val), (), W)

    def recip(self, out, in_, R, W):
        return self.op("dve", lambda e: e.reciprocal(out=out, in_=in_), R, W)

    def dma(self, q, out, in_, R, W):
        return self.op(q, lambda e: e.dma_start(out=out, in_=in_), R, W, dma=True)


class Prog:
    def __init__(self, jobs, debug=None, cc=True):
        self.jobs = jobs
        self.cc = cc
        self.debug = debug or ()
        self.LT = sum(L for L, _ in jobs)
        self.LOT = sum(LO for _, LO in jobs)
        if cc:
            self.LT = self.LOT

    def dram(self, name, shape, dt, kind="Internal"):
        if name in self.debug:
            kind = "ExternalOutput"
        return self.nc.dram_tensor(name, list(shape), dt, kind=kind).ap()

    def build(self):
        nc = bass.Bass("TRN2", target_bir_lowering=False)
        self.nc = nc
        LT, LOT = self.LT, self.LOT
        I = {}

        def inp(name, shape, dt=F32):
            I[name] = nc.dram_tensor(name, list(shape), dt, kind="ExternalInput").ap()

        inp("x", [LT, D])
        inp("w1gu", [D, 2 * DFF]); inp("w1d", [DFF, D]); inp("g1", [128, 8])
        inp("w2gu", [D, 2 * DFF]); inp("w2d", [DFF, D]); inp("g2", [128, 8])
        inp("gfin", [128, 8])
        inp("win", [D, 8768]); inp("gmix", [128, 8]); inp("gq", [128, 1]); inp("gk", [128, 1])
        self.TB = min(1024, min(LO for _, LO in self.jobs))
        self.nblk_total = sum(L // self.TB for L, _ in self.jobs)
        inp("convw", [128, 24, 5]); inp("convb", [128, 24]); (None if self.cc else inp("cmask", [128, 2 * self.nblk_total]))
        inp("dtbias", [128, 64]); inp("alog", [128, 64]); inp("dskip", [128, 32]); inp("ssmnorm", [128, 2048])
        (None if self.cc else inp("smask", [128, 8 * len(self.jobs)])); inp("tri", [128, 256]); inp("negm", [128, 1024])
        inp("wsb", [2048, D]); inp("wab", [D, D]); inp("wout", [D, D])
        inp("bgate", [128, 16]); inp("cosT", [128, LT]); inp("sinT", [128, LT]); inp("pm", [128, 128])
        self.I = I
        self.y = nc.dram_tensor("y", [LOT, D], F32, kind="ExternalOutput").ap()
        self.S_h = self.dram("S_h", [D, LT], F32)
        self.S_xpre_l = [self.dram(f"S_xpre{i}", [CONV_DIM, LO if self.cc else L], F32)
                         for i, (L, LO) in enumerate(self.jobs)]
        if self.cc:
            inp("ccmask", [128, 64])
            nj = len(self.jobs)
            dt_ = lambda n, sh, d: nc.dram_tensor(n, list(sh), d)
            self.E_in = dt_("E_in", [128, 24 * nj * 4], F32)
            self.E_out = dt_("E_out", [4 * 128, 24 * nj * 4], F32)
            self.T_in = dt_("T_in", [128, 64 * nj], F32)
            self.T_out = dt_("T_out", [4 * 128, 64 * nj], F32)
            self.K_in = [[dt_(f"K_in{j}_{kv}", [128, LO], BF16) for kv in range(2)] for j, (_, LO) in enumerate(self.jobs)]
            self.K_out = [[dt_(f"K_out{j}_{kv}", [4 * 128, LO], BF16) for kv in range(2)] for j, (_, LO) in enumerate(self.jobs)]
            self.V_in = [[dt_(f"V_in{j}_{kv}", [LO, 128], BF16) for kv in range(2)] for j, (_, LO) in enumerate(self.jobs)]
            self.V_out = [[dt_(f"V_out{j}_{kv}", [4 * LO, 128], BF16) for kv in range(2)] for j, (_, LO) in enumerate(self.jobs)]
            self.St_in = [[dt_(f"St_in{j}_{d}", [128, 2048], F32) for d in range(2)] for j in range(nj)]
            self.St_out = [[dt_(f"St_out{j}_{d}", [4 * 128, 2048], F32) for d in range(2)] for j in range(nj)]
            self.bcc = {}
        self.S_dt = self.dram("S_dt", [LT, 64], F32)
        self.S_v = self.dram("S_v", [LT, 256], BF16)
        self.S_kT = self.dram("S_kT", [256, LT], BF16)
        self.S_sz = self.dram("S_sz", [LOT, 2048], F32)
        self.S_bcT = self.dram("S_bcT", [1024, LT], BF16)
        self.S_y = self.dram("S_y", [LOT, 2048], F32)
        self.S_aT = self.dram("S_aT", [1024, LOT], BF16)
        self.S_h2 = self.dram("S_h2", [D, LOT], F32)
        self.S_sT = self.dram("S_sT", [2048, LOT], BF16)
        self.S_xtok = self.dram("S_xtok", [LT, 2048], BF16)
        self.S_Btok = self.dram("S_Btok", [LT, 512], BF16)
        self.S_qT = self.dram("S_qT", [1024, LOT], BF16)
        self.S_gT = self.dram("S_gT", [2048, LOT], F32)
        with contextlib.ExitStack() as es:
            self.es = es
            k = KB(nc, es)
            self.k = k
            self.psall = es.enter_context(nc.psum_tensor("psall", [128, 8, 512], F32))
            self.ps = [self.psall[:, i, :] for i in range(8)]
            self.psb = [k.buf() for _ in range(8)]
            self.psi = 0
            cst = es.enter_context(nc.sbuf_tensor("cst", [128, 704], F32))
            cstb = es.enter_context(nc.sbuf_tensor("cstb", [128, 512], BF16))
            k._bar_scratch = cst[:, 700:701]
            self.cB = k.buf()
            self.identF = cst[:, 0:128]
            self.epsT = cst[:, 128:129]
            self.onesB = cstb[:, 0:128]
            self.identB = cstb[:, 128:256]
            k.memset("dve", cst[:, 0:128], 0.0, [self.cB])
            k.memset("dve", cst[:, 128:129], EPS, [self.cB])
            k.memset("dve", cstb[:, 0:128], 1.0, [self.cB])
            k.memset("pool", cst[:, 256:384], 1.0, [self.cB])
            k.op("pool", lambda e: e.affine_select(out=cst[:, 0:128], in_=cst[:, 256:384], pattern=[[-1, 128]],
                                                   compare_op=ALU.is_equal, fill=0.0, base=0,
                                                   channel_multiplier=1), [self.cB], [self.cB])
            k.copy("dve", cstb[:, 128:256], cst[:, 0:128], [self.cB], [self.cB])
            self.PmF = cst[:, 384:512]
            self.onesF = cst[:, 512:640]
            self.oneT = cst[:, 512:513]
            k.memset("dve", cst[:, 512:640], 1.0, [self.cB])
            k.dma("sp", cst[:, 384:512], I["pm"][:, :], [], [self.cB])
            k.barrier()
            if self.cc:
                self.body2()
            else:
                self.body()
            k.barrier()
            k.finalize()
            with nc.Block() as block:
                @block.tensor
                def _(h):
                    k.emit("pe", h)

                @block.scalar
                def _(h):
                    k.emit("act", h)

                @block.vector
                def _(h):
                    k.emit("dve", h)

                @block.gpsimd
                def _(h):
                    k.emit("pool", h)

                @block.sync
                def _(h):
                    k.emit("sp", h)
        return nc

    def sbt(self, name, shape, dt):
        return self.nc.sbuf_tensor(f"{name}_{self.uid()}", shape, dt)

    def psum(self):
        i = self.psi
        self.psi = (i + 1) % 8
        return self.ps[i], self.psb[i]

    def jobof(self, g):
        o = 0
        for j, (L, LO) in enumerate(self.jobs):
            if g < o + LO:
                return j, g - o
            o += LO
        raise ValueError(g)

    def ccb(self, h):
        if id(h) not in self.bcc:
            self.bcc[id(h)] = Buf(None)
        return self.bcc[id(h)]

    def body2(self):
        k = self.k
        offs = []
        o = 0
        for (L, LO) in self.jobs:
            offs.append(o)
            o += LO
        nj = len(self.jobs)
        LOT = self.LOT
        with contextlib.ExitStack() as ph:
            self.ffn_phase(ph, "w1gu", "w1d", "g1", LOT, src=("tok", self.I["x"], 0), dst=("feat", self.S_h, 0))
        k.barrier()
        with contextlib.ExitStack() as ph:
            self.inproj_phase(ph, "a", LOT, 0, 0)
        k.barrier()
        with contextlib.ExitStack() as ph:
            self.inproj_phase(ph, "b", LOT, 0, 0)
        k.barrier()
        for j, (L, LO) in enumerate(self.jobs):
            xp = self.S_xpre_l[j]
            ev = self.E_in.ap().rearrange("p (c j e) -> p c j e", c=24, j=nj)
            rows = xp[:, :].rearrange("(c p) t -> p c t", p=128)
            deps = [self.bS(xp, t) for t in range(0, LO, 512)]
            k.dma("sp", ev[:, :, j, 0:2], rows[:, :, 0:2], deps, [self.ccb(self.E_in)])
            k.dma("sp", ev[:, :, j, 2:4], rows[:, :, LO - 2:LO], deps, [self.ccb(self.E_in)])
        k.cc(self.E_in, self.E_out, [self.ccb(self.E_in)], [self.ccb(self.E_out)])
        for j in range(nj):
            for kv in range(2):
                k.cc(self.K_in[j][kv], self.K_out[j][kv], [self.ccb(self.K_in[j][kv])], [self.ccb(self.K_out[j][kv])])
                k.cc(self.V_in[j][kv], self.V_out[j][kv], [self.ccb(self.V_in[j][kv])], [self.ccb(self.V_out[j][kv])])
        blk0 = 0
        for j, (L, LO) in enumerate(self.jobs):
            self.S_xpre = self.S_xpre_l[j]
            self.job = j
            with contextlib.ExitStack() as ph:
                self.conv_phase(ph, LO, LO, offs[j], blk0)
            k.barrier()
            with contextlib.ExitStack() as ph:
                self.ssd2_phase(ph, LO, offs[j], j, "local")
            k.barrier()
            blk0 += LO // self.TB
        k.cc(self.T_in, self.T_out, [self.ccb(self.T_in)], [self.ccb(self.T_out)])
        for j in range(nj):
            for d in range(2):
                k.cc(self.St_in[j][d], self.St_out[j][d], [self.ccb(self.St_in[j][d])], [self.ccb(self.St_out[j][d])])
        for j, (L, LO) in enumerate(self.jobs):
            self.job = j
            with contextlib.ExitStack() as ph:
                self.ssd2_phase(ph, LO, offs[j], j, "own")
            k.barrier()
            with contextlib.ExitStack() as ph:
                self.attn_phase(ph, L, LO, offs[j], offs[j])
            k.barrier()
        with contextlib.ExitStack() as ph:
            self.merge_phase(ph, LOT, 0, 0)
        k.barrier()
        with contextlib.ExitStack() as ph:
            self.ffn_phase(ph, "w2gu", "w2d", "g2", LOT, src=("feat", self.S_h2, 0), dst=("final", self.y, 0))
        k.barrier()

    def body(self):
        off = 0
        ooff = 0
        blk0 = 0
        job = 0
        for (L, LO) in self.jobs:
            self.S_xpre = self.S_xpre_l[job]
            with contextlib.ExitStack() as ph:
                self.ffn_phase(ph, "w1gu", "w1d", "g1", L, src=("tok", self.I["x"], off),
                               dst=("feat", self.S_h, off))
            self.k.barrier()
            with contextlib.ExitStack() as ph:
                self.inproj_phase(ph, "a", L, off, ooff)
            self.k.barrier()
            with contextlib.ExitStack() as ph:
                self.inproj_phase(ph, "b", LO, off, ooff)
            self.k.barrier()
            with contextlib.ExitStack() as ph:
                self.conv_phase(ph, L, LO, off, blk0)
            self.k.barrier()
            with contextlib.ExitStack() as ph:
                self.ssd_phase(ph, L, LO, off, ooff, job)
            self.k.barrier()
            with contextlib.ExitStack() as ph:
                self.attn_phase(ph, L, LO, off, ooff)
            self.k.barrier()
            with contextlib.ExitStack() as ph:
                self.merge_phase(ph, LO, off, ooff)
            self.k.barrier()
            with contextlib.ExitStack() as ph:
                self.ffn_phase(ph, "w2gu", "w2d", "g2", LO, src=("feat", self.S_h2, ooff), dst=("final", self.y, ooff))
            self.k.barrier()
            off += L
            ooff += LO
            job += 1
            blk0 += L // self.TB

    def ffn_phase(self, ph, wgu_name, wd_name, g_name, ntok, src, dst):
        nc, k, I = self.nc, self.k, self.I
        T = 512
        Wgu = ph.enter_context(self.sbt("Wgu", [128, 8, 2 * DFF], BF16))
        Wd = ph.enter_context(self.sbt("Wd", [128, NFT, D], BF16))
        gT = ph.enter_context(self.sbt("gT", [128, 8], F32))
        xT = ph.enter_context(self.sbt("xT", [128, 8, T], F32))
        xn = ph.enter_context(self.sbt("xn", [128, 8, T], BF16))
        actb = ph.enter_context(self.sbt("actb", [128, NFT, T], BF16))
        xin = [ph.enter_context(self.sbt(f"xin{i}", [128, 512], F32)) for i in range(2)]
        sg = [ph.enter_context(self.sbt(f"sg{i}", [128, T], F32)) for i in range(2)]
        rstd = ph.enter_context(self.sbt("rstd", [128, T], F32))
        hout = [ph.enter_context(self.sbt(f"hout{i}", [128, T], F32)) for i in range(2)]
        bWgu, bWd, bg, bxT, bxn, bact, brstd = (k.buf() for _ in range(7))
        bxin = [k.buf(), k.buf()]
        bsg = [k.buf(), k.buf()]
        bhout = [k.buf(), k.buf()]
        stgt = ph.enter_context(self.sbt("ffnstg", [128, 2 * 1408], F32))
        stg = stgt[:]
        bstg = [k.buf(), k.buf()]
        k.dma("sp", gT[:], I[g_name][:, :], [], [bg])
        gF = ph.enter_context(self.sbt("gF", [128, 8], F32))
        k.dma("sp", gF[:], I["gfin"][:, :], [], [bg])
        wgu = I[wgu_name].rearrange("(kt p) f -> p kt f", p=128)
        wd = I[wd_name].rearrange("(ft p) m -> p ft m", p=128)
        CH = 1408
        n = 0
        bWc = [k.buf() for _ in range(4)]
        bWdc = [k.buf() for _ in range(2)]
        SW = 1408
        for ci, c0 in enumerate((0, DFF, CH, DFF + CH)):
            for kt in range(8):
                eng = ("dve", "act", "dve", "pool", "act", "dve")[n % 6]
                sv_ = stg[:, (n % 2) * SW:(n % 2) * SW + CH]
                k.dma("sp", sv_, wgu[:, kt, c0:c0 + CH], [], [bstg[n % 2]])
                bw_ = bWc[(0, 2, 1, 3)[ci]]
                if eng == "act":
                    k.act(Wgu[:, kt, c0:c0 + CH], sv_, AF.Copy, [bstg[n % 2], bg], [bw_], scale=gT[:, kt:kt + 1])
                else:
                    k.ts(eng, Wgu[:, kt, c0:c0 + CH], sv_, gT[:, kt:kt + 1], None, ALU.mult, None,
                         [bstg[n % 2], bg], [bw_])
                n += 1
        for ft in range(NFT):
            eng = ("dve", "act", "dve", "pool", "act", "dve")[n % 6]
            sv_ = stg[:, (n % 2) * SW:(n % 2) * SW + D]
            k.dma("sp", sv_, wd[:, ft, :], [], [bstg[n % 2]])
            k.copy(eng, Wd[:, ft, :], sv_, [bstg[n % 2]], [bWdc[0 if ft < 12 else 1]])
            n += 1
        bWgu_of = lambda j, u: bWc[(2 if u else 0) + (0 if j < 11 else 1)]
        bWd_of = lambda j: bWdc[0 if j < 12 else 1]
        for t0 in range(0, ntok, T):
            if src[0] == "tok":
                xsrc, xoff = src[1], src[2]
                for s in range(4):
                    r0 = xoff + t0 + s * 128
                    for half in range(2):
                        xi, bxi = xin[half], bxin[half]
                        k.dma("sp", xi[:], xsrc[r0:r0 + 128, half * 512:(half + 1) * 512], [], [bxi])
                        ps, bps = self.psum()
                        for j in range(4):
                            k.tr(ps[:, j * 128:(j + 1) * 128], xi[:, j * 128:(j + 1) * 128], self.identF,
                                 [bxi, self.cB], [bps])
                        k.copy("act" if half == 0 else "dve", xT[:, half * 4:half * 4 + 4, s * 128:(s + 1) * 128],
                               ps[:].rearrange("p (a t) -> p a t", a=4), [bps], [bxT])
            else:
                hsrc, xoff = src[1], src[2]
                k.dma("sp", xT[:], hsrc[:, xoff + t0:xoff + t0 + T].rearrange("(kt p) t -> p kt t", p=128),
                      [self.bS(hsrc, xoff + t0)], [bxT])
            sq = actb[:, 0:8, :]
            k.act(sq, xT[:], AF.Square, [bxT], [bact])
            pst, bpst = self.psum()
            for kt in range(8):
                k.mm(pst[:], self.onesB, sq[:, kt, :], kt == 0, kt == 7, [bact, self.cB], [bpst])
            k.act(rstd[:], pst[:], AF.Sqrt, [bpst, self.cB], [brstd], bias=self.epsT, scale=1.0 / D)
            k.recip(rstd[:], rstd[:], [brstd], [brstd])
            for kt in range(8):
                k.tt("dve" if kt % 2 == 0 else "pool", xn[:, kt, :], xT[:, kt, :], rstd[:], ALU.mult,
                     [bxT, brstd], [bxn])
            for j in range(NFT):
                psg, bpsg = self.psum()
                psu, bpsu = self.psum()
                for kt in range(8):
                    k.mm(psg[:], Wgu[:, kt, j * 128:(j + 1) * 128], xn[:, kt, :], kt == 0, kt == 7,
                         [bWgu_of(j, 0), bxn], [bpsg])
                for kt in range(8):
                    k.mm(psu[:], Wgu[:, kt, DFF + j * 128:DFF + (j + 1) * 128], xn[:, kt, :], kt == 0, kt == 7,
                         [bWgu_of(j, 1), bxn], [bpsu])
                k.act(sg[j % 2][:], psg[:], AF.Silu, [bpsg], [bsg[j % 2]])
                k.tt("dve", actb[:, j, :], sg[j % 2][:], psu[:], ALU.mult, [bsg[j % 2], bpsu], [bact])
            for m in range(8):
                ps, bps = self.psum()
                for j in range(NFT):
                    k.mm(ps[:], Wd[:, j, m * 128:(m + 1) * 128], actb[:, j, :], j == 0, j == NFT - 1,
                         [bWd_of(j), bact], [bps])
                ho, bho = hout[m % 2], bhout[m % 2]
                if dst[0] == "final":
                    k.stt(xT[:, m, :], ps[:], 0.5, xT[:, m, :], ALU.mult, ALU.add, [bps, bxT], [bxT])
                    continue
                k.stt(ho[:], ps[:], 0.5, xT[:, m, :], ALU.mult, ALU.add, [bps, bxT], [bho])
                if dst[0] == "feat":
                    hd, doff = dst[1], dst[2]
                    k.dma("pool", hd[m * 128:(m + 1) * 128, doff + t0:doff + t0 + T], ho[:], [bho],
                          [self.bS(hd, doff + t0)])

            if dst[0] == "final":
                yd, doff = dst[1], dst[2]
                sq = actb[:, 0:8, :]
                k.act(sq, xT[:], AF.Square, [bxT], [bact])
                pst, bpst = self.psum()
                for kt in range(8):
                    k.mm(pst[:], self.onesB, sq[:, kt, :], kt == 0, kt == 7, [bact, self.cB], [bpst])
                k.act(rstd[:], pst[:], AF.Sqrt, [bpst, self.cB], [brstd], bias=self.epsT, scale=1.0 / D)
                k.recip(rstd[:], rstd[:], [brstd], [brstd])
                for kt in range(8):
                    k.stt(xT[:, kt, :], xT[:, kt, :], gF[:, kt:kt + 1], rstd[:], ALU.mult, ALU.mult,
                          [bxT, brstd, bg], [bxT])
                for s in range(4):
                    r0 = doff + t0 + s * 128
                    for half in range(2):
                        xi, bxi = xin[half], bxin[half]
                        ps, bps = self.psum()
                        for j in range(4):
                            kt = half * 4 + j
                            k.tr(ps[:, j * 128:(j + 1) * 128], xT[:, kt, s * 128:(s + 1) * 128], self.identF,
                                 [bxT, self.cB], [bps])
                        k.copy("act" if half == 0 else "dve", xi[:], ps[:], [bps], [bxi])
                        k.dma("pool", yd[r0:r0 + 128, half * 512:(half + 1) * 512], xi[:], [bxi], [])

    def bS(self, ap, t0):
        d = self.__dict__.setdefault("_sb", {})
        key = (id(ap), t0 // 512)
        if key not in d:
            d[key] = Buf(None)
        return d[key]


def _add_methods(cls):
    def deco(f):
        setattr(cls, f.__name__, f)
        return f
    return deco


@_add_methods(Prog)
def load_weight_cols(self, ph, W, bW, src, cols, gT, bg, chunked=False):
    nc, k = self.nc, self.k
    cache = self.__dict__.setdefault("_stgc", {})
    if cache.get("ph") is not ph:
        cache["ph"] = ph
        cache["stg"] = [ph.enter_context(self.sbt(f"wstg{i}", [128, 2048], F32)) for i in range(3)]
        cache["bst"] = [k.buf() for _ in range(3)]
    stg, bst = cache["stg"], cache["bst"]
    nst = len(stg)
    srcv = src.rearrange("(kt p) f -> p kt f", p=128)
    nkt = srcv.shape[1]
    n = 0
    o = 0
    if chunked:
        self._wch = []
    for (c0, cn) in cols:
        for cc in range(0, cn, 2048):
            w = min(2048, cn - cc)
            for kt in range(nkt):
                s, bs = stg[n % nst], bst[n % nst]
                if chunked and kt == 0:
                    bW = k.buf()
                    self._wch.append((o + cc, o + cc + w, bW))
                k.dma("sp", s[:, 0:w], srcv[:, kt, c0 + cc:c0 + cc + w], [], [bs])
                eng = ("dve", "act", "dve", "pool", "act", "dve")[n % 6]
                if gT is not None and eng == "act":
                    k.act(W[:, kt, o + cc:o + cc + w], s[:, 0:w], AF.Copy, [bs, bg], [bW], scale=gT[:, kt:kt + 1])
                elif gT is not None:
                    k.ts(eng, W[:, kt, o + cc:o + cc + w], s[:, 0:w], gT[:, kt:kt + 1], None, ALU.mult, None,
                         [bs, bg], [bW])
                else:
                    k.copy(eng, W[:, kt, o + cc:o + cc + w], s[:, 0:w], [bs], [bW])
                n += 1
        o += cn


@_add_methods(Prog)
def uid(self):
    self._uid = getattr(self, "_uid", 0) + 1
    return self._uid


@_add_methods(Prog)
def norm_tile(self, hT, bhT, un, bun, sq, bsq, rstd, brstd, T=512, nkt=8, dim=D):
    k = self.k
    k.act(sq, hT, AF.Square, [bhT], [bsq])
    pst, bpst = self.psum()
    for kt in range(nkt):
        k.mm(pst[:, 0:T], self.onesB, sq[:, kt, :], kt == 0, kt == nkt - 1, [bsq, self.cB], [bpst])
    k.act(rstd, pst[:, 0:T], AF.Sqrt, [bpst, self.cB], [brstd], bias=self.epsT, scale=1.0 / dim)
    k.recip(rstd, rstd, [brstd], [brstd])
    for kt in range(nkt):
        k.tt("dve" if kt % 2 == 0 else "pool", un[:, kt, :], hT[:, kt, :], rstd, ALU.mult, [bhT, brstd], [bun])


@_add_methods(Prog)
def qk_head(self, ps, bps, gcol, bgc, cosT, sinT, btab, tmp, btmp, outbf, bout, T=512, fill=None):
    k = self.k
    sqh, rs, xn, t1 = (t[:] for t in tmp)
    k.act(sqh, ps[:, 0:T], AF.Square, [bps], [btmp])
    if fill:
        fill()
    p2, bp2 = self.psum()
    k.mm(p2[:, 0:T], self.onesB, sqh, True, True, [btmp, self.cB], [bp2])
    k.act(rs, p2[:, 0:T], AF.Sqrt, [bp2, self.cB], [btmp], bias=self.epsT, scale=1.0 / 128)
    k.recip(rs, rs, [btmp], [btmp])
    k.stt(xn, ps[:, 0:T], gcol, rs, ALU.mult, ALU.mult, [bps, bgc, btmp], [btmp])
    if fill:
        fill()
    p3, bp3 = self.psum()
    k.mm(p3[:, 0:T], self.PmF, xn, True, True, [btmp, self.cB], [bp3])
    k.tt("pool", t1, xn, cosT, ALU.mult, [btmp, btab], [btmp])
    k.tt("dve", xn, p3[:, 0:T], sinT, ALU.mult, [bp3, btab], [btmp])
    k.tt("pool", outbf, t1, xn, ALU.add, [btmp], [bout])


@_add_methods(Prog)
def inproj_phase(self, ph, part, ntok, off, ooff):
    nc, k, I = self.nc, self.k, self.I
    T = 512
    if part == "a":
        cols = [(OFF_XBC, 3072), (OFF_DTF, 64), (OFF_K, 256), (OFF_V, 256)]
    else:
        cols = [(OFF_Z, 2048), (OFF_Q, 1024), (OFF_G, 2048)]
    ncol = sum(c for _, c in cols)
    W = ph.enter_context(self.sbt("Win", [128, 8, ncol], BF16))
    gT = ph.enter_context(self.sbt("gmixT", [128, 8], F32))
    sv = ph.enter_context(self.sbt("smallv", [128, 32], F32))
    bW, bg, bsv = k.buf(), k.buf(), k.buf()
    k.dma("sp", gT[:], I["gmix"][:, :], [], [bg])
    k.dma("sp", sv[:, 0:1], I["gq"][:, :], [], [bsv])
    k.dma("sp", sv[:, 1:2], I["gk"][:, :], [], [bsv])
    k.dma("sp", sv[:, 2:18], I["bgate"][:, :], [], [bsv])
    k.ts("dve", sv[:, 0:1], sv[:, 0:1], 128.0 ** -0.5, None, ALU.mult, None, [bsv], [bsv])
    self.load_weight_cols(ph, W, bW, I["win"], cols, gT, bg, chunked=True)
    wch = list(self._wch)

    def wb(col):
        for a_, b_, buf_ in wch:
            if a_ <= col < b_:
                return buf_
        raise ValueError(col)
    hT = ph.enter_context(self.sbt("hT", [128, 8, T], F32))
    un = ph.enter_context(self.sbt("un", [128, 8, T], BF16))
    sq = ph.enter_context(self.sbt("sq", [128, 8, T], BF16))
    rstd = ph.enter_context(self.sbt("rstd", [128, T], F32))
    bhT, bun, bsq, brstd = (k.buf() for _ in range(4))
    tabs = ph.enter_context(self.sbt("tabs", [128, 2, T], F32))
    btab = k.buf()
    tmpq = [(ph.enter_context(self.sbt(f"sqh{i}", [128, T], BF16)),
             ph.enter_context(self.sbt(f"rs{i}", [128, T], F32)),
             ph.enter_context(self.sbt(f"xnq{i}", [128, T], F32)),
             ph.enter_context(self.sbt(f"t1q{i}", [128, T], F32))) for i in range(2)]
    btmpq = [k.buf(), k.buf()]
    qo = [ph.enter_context(self.sbt(f"qo{i}", [128, T], BF16)) for i in range(2)]
    bqo = [k.buf(), k.buf()]
    if part == "a":
        xst = [ph.enter_context(self.sbt(f"xst{i}", [128, 4, T], F32)) for i in range(2)]
        bxst = [k.buf(), k.buf()]
        dst_ = ph.enter_context(self.sbt("dtst", [128, 4, 64], F32))
        vst = ph.enter_context(self.sbt("vst", [128, 4, 256], BF16))
        bdst, bvst = k.buf(), k.buf()
    else:
        zst = [ph.enter_context(self.sbt(f"zst{i}", [128, 2048], F32)) for i in range(2)]
        bzst = [k.buf(), k.buf()]
        gst = [ph.enter_context(self.sbt(f"gst{i}", [128, 4, T], F32)) for i in range(2)]
        bgst = [k.buf(), k.buf()]
    nq = 0
    for t0 in range(0, ntok, T):
        g0 = off + t0
        jb, tl = self.jobof(g0) if self.cc else (None, t0)
        xpre = self.S_xpre_l[jb] if self.cc else self.S_xpre
        k.dma("sp", hT[:], self.S_h[:, g0:g0 + T].rearrange("(kt p) t -> p kt t", p=128),
              [self.bS(self.S_h, g0)], [bhT])
        k.dma("sp", tabs[:, 0, :], I["cosT"][:, g0:g0 + T], [], [btab])
        k.dma("sp", tabs[:, 1, :], I["sinT"][:, g0:g0 + T], [], [btab])
        self.norm_tile(hT[:], bhT, un, bun, sq[:], bsq, rstd[:], brstd)
        fillers = []

        def fill(n=2):
            for _ in range(n):
                if fillers:
                    fillers.pop(0)()

        if part == "a":
            def xbc_group(ct):
                def f():
                    c4, j = ct // 4, ct % 4
                    st, bst = xst[c4 % 2], bxst[c4 % 2]
                    ps, bps = self.psum()
                    for kt in range(8):
                        k.mm(ps[:], W[:, kt, ct * 128:(ct + 1) * 128], un[:, kt, :], kt == 0, kt == 7,
                             [wb(ct * 128), bun], [bps])
                    k.copy("act" if j % 2 == 0 else "dve", st[:, j, :], ps[:], [bps], [bst])
                    if j == 3:
                        k.dma("pool", xpre[c4 * 512:(c4 + 1) * 512, tl:tl + T].rearrange("(c p) t -> p c t", p=128),
                              st[:], [bst], [self.bS(xpre, tl)])
                return f

            def dtv_group(s_):
                def f():
                    ps, bps = self.psum()
                    for kt in range(8):
                        k.mm(ps[:, 0:64], un[:, kt, s_ * 128:(s_ + 1) * 128], W[:, kt, 3072:3136], kt == 0, kt == 7,
                             [wb(3072), bun], [bps])
                    k.copy("act", dst_[:, s_, :], ps[:, 0:64], [bps], [bdst])
                    ps, bps = self.psum()
                    for kt in range(8):
                        k.mm(ps[:, 0:256], un[:, kt, s_ * 128:(s_ + 1) * 128], W[:, kt, 3392:3648], kt == 0, kt == 7,
                             [wb(3392), bun], [bps])
                    k.copy("dve", vst[:, s_, :], ps[:, 0:256], [bps], [bvst])
                    if s_ == 3:
                        k.dma("pool", self.S_dt[g0:g0 + T, :].rearrange("(s p) c -> p s c", p=128), dst_[:], [bdst],
                              [self.bS(self.S_dt, g0)])
                        if self.cc:
                            for kv in range(2):
                                vh = self.V_in[jb][kv]
                                k.dma("pool", vh.ap()[tl:tl + T, :].rearrange("(s p) c -> p s c", p=128),
                                      vst[:, :, kv * 128:(kv + 1) * 128], [bvst], [self.ccb(vh)])
                        else:
                            k.dma("pool", self.S_v[g0:g0 + T, :].rearrange("(s p) c -> p s c", p=128), vst[:], [bvst],
                                  [self.bS(self.S_v, g0)])
                return f

            fillers.extend(xbc_group(ct) for ct in range(24))
            fillers.extend(dtv_group(s_) for s_ in range(4))
            fill(4)
            for hd in range(2):
                ps, bps = self.psum()
                for kt in range(8):
                    k.mm(ps[:], W[:, kt, 3136 + hd * 128:3136 + (hd + 1) * 128], un[:, kt, :], kt == 0, kt == 7,
                         [wb(3136 + hd * 128), bun], [bps])
                self.qk_head(ps, bps, sv[:, 1:2], bsv, tabs[:, 0, :], tabs[:, 1, :], btab,
                             tmpq[nq % 2], btmpq[nq % 2], qo[nq % 2][:], bqo[nq % 2], fill=fill)
                if self.cc:
                    kh = self.K_in[jb][hd]
                    k.dma("pool", kh.ap()[:, tl:tl + T], qo[nq % 2][:], [bqo[nq % 2]], [self.ccb(kh)])
                else:
                    k.dma("pool", self.S_kT[hd * 128:(hd + 1) * 128, g0:g0 + T], qo[nq % 2][:], [bqo[nq % 2]],
                          [self.bS(self.S_kT, g0)])
                nq += 1
                fill(2)
            fill(len(fillers))
        else:
            o0 = ooff + t0

            def z_group(s_, c):
                def f():
                    zs, bzs = zst[s_ % 2], bzst[s_ % 2]
                    ps, bps = self.psum()
                    for kt in range(8):
                        k.mm(ps[:], un[:, kt, s_ * 128:(s_ + 1) * 128], W[:, kt, c * 512:(c + 1) * 512], kt == 0,
                             kt == 7, [wb(c * 512), bun], [bps])
                    k.act(zs[:, c * 512:(c + 1) * 512], ps[:], AF.Silu, [bps], [bzs])
                    if c == 3:
                        k.dma("pool", self.S_sz[o0 + s_ * 128:o0 + (s_ + 1) * 128, :], zs[:], [bzs],
                              [self.bS(self.S_sz, o0)])
                return f

            def g_group(ct):
                def f():
                    c4, j = ct // 4, ct % 4
                    st, bst = gst[c4 % 2], bgst[c4 % 2]
                    ps, bps = self.psum()
                    for kt in range(8):
                        k.mm(ps[:], W[:, kt, 3072 + ct * 128:3072 + (ct + 1) * 128], un[:, kt, :], kt == 0,
                             kt == 7, [wb(3072 + ct * 128), bun], [bps])
                    k.act(st[:, j, :], ps[:], AF.Sigmoid, [bps, bsv], [bst], bias=sv[:, 2 + ct:3 + ct])
                    if j == 3:
                        k.dma("pool", self.S_gT[c4 * 512:(c4 + 1) * 512, o0:o0 + T].rearrange("(c p) t -> p c t", p=128),
                              st[:], [bst], [self.bS(self.S_gT, o0)])
                return f

            fillers.extend(z_group(s_, c) for s_ in range(4) for c in range(4))
            fillers.extend(g_group(ct) for ct in range(16))
            for hd in range(8):
                ps, bps = self.psum()
                for kt in range(8):
                    k.mm(ps[:], W[:, kt, 2048 + hd * 128:2048 + (hd + 1) * 128], un[:, kt, :], kt == 0, kt == 7,
                         [wb(2048 + hd * 128), bun], [bps])
                self.qk_head(ps, bps, sv[:, 0:1], bsv, tabs[:, 0, :], tabs[:, 1, :], btab,
                             tmpq[nq % 2], btmpq[nq % 2], qo[nq % 2][:], bqo[nq % 2], fill=lambda: fill(2))
                k.dma("pool", self.S_qT[hd * 128:(hd + 1) * 128, o0:o0 + T], qo[nq % 2][:], [bqo[nq % 2]],
                      [self.bS(self.S_qT, o0)])
                nq += 1
            fill(len(fillers))


@_add_methods(Prog)
def conv_phase(self, ph, L, LO, off, blk0):
    nc, k, I = self.nc, self.k, self.I
    TB = self.TB
    NS = TB // 128
    cw = ph.enter_context(self.sbt("convw", [128, 24, 5], F32))
    cb = ph.enter_context(self.sbt("convb", [128, 24], F32))
    cm = ph.enter_context(self.sbt("cmask", [128, 2 * self.nblk_total], F32))
    bcw = k.buf()
    k.dma("sp", cw[:], I["convw"][:, :, :], [], [bcw])
    k.dma("sp", cb[:], I["convb"][:, :], [], [bcw])
    if self.cc:
        nj = len(self.jobs)
        Eg = ph.enter_context(self.sbt("Eg", [128, 4, 24 * nj * 4], F32))
        c2 = ph.enter_context(self.sbt("ccm", [128, 64], F32))
        HLR = ph.enter_context(self.sbt("HLR", [128, 2, 24, 2], F32))
        bEg, bH = k.buf(), k.buf()
        k.dma("sp", Eg[:], self.E_out.ap().rearrange("(r p) f -> p r f", p=128), [self.ccb(self.E_out)], [bEg])
        k.dma("sp", c2[:], I["ccmask"][:, :], [], [bEg])
        for side in range(2):
            for r in range(4):
                ev = Eg[:, r, :].rearrange("p (c j e) -> p c j e", c=24, j=nj)
                src = ev[:, :, self.job, 2:4] if side == 0 else ev[:, :, self.job, 0:2]
                sc = c2[:, side * 4 + r:side * 4 + r + 1]
                if r == 0:
                    k.ts("dve", HLR[:, side, :, :], src, sc, None, ALU.mult, None, [bEg], [bH])
                else:
                    k.stt(HLR[:, side, :, :], src, sc, HLR[:, side, :, :], ALU.mult, ALU.add, [bEg, bH], [bH])
    else:
        k.dma("sp", cm[:], I["cmask"][:, :], [], [bcw])
    XB = [ph.enter_context(self.sbt(f"XB{i}", [128, 12, TB + 4], F32)) for i in range(2)]
    bXB = [k.buf(), k.buf()]
    acc = [ph.enter_context(self.sbt(f"cacc{i}", [128, TB], F32)) for i in range(3)]
    bacc = [k.buf() for _ in range(3)]
    xc = [ph.enter_context(self.sbt(f"xc{i}", [128, TB], BF16)) for i in range(3)]
    bxc = [k.buf() for _ in range(3)]
    xtok = ph.enter_context(self.sbt("xtok", [128, NS, 2048], BF16))
    btok = ph.enter_context(self.sbt("btok", [128, NS, 512], BF16))
    bxtok, bbtok = k.buf(), k.buf()
    n = 0
    for bi, t0 in enumerate(range(0, L, TB)):
        g0 = off + t0
        gl = off + (t0 - 2) % L
        gr = off + (t0 + TB) % L
        bidx = blk0 + bi
        for half in range(2):
            xb, bxb = XB[half], bXB[half]
            rows = self.S_xpre[half * 1536:(half + 1) * 1536, :].rearrange("(c p) t -> p c t", p=128)
            ll, lr = (t0 - 2) % L, (t0 + TB) % L
            deps = [self.bS(self.S_xpre, g) for g in {t0 + j for j in range(0, TB, 512)} | {ll, lr} | {max(t0 - 2, 0), min(t0 + TB, L - 1)}]
            lq = "sp" if half == 0 else "pool"
            k.dma(lq, xb[:, :, 2:TB + 2], rows[:, :, t0:t0 + TB], deps, [bxb])
            if self.cc:
                if t0 == 0:
                    k.copy("pool", xb[:, :, 0:2], HLR[:, 0, half * 12:(half + 1) * 12, :], [bH], [bxb])
                else:
                    k.dma("sp", xb[:, :, 0:2], rows[:, :, t0 - 2:t0], deps, [bxb])
                if t0 + TB == L:
                    k.copy("pool", xb[:, :, TB + 2:TB + 4], HLR[:, 1, half * 12:(half + 1) * 12, :], [bH], [bxb])
                else:
                    k.dma("sp", xb[:, :, TB + 2:TB + 4], rows[:, :, t0 + TB:t0 + TB + 2], deps, [bxb])
            else:
                k.dma("sp", xb[:, :, 0:2], rows[:, :, ll:ll + 2], deps, [bxb])
                k.dma("sp", xb[:, :, TB + 2:TB + 4], rows[:, :, lr:lr + 2], deps, [bxb])
                k.ts("pool", xb[:, :, 0:2], xb[:, :, 0:2], cm[:, 2 * bidx:2 * bidx + 1], None, ALU.mult, None,
                     [bxb, bcw], [bxb])
                k.ts("pool", xb[:, :, TB + 2:TB + 4], xb[:, :, TB + 2:TB + 4], cm[:, 2 * bidx + 1:2 * bidx + 2], None,
                     ALU.mult, None, [bxb, bcw], [bxb])
            def taps(c, n_):
                ct = half * 12 + c
                a, ba = acc[n_ % 3], bacc[n_ % 3]
                k.act(a[:], xb[:, c, 0:TB], AF.Identity, [bxb, bcw], [ba], bias=cb[:, ct:ct + 1], scale=cw[:, ct, 0:1])
                for kk in range(1, 5):
                    k.stt(a[:], xb[:, c, kk:kk + TB], cw[:, ct, kk:kk + 1], a[:], ALU.mult, ALU.add,
                          [bxb, bcw, ba], [ba])

            def finish(c, n_):
                ct = half * 12 + c
                a, ba = acc[n_ % 3], bacc[n_ % 3]
                o, bo = xc[n_ % 3], bxc[n_ % 3]
                k.act(o[:], a[:], AF.Silu, [ba], [bo])
                if ct >= 16:
                    r0 = (ct - 16) * 128
                    k.dma("act", self.S_bcT[r0:r0 + 128, g0:g0 + TB], o[:], [bo], [self.bS(self.S_bcT, g0)])
                if ct < 20:
                    ps, bps = self.psum()
                    psb = ps[:].bitcast(BF16)
                    for s_ in range(NS):
                        k.tr(psb[:, s_ * 128:(s_ + 1) * 128], o[:, s_ * 128:(s_ + 1) * 128], self.identB,
                             [bo, self.cB], [bps])
                    src = psb[:, 0:NS * 128].rearrange("p (s c) -> p s c", s=NS)
                    if ct < 16:
                        k.copy("act", xtok[:, :, ct * 128:(ct + 1) * 128], src, [bps], [bxtok])
                    else:
                        k.copy("act", btok[:, :, (ct - 16) * 128:(ct - 15) * 128], src, [bps], [bbtok])

            taps(0, n)
            for c in range(12):
                if c + 1 < 12:
                    taps(c + 1, n + c + 1)
                finish(c, n + c)
            n += 12
        k.dma("act", self.S_xtok[g0:g0 + TB, :].rearrange("(s p) c -> p s c", p=128), xtok[:], [bxtok],
              [self.bS(self.S_xtok, g0)])
        k.dma("act", self.S_Btok[g0:g0 + TB, :].rearrange("(s p) c -> p s c", p=128), btok[:], [bbtok],
              [self.bS(self.S_Btok, g0)])


def bc(ap, shape):
    return ap.to_broadcast(list(shape))


@_add_methods(Prog)
def ssd_phase(self, ph, L, LO, off, ooff, job):
    nc, k, I = self.nc, self.k, self.I
    NC, NO = L // 128, LO // 128
    SEG = NO
    cs = ph.enter_context(self.sbt("ssdc", [128, 176], F32))
    normrep = ph.enter_context(self.sbt("normrep", [128, 2048], F32))
    tri = ph.enter_context(self.sbt("tri", [128, 2, 128], F32))
    negf = ph.enter_context(self.sbt("negf", [128, 2, 512], F32))
    negm = ph.enter_context(self.sbt("negm", [128, 2, 512], BF16))
    bcs = k.buf()
    k.dma("sp", cs[:, 0:64], I["dtbias"][:, :], [], [bcs])
    k.dma("sp", cs[:, 64:128], I["alog"][:, :], [], [bcs])
    k.dma("sp", cs[:, 128:160], I["dskip"][:, :], [], [bcs])
    k.dma("sp", cs[:, 160:168], I["smask"][:, job * 8:job * 8 + 8], [], [bcs])
    k.dma("sp", normrep[:], I["ssmnorm"][:, :], [], [bcs])
    k.dma("sp", tri[:], I["tri"].rearrange("p (a l) -> p a l", a=2), [], [bcs])
    k.dma("sp", negf[:], I["negm"].rearrange("p (a l) -> p a l", a=2), [], [bcs])
    k.copy("dve", negm[:], negf[:], [bcs], [bcs])
    k.act(cs[:, 64:128], cs[:, 64:128], AF.Exp, [bcs], [bcs])
    k.ts("dve", cs[:, 64:128], cs[:, 64:128], -1.0, None, ALU.mult, None, [bcs], [bcs])
    dtb, Aneg, Drep, sm = cs[:, 0:64], cs[:, 64:128], cs[:, 128:160], cs[:, 160:168]
    hst = ph.enter_context(self.sbt("hstate", [128, 2048], F32))
    hpb = ph.enter_context(self.sbt("hprevb", [128, 2048], BF16))
    bhst, bhpb = k.buf(), k.buf()
    xts = [ph.enter_context(self.sbt(f"xt{i}", [128, 32, 64], BF16)) for i in range(2)]
    Bts = [ph.enter_context(self.sbt(f"Bt{i}", [128, 512], BF16)) for i in range(2)]
    dtrs = [ph.enter_context(self.sbt(f"dtr{i}", [128, 32], F32)) for i in range(2)]
    bcTs = [ph.enter_context(self.sbt(f"bcT{i}", [128, 8, 128], BF16)) for i in range(2)]
    smt = [ph.enter_context(self.sbt(f"smt{i}", [128, 8, 32], F32)) for i in range(2)]
    bin_ = [k.buf(), k.buf()]
    bsm = [k.buf(), k.buf()]
    xdt = ph.enter_context(self.sbt("xdt", [128, 32, 64], BF16))
    xdtd = [ph.enter_context(self.sbt(f"xdtd{i}", [128, 32, 64], BF16)) for i in range(2)]
    bxdt = k.buf()
    bxdtd = [k.buf(), k.buf()]
    Rt = ph.enter_context(self.sbt("Rt", [128, 32, 128], F32))
    Lt = ph.enter_context(self.sbt("Lt", [128, 32, 128], F32))
    MT = ph.enter_context(self.sbt("MT", [128, 32, 128], BF16))
    bR, bLt, bMT = k.buf(), k.buf(), k.buf()
    yacc = [ph.enter_context(self.sbt(f"yacc{i}", [128, 2048], F32)) for i in range(2)]
    byacc = [k.buf(), k.buf()]
    aux = [ph.enter_context(self.sbt(f"yaux{i}", [128, 2048], F32)) for i in range(2)]
    baux = [k.buf(), k.buf()]
    tmp2 = [ph.enter_context(self.sbt(f"ytmp{i}", [128, 512], F32)) for i in range(2)]
    btmp2 = [k.buf(), k.buf()]
    ssq = ph.enter_context(self.sbt("ssq", [128, 8], F32))
    bssq = k.buf()
    sbf = ph.enter_context(self.sbt("sbf", [128, 2048], BF16))
    sTs = [ph.enter_context(self.sbt(f"sTs{i}", [128, 16, 128], BF16)) for i in range(2)]
    bsbf = k.buf()
    bsTs = [k.buf(), k.buf()]
    st = {"n": 0, "nf": 0}

    def step(c, d, full):
        i = st["n"] % 2
        st["n"] += 1
        g = off + c * 128
        xt, Bt, dtr, bcT, s8 = xts[i], Bts[i], dtrs[i], bcTs[i], smt[i]
        bi, bs = bin_[i], bsm[i]
        deps = [self.bS(self.S_xtok, g), self.bS(self.S_Btok, g), self.bS(self.S_dt, g), self.bS(self.S_bcT, g)]
        k.dma("sp", xt[:].rearrange("p h q -> p (h q)"), self.S_xtok[g:g + 128, :], deps, [bi])
        k.dma("sp", Bt[:], self.S_Btok[g:g + 128, :], deps, [bi])
        k.dma("sp", dtr[:], self.S_dt[g:g + 128, d * 32:(d + 1) * 32], deps, [bi])
        if full:
            k.dma("sp", bcT[:], self.S_bcT[:, g:g + 128].rearrange("(r p) t -> p r t", p=128), deps, [bi])
        dt_, a_, ac_, nac_, eA_, wd_, cd_, tm_ = (s8[:, j, :] for j in range(8))
        k.tt("dve", tm_, dtr[:], dtb[:, d * 32:(d + 1) * 32], ALU.add, [bi, bcs], [bs])
        k.act(tm_, tm_, AF.Exp, [bs], [bs])
        k.act(dt_, tm_, AF.Ln, [bs], [bs], bias=self.oneT)
        k.tt("dve", a_, dt_, Aneg[:, d * 32:(d + 1) * 32], ALU.mult, [bs, bcs], [bs])
        ps1, bps1 = self.psum()
        k.mm(ps1[:, 0:32], tri[:, d, :], a_, True, True, [bcs, bs], [bps1])
        k.mm(ps1[:, 32:64], self.onesF, a_, True, True, [self.cB, bs], [bps1])
        k.copy("act", ac_, ps1[:, 0:32], [bps1], [bs])
        k.tt("dve", tm_, ps1[:, 32:64], ac_, ALU.subtract, [bps1, bs], [bs])
        k.act(wd_, tm_, AF.Exp, [bs], [bs])
        k.act(cd_, ps1[:, 32:64], AF.Exp, [bps1], [bs])
        k.tt("dve", wd_, wd_, dt_, ALU.mult, [bs], [bs])
        xd, bxd = xdtd[i], bxdtd[i]
        k.tt("pool", xd[:], xt[:], bc(wd_.unsqueeze(2), [128, 32, 64]), ALU.mult, [bi, bs], [bxd])
        if full:
            j = st["nf"] % 2
            st["nf"] += 1
            ya, bya = yacc[j], byacc[j]
            ax, bax = aux[j], baux[j]
            oo = ooff + c * 128
            k.act(eA_, ac_, AF.Exp, [bs], [bs])
            k.ts("dve", nac_, ac_, -1.0, None, ALU.mult, None, [bs], [bs])
            k.tt("pool", Rt[:], bc(a_.unsqueeze(2), [128, 32, 128]), bc(tri[:, d, :].unsqueeze(1), [128, 32, 128]),
                 ALU.mult, [bs, bcs], [bR])
            k.tt("dve", xdt[:], xt[:], bc(dt_.unsqueeze(2), [128, 32, 64]), ALU.mult, [bi, bs], [bxdt])
            k.copy("act", hpb[:], hst[:], [bhst], [bhpb])
            if d == 0:
                k.tt("pool", ax[:].rearrange("p (h q) -> p h q", q=64), xt[:], bc(Drep.unsqueeze(2), [128, 32, 64]),
                     ALU.mult, [bi, bcs], [bax])
            else:
                k.dma("sp", ax[:], self.S_y[oo:oo + 128, :], [self.bS(self.S_y, oo)], [bax])
            for hg in range(8):
                ps, bps = self.psum()
                k.mm(ps[:], self.identB, negm[:, d, :], True, False, [self.cB, bcs], [bps])
                k.mm(ps[:], self.onesF, Rt[:, hg * 4:(hg + 1) * 4, :].rearrange("p h l -> p (h l)"), False, True,
                     [self.cB, bR], [bps])
                for hh in range(4):
                    h = hg * 4 + hh
                    k.act(Lt[:, h, :], ps[:, hh * 128:(hh + 1) * 128], AF.Exp, [bps, bs], [bLt],
                          bias=nac_[:, h:h + 1])
            pcb, bpcb = self.psum()
            for g4 in range(4):
                k.mm(pcb[:, g4 * 128:(g4 + 1) * 128], bcT[:, g4, :], bcT[:, 4 + g4, :], True, True, [bi], [bpcb])
            for g4 in range(4):
                k.tt("dve", MT[:, g4 * 8:(g4 + 1) * 8, :], Lt[:, g4 * 8:(g4 + 1) * 8, :],
                     bc(pcb[:, g4 * 128:(g4 + 1) * 128].unsqueeze(1), [128, 8, 128]), ALU.mult, [bLt, bpcb], [bMT])
            for g4 in range(4):
                psy, bpsy = self.psum()
                for e in range(8):
                    h = g4 * 8 + e
                    k.mm(psy[:, e * 64:(e + 1) * 64], MT[:, h, :], xdt[:, h, :], True, True, [bMT, bxdt], [bpsy])
                pso, bpso = self.psum()
                k.mm(pso[:], bcT[:, 4 + g4, :], hpb[:, g4 * 512:(g4 + 1) * 512], True, True, [bi, bhpb], [bpso])
                t2, bt2 = tmp2[g4 % 2], btmp2[g4 % 2]
                k.tt("dve", t2[:].rearrange("p (e q) -> p e q", q=64), pso[:].rearrange("p (e q) -> p e q", q=64),
                     bc(eA_[:, g4 * 8:(g4 + 1) * 8].unsqueeze(2), [128, 8, 64]), ALU.mult, [bpso, bs], [bt2])
                k.tt("dve", ya[:, g4 * 512:(g4 + 1) * 512], psy[:], t2[:], ALU.add, [bpsy, bt2], [bya])
            if "D_s8" in self.debug and c == 0 and d == 0:
                k.dma("pool", self.dram("D_s8", [128, 256], F32), s8[:].rearrange("p a b -> p (a b)"), [bs], [])
                k.dma("pool", self.dram("D_Lt", [128, 4096], F32), Lt[:].rearrange("p a b -> p (a b)"), [bLt], [])
                k.dma("pool", self.dram("D_MT", [128, 4096], BF16), MT[:].rearrange("p a b -> p (a b)"), [bMT], [])
                k.dma("pool", self.dram("D_R", [128, 4096], F32), Rt[:].rearrange("p a b -> p (a b)"), [bR], [])
                k.dma("pool", self.dram("D_ya", [128, 2048], F32), ya[:], [bya], [])
                k.dma("pool", self.dram("D_xdt", [128, 2048], BF16), xdt[:].rearrange("p a b -> p (a b)"), [bxdt], [])
            k.tt("pool", ya[:], ya[:], ax[:], ALU.add, [bya, bax], [bya])
            if d == 0:
                k.dma("pool", self.S_y[oo:oo + 128, :], ya[:], [bya], [self.bS(self.S_y, oo)])
            else:
                k.dma("sp", ax[:], self.S_sz[oo:oo + 128, :], [self.bS(self.S_sz, oo)], [bax])
                k.tt("pool", ya[:], ya[:], ax[:], ALU.mult, [bya, bax], [bya])
                k.act(ax[:], ya[:], AF.Square, [bya], [bax])
                k.op("dve", lambda e, o_=ssq[:, 0:4], i_=ax[:].rearrange("p (g c) -> p g c", g=4):
                     e.tensor_reduce(out=o_, in_=i_, axis=AX.X, op=ALU.add), [bax], [bssq])
                k.act(ssq[:, 4:8], ssq[:, 0:4], AF.Sqrt, [bssq, self.cB], [bssq], bias=self.epsT, scale=1.0 / 512)
                k.recip(ssq[:, 4:8], ssq[:, 4:8], [bssq], [bssq])
                for g4 in range(4):
                    k.stt(sbf[:, g4 * 512:(g4 + 1) * 512], ya[:, g4 * 512:(g4 + 1) * 512], ssq[:, 4 + g4:5 + g4],
                          normrep[:, g4 * 512:(g4 + 1) * 512], ALU.mult, ALU.mult, [bya, bssq, bcs], [bsbf])
                sT, bsT = sTs[j], bsTs[j]
                for hb in range(2):
                    ps, bps = self.psum()
                    psb = ps[:].bitcast(BF16)
                    for q in range(8):
                        ct = hb * 8 + q
                        k.tr(psb[:, q * 128:(q + 1) * 128], sbf[:, ct * 128:(ct + 1) * 128], self.identB,
                             [bsbf, self.cB], [bps])
                    k.copy("act", sT[:, hb * 8:(hb + 1) * 8, :], psb.rearrange("p (c t) -> p c t", c=8), [bps], [bsT])
                k.dma("pool", self.S_sT[:, oo:oo + 128].rearrange("(c p) t -> p c t", p=128), sT[:], [bsT],
                      [self.bS(self.S_sT, oo)])
        for g4 in range(4):
            psS, bpsS = self.psum()
            k.mm(psS[:], Bt[:, g4 * 128:(g4 + 1) * 128], xd[:, g4 * 8:(g4 + 1) * 8, :].rearrange("p e q -> p (e q)"),
                 True, True, [bi, bxd], [bpsS])
            hv = hst[:, g4 * 512:(g4 + 1) * 512].rearrange("p (e q) -> p e q", q=64)
            k.tt("dve", hv, hv, bc(cd_[:, g4 * 8:(g4 + 1) * 8].unsqueeze(2), [128, 8, 64]), ALU.mult, [bhst, bs], [bhst])
            k.tt("dve", hst[:, g4 * 512:(g4 + 1) * 512], hst[:, g4 * 512:(g4 + 1) * 512], psS[:], ALU.add,
                 [bhst, bpsS], [bhst])

    def scale_state(col):
        k.ts("dve", hst[:], hst[:], sm[:, col:col + 1], None, ALU.mult, None, [bhst, bcs], [bhst])

    k.memset("dve", hst[:], 0.0, [bhst])
    for j in (1, 2, 3):
        scale_state(j - 1)
        for c in range(j * SEG, (j + 1) * SEG):
            step(c, 0, False)
    scale_state(3)
    for c in range(NO):
        step(c, 0, True)
    k.memset("dve", hst[:], 0.0, [bhst])
    for j in (3, 2, 1):
        scale_state(4 + (3 - j))
        for c in range((j + 1) * SEG - 1, j * SEG - 1, -1):
            step(c, 1, False)
    scale_state(7)
    for c in range(NO - 1, -1, -1):
        step(c, 1, True)


def _lay(v):
    return np.ascontiguousarray(np.asarray(v, np.float32).reshape(-1, 128).T)


def _rep(v):
    return np.ascontiguousarray(np.tile(np.asarray(v, np.float32).reshape(1, -1), (128, 1)))


def _rope_tables(L, shift):
    pos = (np.arange(L) + shift) % L
    row = (pos // GRID_W).astype(np.float32)
    col = (pos % GRID_W).astype(np.float32)
    inv = (np.float32(10000.0) ** (-np.arange(0, 64, 2, dtype=np.float32) / np.float32(64))).astype(np.float32)
    ang = np.concatenate([row[:, None] * inv, col[:, None] * inv], -1).astype(np.float32)
    c = np.repeat(np.cos(ang), 2, axis=1).T
    s = np.repeat(np.sin(ang), 2, axis=1).T
    return np.ascontiguousarray(c.astype(np.float32)), np.ascontiguousarray(s.astype(np.float32))


def const_inputs(Wt, jobs, TB):
    kk = np.arange(128)
    triF = (kk[:, None] <= kk[None, :]).astype(np.float32)
    triB = (kk[:, None] >= kk[None, :]).astype(np.float32)
    negF = np.where(kk[None, :] < kk[:, None], NEG, 0.0).astype(np.float32)
    negB = np.where(kk[None, :] > kk[:, None], NEG, 0.0).astype(np.float32)
    pm = np.zeros((128, 128), np.float32)
    for i in range(64):
        pm[2 * i + 1, 2 * i] = -1.0
        pm[2 * i, 2 * i + 1] = 1.0
    f = lambda a: np.ascontiguousarray(np.asarray(a, np.float32))
    cw = f(Wt["conv_w"])[0]
    return dict(
        w1gu=f(Wt["w_ffn1_gu"])[0], w1d=f(Wt["w_ffn1_down"])[0], g1=_lay(Wt["norm_ffn1"]),
        w2gu=f(Wt["w_ffn2_gu"])[0], w2d=f(Wt["w_ffn2_down"])[0], g2=_lay(Wt["norm_ffn2"]),
        gfin=_lay(Wt["norm_final"]), win=f(Wt["w_in"])[0], gmix=_lay(Wt["norm_mix"]),
        gq=f(Wt["q_norm"]).reshape(128, 1), gk=f(Wt["k_norm"]).reshape(128, 1), bgate=_lay(Wt["b_gate"]),
        convw=np.ascontiguousarray(cw.T.reshape(24, 128, 5).transpose(1, 0, 2)), convb=_lay(Wt["conv_b"]),
        dtbias=_rep(np.concatenate([f(Wt["dt_bias_f"]).ravel(), f(Wt["dt_bias_b"]).ravel()])),
        alog=_rep(np.concatenate([f(Wt["A_log_f"]).ravel(), f(Wt["A_log_b"]).ravel()])),
        dskip=_rep(Wt["D_skip"]), ssmnorm=_rep(Wt["ssm_norm"]),
        tri=np.concatenate([triF, triB], 1),
        negm=np.concatenate([np.tile(negF, (1, 4)), np.tile(negB, (1, 4))], 1), pm=pm,
        wsb=f(Wt["w_ssm_branch"])[0], wab=f(Wt["w_attn_branch"])[0], wout=f(Wt["w_out"])[0],
    )


def core_inputs(xs, qi, jobs, TB):
    xr, cos, sin, cms, sms = [], [], [], [], []
    for x, (L, LO) in zip(xs, jobs):
        xr.append(np.roll(x, -qi * LO, axis=0))
        c, s = _rope_tables(L, qi * LO)
        cos.append(c)
        sin.append(s)
        nb = L // TB
        cm = np.ones(2 * nb, np.float32)
        bpos = ((4 - qi) % 4) * LO
        for b in range(nb):
            if b * TB == bpos:
                cm[2 * b] = 0
            if ((b + 1) * TB) % L == bpos:
                cm[2 * b + 1] = 0
        cms.append(cm)
        m = np.ones(8, np.float32)
        for j in (1, 2, 3):
            if (qi + j) % 4 == 0:
                m[j - 1] = 0
            if (qi + j) % 4 == 3:
                m[4 + (3 - j)] = 0
        if qi == 0:
            m[3] = 0
        if qi == 3:
            m[7] = 0
        sms.append(m)
    return dict(x=np.ascontiguousarray(np.concatenate(xr, 0)), cosT=np.concatenate(cos, 1), sinT=np.concatenate(sin, 1),
                cmask=_rep(np.concatenate(cms)), smask=_rep(np.concatenate(sms)))


@_add_methods(Prog)
def attn_phase(self, ph, L, LO, off, ooff):
    nc, k, I = self.nc, self.k, self.I
    NK = L // 128
    T = 512
    KT = ph.enter_context(self.sbt("KT", [128, 2, L], BF16))
    V = ph.enter_context(self.sbt("Vsb", [128, NK, 256], BF16))
    bKT, bV = k.buf(), k.buf()
    if self.cc:
        for kv in range(2):
            ko, vo = self.K_out[self.job][kv], self.V_out[self.job][kv]
            k.dma("sp", KT[:, kv, :].rearrange("p (r t) -> p r t", r=4), ko.ap().rearrange("(r p) t -> p r t", p=128),
                  [self.ccb(ko)], [bKT])
            VS = min(32, NK)
            for j0 in range(0, NK, VS):
                k.dma("sp", V[:, j0:j0 + VS, kv * 128:(kv + 1) * 128],
                      vo.ap()[j0 * 128:(j0 + VS) * 128, :].rearrange("(j p) c -> p j c", p=128), [self.ccb(vo)], [bV])
    else:
        alldeps = [self.bS(self.S_kT, off + t) for t in range(0, L, 512)]
        for kv in range(2):
            k.dma("sp", KT[:, kv, :], self.S_kT[kv * 128:(kv + 1) * 128, off:off + L], alldeps, [bKT])
        vdeps = [self.bS(self.S_v, off + t) for t in range(0, L, 512)]
        for j0 in range(0, NK, 16):
            k.dma("sp", V[:, j0:j0 + 16, :], self.S_v[off + j0 * 128:off + (j0 + 16) * 128, :].rearrange("(j p) c -> p j c", p=128),
                  vdeps, [bV])
    QT = [ph.enter_context(self.sbt(f"QT{i}", [128, 8, T], BF16)) for i in range(2)]
    bQT = [k.buf(), k.buf()]
    NP = NK // 2
    NPT = 6
    pt = [ph.enter_context(self.sbt(f"pt{i}", [128, 2, T], BF16)) for i in range(NPT)]
    bpt = [k.buf() for _ in range(NPT)]
    rc = [ph.enter_context(self.sbt(f"rc{i}", [128, T], F32)) for i in range(2)]
    ao = [ph.enter_context(self.sbt(f"ao{i}", [128, T], BF16)) for i in range(2)]
    brc = [k.buf(), k.buf()]
    bao = [k.buf(), k.buf()]
    NA = 6
    NCY = 7
    accs = [[ph.enter_context(self.sbt(f"lacc{i}_{e}", [128, 2, T], F32)) for e in range(NA)] for i in range(1)]
    baccs = [[k.buf() for _ in range(NA)] for _ in range(1)]
    accb = [[ph.enter_context(self.sbt(f"laccb{i}_{e}", [128, 2, T], BF16)) for e in range(NA)] for i in range(1)]
    baccb = [[k.buf() for _ in range(NA)] for _ in range(1)]
    bS2 = [k.buf(), k.buf()]
    nh = 0
    pending = []
    for qb in range(LO // T):
        o0 = ooff + qb * T
        qt, bqt = QT[qb % 2], bQT[qb % 2]
        k.dma("sp", qt[:], self.S_qT[:, o0:o0 + T].rearrange("(h p) t -> p h t", p=128), [self.bS(self.S_qT, o0)], [bqt])
        for h in range(8):
            kv = h // 4
            pso, bpso = self.ps[4 + 2 * (nh % 2)], self.psb[4 + 2 * (nh % 2)]
            psl, bpsl = self.ps[5 + 2 * (nh % 2)], self.psb[5 + 2 * (nh % 2)]
            acc, bacc = accs[0], baccs[0]
            acb, bacb = accb[0], baccb[0]
            seen = [False] * NA
            lastjj = [max([jj for jj in range(NP) if jj % NCY == e], default=-1) for e in range(NA)]
            pe_first = [True]

            def mm_s(jj):
                p = jj % 2
                for u in range(2):
                    j = 2 * jj + u
                    k.mm(self.ps[2 * p + u][:], KT[:, kv, j * 128:(j + 1) * 128], qt[:, h, :], True, True,
                         [bKT, bqt], [bS2[p]])
            mm_s(0)
            if NP > 1:
                mm_s(1)
            while pending:
                pending.pop(0)()
            for jj in range(NP):
                p = jj % 2
                p_, bp_ = pt[jj % NPT], bpt[jj % NPT]
                k.act(p_[:], self.psall[:, 2 * p:2 * p + 2, :], AF.Exp, [bS2[p]], [bp_])
                for u in range(2):
                    j = 2 * jj + u
                    k.mm(pso[:], V[:, j, kv * 128:(kv + 1) * 128], p_[:, u, :], j == 0, j == NK - 1, [bV, bp_], [bpso])
                e = jj % NCY
                if e >= NA:
                    for u in range(2):
                        k.mm(psl[:], self.onesB, p_[:, u, :], pe_first[0], False, [self.cB, bp_], [bpsl])
                        pe_first[0] = False
                else:
                    eng = "pool" if e >= 4 else "dve"
                    fin = (jj == lastjj[e])
                    dst, bdst = (acb[e], bacb[e]) if fin else (acc[e], bacc[e])
                    if not seen[e]:
                        k.copy(eng, dst[:], p_[:], [bp_], [bdst])
                        seen[e] = True
                    else:
                        k.tt(eng, dst[:], acc[e][:], p_[:], ALU.add, [bacc[e], bp_], [bdst])
                if jj + 2 < NP:
                    mm_s(jj + 2)
            def finalize(seen=seen, pe_first=pe_first, psl=psl, bpsl=bpsl, pso=pso, bpso=bpso, nh_=nh, h=h, o0=o0):
                parts = [(e, u) for e in range(NA) if seen[e] for u in range(2)]
                for i_, (e, u) in enumerate(parts):
                    k.mm(psl[:], self.onesB, acb[e][:, u, :], pe_first[0], i_ == len(parts) - 1,
                         [self.cB, bacb[e]], [bpsl])
                    pe_first[0] = False
                r, br = rc[nh_ % 2], brc[nh_ % 2]
                a_, ba_ = ao[nh_ % 2], bao[nh_ % 2]
                k.recip(r[:], psl[:], [bpsl], [br])
                k.tt("dve", a_[:], pso[:], r[:], ALU.mult, [bpso, br], [ba_])
                k.dma("pool", self.S_aT[h * 128:(h + 1) * 128, o0:o0 + T], a_[:], [ba_], [self.bS(self.S_aT, o0)])
            pending.append(finalize)
            nh += 1
    while pending:
        pending.pop(0)()


@_add_methods(Prog)
def merge_phase(self, ph, LO, off, ooff):
    nc, k, I = self.nc, self.k, self.I
    T = 512
    Ws = ph.enter_context(self.sbt("Ws", [128, 16, D], BF16))
    Wa = ph.enter_context(self.sbt("Wa", [128, 8, D], BF16))
    Wo = ph.enter_context(self.sbt("Wo", [128, 8, D], BF16))
    bWs, bWa, bWo = k.buf(), k.buf(), k.buf()
    self.load_weight_cols(ph, Ws, bWs, I["wsb"], [(0, D)], None, None)
    self.load_weight_cols(ph, Wa, bWa, I["wab"], [(0, D)], None, None)
    self.load_weight_cols(ph, Wo, bWo, I["wout"], [(0, D)], None, None)
    sT = ph.enter_context(self.sbt("sT", [128, 16, T], BF16))
    aT = ph.enter_context(self.sbt("aT", [128, 8, T], BF16))
    gT = ph.enter_context(self.sbt("gT", [128, 16, T], F32))
    hT = ph.enter_context(self.sbt("hT", [128, 8, T], F32))
    mT = ph.enter_context(self.sbt("mT", [128, 8, T], BF16))
    m1 = [ph.enter_context(self.sbt(f"m1{i}", [128, T], F32)) for i in range(2)]
    m2 = [ph.enter_context(self.sbt(f"m2{i}", [128, T], F32)) for i in range(2)]
    ho = [ph.enter_context(self.sbt(f"h2o{i}", [128, T], F32)) for i in range(2)]
    bsT, baT, bgT, bhT, bmT = (k.buf() for _ in range(5))
    bm1 = [k.buf(), k.buf()]
    bm2 = [k.buf(), k.buf()]
    bho = [k.buf(), k.buf()]
    for t0 in range(0, LO, T):
        o0 = ooff + t0
        g0 = off + t0
        k.dma("sp", sT[:], self.S_sT[:, o0:o0 + T].rearrange("(c p) t -> p c t", p=128), [self.bS(self.S_sT, o0)], [bsT])
        k.dma("sp", aT[:], self.S_aT[:, o0:o0 + T].rearrange("(c p) t -> p c t", p=128), [self.bS(self.S_aT, o0)], [baT])
        k.dma("sp", gT[:], self.S_gT[:, o0:o0 + T].rearrange("(c p) t -> p c t", p=128), [self.bS(self.S_gT, o0)], [bgT])
        k.dma("sp", hT[:], self.S_h[:, g0:g0 + T].rearrange("(c p) t -> p c t", p=128), [self.bS(self.S_h, g0)], [bhT])
        for m in range(8):
            p1, bp1 = self.psum()
            for kt in range(16):
                k.mm(p1[:], Ws[:, kt, m * 128:(m + 1) * 128], sT[:, kt, :], kt == 0, kt == 15, [bWs, bsT], [bp1])
            p2, bp2 = self.psum()
            for kt in range(8):
                k.mm(p2[:], Wa[:, kt, m * 128:(m + 1) * 128], aT[:, kt, :], kt == 0, kt == 7, [bWa, baT], [bp2])
            a1, ba1 = m1[m % 2], bm1[m % 2]
            a2, ba2 = m2[m % 2], bm2[m % 2]
            k.tt("dve", a1[:], p1[:], gT[:, m, :], ALU.mult, [bp1, bgT], [ba1])
            k.tt("dve", a2[:], p2[:], gT[:, 8 + m, :], ALU.mult, [bp2, bgT], [ba2])
            k.tt("pool", mT[:, m, :], a1[:], a2[:], ALU.add, [ba1, ba2], [bmT])
        for m in range(8):
            p1, bp1 = self.psum()
            for kt in range(8):
                k.mm(p1[:], Wo[:, kt, m * 128:(m + 1) * 128], mT[:, kt, :], kt == 0, kt == 7, [bWo, bmT], [bp1])
            h_, bh_ = ho[m % 2], bho[m % 2]
            k.tt("dve", h_[:], p1[:], hT[:, m, :], ALU.add, [bp1, bhT], [bh_])
            k.dma("pool", self.S_h2[m * 128:(m + 1) * 128, o0:o0 + T], h_[:], [bh_], [self.bS(self.S_h2, o0)])


def core_inputs_cc(xs, qi, jobs):
    xo, cos, sin = [], [], []
    for x, (L, LO) in zip(xs, jobs):
        xo.append(x[qi * LO:(qi + 1) * LO])
        c, s = _rope_tables(L, 0)
        cos.append(c[:, qi * LO:(qi + 1) * LO])
        sin.append(s[:, qi * LO:(qi + 1) * LO])
    m = np.zeros(64, np.float32)
    for r in range(4):
        m[0 + r] = 1.0 if r == qi - 1 else 0.0
        m[4 + r] = 1.0 if r == qi + 1 else 0.0
        m[8 + r] = 1.0 if r < qi else 0.0
        m[12 + r] = 1.0 if r > qi else 0.0
        for r2 in range(4):
            m[16 + r * 4 + r2] = 1.0 if r < r2 < qi else 0.0
            m[32 + r * 4 + r2] = 1.0 if qi < r2 < r else 0.0
    return dict(x=np.ascontiguousarray(np.concatenate(xo, 0)), cosT=np.ascontiguousarray(np.concatenate(cos, 1)),
                sinT=np.ascontiguousarray(np.concatenate(sin, 1)), ccmask=_rep(m))


JOBS = [(16384, 4096), (8192, 2048)]
_CACHE = {}


def kernel(x_prompt, x_sample, **Wt):
    x_prompt = np.asarray(x_prompt, np.float32)
    x_sample = np.asarray(x_sample, np.float32)
    prog = Prog(JOBS)
    nc = prog.build()
    shared = const_inputs(Wt, prog.jobs, prog.TB)
    in_maps = []
    for c in range(8):
        p, qi = c // 4, c % 4
        m = dict(shared)
        if prog.cc:
            m.update(core_inputs_cc([x_prompt[p], x_sample[p]], qi, prog.jobs))
        else:
            m.update(core_inputs([x_prompt[p], x_sample[p]], qi, prog.jobs, prog.TB))
        in_maps.append({k_: v_ for k_, v_ in m.items() if k_ in prog.I})
    res = run_bass_kernel_spmd(nc, in_maps, core_ids=list(range(8)))
    y_p = np.empty(x_prompt.shape, np.float32)
    y_s = np.empty(x_sample.shape, np.float32)
    (L0, O0), (L1, O1) = JOBS
    for c in range(8):
        p, qi = c // 4, c % 4
        y = np.asarray(res.results[c]["y"], np.float32)
        y_p[p, qi * O0:(qi + 1) * O0] = y[0:O0]
        y_s[p, qi * O1:(qi + 1) * O1] = y[O0:O0 + O1]
    return (y_p, y_s)


@_add_methods(Prog)
def ssd2_phase(self, ph, LO, off, job, mode):
    nc, k, I = self.nc, self.k, self.I
    NO = LO // 128
    nj = len(self.jobs)
    cs = ph.enter_context(self.sbt("ssdc", [128, 240], F32))
    tri = ph.enter_context(self.sbt("tri", [128, 2, 128], F32))
    bcs = k.buf()
    k.dma("sp", cs[:, 0:64], I["dtbias"][:, :], [], [bcs])
    k.dma("sp", cs[:, 64:128], I["alog"][:, :], [], [bcs])
    k.dma("sp", cs[:, 128:160], I["dskip"][:, :], [], [bcs])
    k.dma("sp", cs[:, 160:224], I["ccmask"][:, :], [], [bcs])
    k.dma("sp", tri[:], I["tri"].rearrange("p (a l) -> p a l", a=2), [], [bcs])
    k.act(cs[:, 64:128], cs[:, 64:128], AF.Exp, [bcs], [bcs])
    k.ts("dve", cs[:, 64:128], cs[:, 64:128], -1.0, None, ALU.mult, None, [bcs], [bcs])
    dtb, Aneg, Drep, cmk = cs[:, 0:64], cs[:, 64:128], cs[:, 128:160], cs[:, 160:224]
    names = ("dtq", "aq", "nacq", "eAq", "cdq", "wdq", "wlq", "t1q", "t2q")
    Q = {n: ph.enter_context(self.sbt(n, [128, NO, 32], F32)) for n in names}
    bQ = k.buf()
    Tq = ph.enter_context(self.sbt("Tq", [128, 64], F32))
    bTq = k.buf()
    NB = 4 if mode == "local" else 2
    xts = [ph.enter_context(self.sbt(f"xt{i}", [128, 32, 64], BF16)) for i in range(NB)]
    Bts = [ph.enter_context(self.sbt(f"Bt{i}", [128, 512], BF16)) for i in range(NB)]
    bin_ = [k.buf() for _ in range(NB)]
    xdtd = [ph.enter_context(self.sbt(f"xdtd{i}", [128, 32, 64], BF16)) for i in range(NB)]
    bxdtd = [k.buf() for _ in range(NB)]
    hst = ph.enter_context(self.sbt("hstate", [128, 2048], F32))
    bhst = k.buf()

    def flat(t):
        return t[:].rearrange("p n h -> p (n h)")

    def prep(d):
        W = NO * 32
        dsl = slice(d * 32, (d + 1) * 32)
        deps = [self.bS(self.S_dt, off + t) for t in range(0, LO, 512)]
        k.dma("sp", Q["dtq"][:], self.S_dt[off:off + LO, dsl].rearrange("(n p) h -> p n h", p=128), deps, [bQ])
        k.tt("dve", Q["dtq"][:], Q["dtq"][:], bc(dtb[:, dsl].unsqueeze(1), [128, NO, 32]), ALU.add, [bQ, bcs], [bQ])
        k.act(flat(Q["t1q"]), flat(Q["dtq"]), AF.Exp, [bQ], [bQ])
        k.act(flat(Q["dtq"]), flat(Q["t1q"]), AF.Ln, [bQ], [bQ], bias=self.oneT)
        k.tt("dve", Q["aq"][:], Q["dtq"][:], bc(Aneg[:, dsl].unsqueeze(1), [128, NO, 32]), ALU.mult, [bQ, bcs], [bQ])
        for c0 in range(0, W, 512):
            w = min(512, W - c0)
            ps, bps = self.psum()
            k.mm(ps[:, 0:w], tri[:, d, :], flat(Q["aq"])[:, c0:c0 + w], True, True, [bcs, bQ], [bps])
            k.copy("act", flat(Q["t1q"])[:, c0:c0 + w], ps[:, 0:w], [bps], [bQ])
            ps, bps = self.psum()
            k.mm(ps[:, 0:w], self.onesF, flat(Q["aq"])[:, c0:c0 + w], True, True, [self.cB, bQ], [bps])
            k.copy("dve", flat(Q["t2q"])[:, c0:c0 + w], ps[:, 0:w], [bps], [bQ])
        k.act(flat(Q["nacq"]), flat(Q["dtq"]), AF.Ln, [bQ], [bQ])
        k.tt("dve", flat(Q["nacq"]), flat(Q["nacq"]), flat(Q["t1q"]), ALU.subtract, [bQ], [bQ])
        k.act(flat(Q["eAq"]), flat(Q["t1q"]), AF.Exp, [bQ], [bQ])
        k.act(flat(Q["cdq"]), flat(Q["t2q"]), AF.Exp, [bQ], [bQ])
        k.tt("dve", flat(Q["t1q"]), flat(Q["t2q"]), flat(Q["t1q"]), ALU.subtract, [bQ], [bQ])
        k.act(flat(Q["wdq"]), flat(Q["t1q"]), AF.Exp, [bQ], [bQ])
        k.tt("dve", flat(Q["wdq"]), flat(Q["wdq"]), flat(Q["dtq"]), ALU.mult, [bQ], [bQ])
        if mode == "local":
            order = list(range(NO - 1, -1, -1)) if d == 0 else list(range(NO))
            k.memset("dve", Q["t1q"][:, order[0], :], 0.0, [bQ])
            for a_, b_ in zip(order[:-1], order[1:]):
                k.tt("dve", Q["t1q"][:, b_, :], Q["t1q"][:, a_, :], Q["t2q"][:, a_, :], ALU.add, [bQ], [bQ])
            last = order[-1]
            k.tt("dve", Tq[:, dsl], Q["t1q"][:, last, :], Q["t2q"][:, last, :], ALU.add, [bQ], [bTq])
            k.act(flat(Q["wlq"]), flat(Q["t1q"]), AF.Exp, [bQ], [bQ])
            k.tt("dve", flat(Q["wlq"]), flat(Q["wlq"]), flat(Q["wdq"]), ALU.mult, [bQ], [bQ])

    st = {"n": 0, "nf": 0}

    def load_chunk(c, full, bufs=None):
        i = st["n"] % NB
        st["n"] += 1
        g = off + c * 128
        xt, Bt, bi = xts[i], Bts[i], bin_[i]
        deps = [self.bS(self.S_xtok, g), self.bS(self.S_Btok, g), self.bS(self.S_bcT, g)]
        k.dma("sp", xt[:].rearrange("p h q -> p (h q)"), self.S_xtok[g:g + 128, :], deps, [bi])
        k.dma("sp", Bt[:], self.S_Btok[g:g + 128, :], deps, [bi])
        if full:
            k.dma("sp", bufs[i][:], self.S_bcT[:, g:g + 128].rearrange("(r p) t -> p r t", p=128), deps, [bi])
        return i, xt, Bt, bi

    if mode == "local":
        Sloc = [ph.enter_context(self.sbt(f"Sloc{d}", [128, 2048], F32)) for d in range(2)]
        bSl = [k.buf(), k.buf()]
        wl2 = ph.enter_context(self.sbt("wlq2", [128, NO, 32], F32))
        prep(0)
        k.copy("pool", wl2[:], Q["wlq"][:], [bQ], [bQ])
        prep(1)
        wls = [wl2, Q["wlq"]]
        xd2 = [ph.enter_context(self.sbt(f"xdtdb{i}", [128, 32, 64], BF16)) for i in range(NB)]
        bxd2 = [k.buf() for _ in range(NB)]
        for c in range(NO):
            i, xt, Bt, bi = load_chunk(c, False)
            for d in range(2):
                xd, bxd = (xdtd[i], bxdtd[i]) if d == 0 else (xd2[i], bxd2[i])
                k.tt("pool" if (c + d) % 2 == 0 else "dve", xd[:], xt[:], bc(wls[d][:, c, :].unsqueeze(2), [128, 32, 64]),
                     ALU.mult, [bi, bQ], [bxd])
                for g4 in range(4):
                    k.mm(self.ps[4 * d + g4][:], Bt[:, g4 * 128:(g4 + 1) * 128],
                         xd[:, g4 * 8:(g4 + 1) * 8, :].rearrange("p e q -> p (e q)"), c == 0, c == NO - 1,
                         [bi, bxd], [self.psb[4 * d + g4]])
        for d in range(2):
            for g4 in range(4):
                k.copy("act" if g4 % 2 == 0 else "dve", Sloc[d][:, g4 * 512:(g4 + 1) * 512], self.ps[4 * d + g4][:],
                       [self.psb[4 * d + g4]], [bSl[d]])
            sh = self.St_in[job][d]
            k.dma("pool", sh.ap()[:, :], Sloc[d][:], [bSl[d]], [self.ccb(sh)])
        k.dma("pool", self.T_in.ap()[:, job * 64:(job + 1) * 64], Tq[:], [bTq], [self.ccb(self.T_in)])
        return

    normrep = ph.enter_context(self.sbt("normrep", [128, 2048], F32))
    negf = ph.enter_context(self.sbt("negf", [128, 2, 512], F32))
    negm = ph.enter_context(self.sbt("negm", [128, 2, 512], BF16))
    k.dma("sp", normrep[:], I["ssmnorm"][:, :], [], [bcs])
    k.dma("sp", negf[:], I["negm"].rearrange("p (a l) -> p a l", a=2), [], [bcs])
    k.copy("dve", negm[:], negf[:], [bcs], [bcs])
    Tg = ph.enter_context(self.sbt("Tg", [128, 4, 64 * nj], F32))
    cf = ph.enter_context(self.sbt("cf", [128, 4, 32], F32))
    bTg, bcf = k.buf(), k.buf()
    k.dma("sp", Tg[:], self.T_out.ap().rearrange("(r p) f -> p r f", p=128), [self.ccb(self.T_out)], [bTg])
    hpb = ph.enter_context(self.sbt("hprevb", [128, 2048], BF16))
    bhpb = k.buf()
    bcTs = [ph.enter_context(self.sbt(f"bcT{i}", [128, 8, 128], BF16)) for i in range(2)]
    Rt = ph.enter_context(self.sbt("Rt", [128, 32, 128], F32))
    Lt = ph.enter_context(self.sbt("Lt", [128, 32, 128], F32))
    MT = ph.enter_context(self.sbt("MT", [128, 32, 128], BF16))
    MT2 = ph.enter_context(self.sbt("MT2", [128, 32, 128], BF16))
    bR, bLt, bMT = k.buf(), k.buf(), k.buf()
    yacc = [ph.enter_context(self.sbt(f"yacc{i}", [128, 2048], F32)) for i in range(2)]
    byacc = [k.buf(), k.buf()]
    aux = [ph.enter_context(self.sbt(f"yaux{i}", [128, 2048], F32)) for i in range(2)]
    baux = [k.buf(), k.buf()]
    tmp2 = [ph.enter_context(self.sbt(f"ytmp{i}", [128, 512], F32)) for i in range(2)]
    btmp2 = [k.buf(), k.buf()]
    ssq = ph.enter_context(self.sbt("ssq", [128, 8], F32))
    bssq = k.buf()
    sbf = ph.enter_context(self.sbt("sbf", [128, 2048], BF16))
    sTs = [ph.enter_context(self.sbt(f"sTs{i}", [128, 16, 128], BF16)) for i in range(2)]
    bsbf = k.buf()
    bsTs = [k.buf(), k.buf()]

    def combine(d):
        dsl = slice(job * 64 + d * 32, job * 64 + (d + 1) * 32)
        for r in range(4):
            first = True
            for r2 in range(4):
                if (d == 0 and not (r < r2)) or (d == 1 and not (r2 < r)):
                    continue
                mcol = 16 + d * 16 + r * 4 + r2
                if first:
                    k.ts("dve", cf[:, r, :], Tg[:, r2, dsl], cmk[:, mcol:mcol + 1], None, ALU.mult, None,
                         [bTg, bcs], [bcf])
                    first = False
                else:
                    k.stt(cf[:, r, :], Tg[:, r2, dsl], cmk[:, mcol:mcol + 1], cf[:, r, :], ALU.mult, ALU.add,
                          [bTg, bcs, bcf], [bcf])
            if first:
                k.memset("dve", cf[:, r, :], 0.0, [bcf])
        k.act(cf[:].rearrange("p r h -> p (r h)"), cf[:].rearrange("p r h -> p (r h)"), AF.Exp, [bcf], [bcf])
        for r in range(4):
            k.ts("dve", cf[:, r, :], cf[:, r, :], cmk[:, 8 + d * 4 + r:9 + d * 4 + r], None, ALU.mult, None,
                 [bcf, bcs], [bcf])
        so = self.St_out[job][d]
        for r in range(4):
            ax, bax = aux[r % 2], baux[r % 2]
            k.dma("sp", ax[:], so.ap()[r * 128:(r + 1) * 128, :], [self.ccb(so)], [bax])
            cb_ = bc(cf[:, r, :].unsqueeze(2), [128, 32, 64])
            av = ax[:].rearrange("p (h q) -> p h q", q=64)
            hv = hst[:].rearrange("p (h q) -> p h q", q=64)
            if r == 0:
                k.tt("dve", hv, av, cb_, ALU.mult, [bax, bcf], [bhst])
            else:
                k.tt("pool", av, av, cb_, ALU.mult, [bax, bcf], [bax])
                k.tt("dve", hst[:], hst[:], ax[:], ALU.add, [bhst, bax], [bhst])

    MTs = [MT, MT2]
    bMTs = [bMT, k.buf()]
    ctx = {}

    def stageA(c, d):
        i, xt, Bt, bi = load_chunk(c, True, bcTs)
        bcT = bcTs[i]
        bs = bQ
        a_, nac_, wd_ = (Q[n][:, c, :] for n in ("aq", "nacq", "wdq"))
        xd, bxd = xdtd[i], bxdtd[i]
        k.tt("pool", xd[:], xt[:], bc(wd_.unsqueeze(2), [128, 32, 64]), ALU.mult, [bi, bs], [bxd])
        j = st["nf"] % 2
        st["nf"] += 1
        ax, bax = aux[j], baux[j]
        mt, bmt = MTs[j], bMTs[j]
        oo = off + c * 128
        k.tt("pool", Rt[:], bc(a_.unsqueeze(2), [128, 32, 128]), bc(tri[:, d, :].unsqueeze(1), [128, 32, 128]),
             ALU.mult, [bs, bcs], [bR])
        if d == 0:
            k.tt("pool", ax[:].rearrange("p (h q) -> p h q", q=64), xt[:], bc(Drep.unsqueeze(2), [128, 32, 64]),
                 ALU.mult, [bi, bcs], [bax])
        else:
            k.dma("sp", ax[:], self.S_y[oo:oo + 128, :], [self.bS(self.S_y, oo)], [bax])
        ctx[c] = (i, j)

    def stageA2(c, d):
        i, j = ctx[c]
        bi, bcT = bin_[i], bcTs[i]
        bs = bQ
        nac_ = Q["nacq"][:, c, :]
        mt, bmt = MTs[j], bMTs[j]
        for hg in range(8):
            ps, bps = self.psum()
            k.mm(ps[:], self.identB, negm[:, d, :], True, False, [self.cB, bcs], [bps])
            k.mm(ps[:], self.onesF, Rt[:, hg * 4:(hg + 1) * 4, :].rearrange("p h l -> p (h l)"), False, True,
                 [self.cB, bR], [bps])
            for hh in range(4):
                h = hg * 4 + hh
                k.act(Lt[:, h, :], ps[:, hh * 128:(hh + 1) * 128], AF.Exp, [bps, bs], [bLt], bias=nac_[:, h:h + 1])
        pcb, bpcb = self.psum()
        for g4 in range(4):
            k.mm(pcb[:, g4 * 128:(g4 + 1) * 128], bcT[:, g4, :], bcT[:, 4 + g4, :], True, True, [bi], [bpcb])
        for g4 in range(4):
            k.tt("dve", mt[:, g4 * 8:(g4 + 1) * 8, :], Lt[:, g4 * 8:(g4 + 1) * 8, :],
                 bc(pcb[:, g4 * 128:(g4 + 1) * 128].unsqueeze(1), [128, 8, 128]), ALU.mult, [bLt, bpcb], [bmt])

    def stageB(c, d):
        i, j = ctx[c]
        xt, Bt, bi, bcT = xts[i], Bts[i], bin_[i], bcTs[i]
        xd, bxd = xdtd[i], bxdtd[i]
        mt, bmt = MTs[j], bMTs[j]
        ya, bya = yacc[j], byacc[j]
        bs = bQ
        eA_, cd_ = Q["eAq"][:, c, :], Q["cdq"][:, c, :]
        k.copy("act", hpb[:], hst[:], [bhst], [bhpb])
        for g4 in range(4):
            psy, bpsy = self.psum()
            for e in range(8):
                h = g4 * 8 + e
                k.mm(psy[:, e * 64:(e + 1) * 64], mt[:, h, :], xt[:, h, :], True, True, [bmt, bi], [bpsy])
            pso, bpso = self.psum()
            k.mm(pso[:], bcT[:, 4 + g4, :], hpb[:, g4 * 512:(g4 + 1) * 512], True, True, [bi, bhpb], [bpso])
            t2, bt2 = tmp2[g4 % 2], btmp2[g4 % 2]
            k.tt("dve", t2[:].rearrange("p (e q) -> p e q", q=64), pso[:].rearrange("p (e q) -> p e q", q=64),
                 bc(eA_[:, g4 * 8:(g4 + 1) * 8].unsqueeze(2), [128, 8, 64]), ALU.mult, [bpso, bs], [bt2])
            k.tt("dve", ya[:, g4 * 512:(g4 + 1) * 512], psy[:], t2[:], ALU.add, [bpsy, bt2], [bya])
        for g4 in range(4):
            psS, bpsS = self.psum()
            k.mm(psS[:], Bt[:, g4 * 128:(g4 + 1) * 128], xd[:, g4 * 8:(g4 + 1) * 8, :].rearrange("p e q -> p (e q)"),
                 True, True, [bi, bxd], [bpsS])
            hv = hst[:, g4 * 512:(g4 + 1) * 512].rearrange("p (e q) -> p e q", q=64)
            k.tt("dve", hv, hv, bc(cd_[:, g4 * 8:(g4 + 1) * 8].unsqueeze(2), [128, 8, 64]), ALU.mult, [bhst, bs], [bhst])
            k.tt("dve", hst[:, g4 * 512:(g4 + 1) * 512], hst[:, g4 * 512:(g4 + 1) * 512], psS[:], ALU.add,
                 [bhst, bpsS], [bhst])

    def stageC(c, d):
        i, j = ctx.pop(c)
        ya, bya = yacc[j], byacc[j]
        ax, bax = aux[j], baux[j]
        oo = off + c * 128
        k.tt("pool", ya[:], ya[:], ax[:], ALU.add, [bya, bax], [bya])
        if d == 0:
            k.dma("pool", self.S_y[oo:oo + 128, :], ya[:], [bya], [self.bS(self.S_y, oo)])
            return
        k.dma("sp", ax[:], self.S_sz[oo:oo + 128, :], [self.bS(self.S_sz, oo)], [bax])
        k.tt("pool", ya[:], ya[:], ax[:], ALU.mult, [bya, bax], [bya])
        k.act(ax[:], ya[:], AF.Square, [bya], [bax])
        k.op("dve", lambda e, o_=ssq[:, 0:4], i_=ax[:].rearrange("p (g c) -> p g c", g=4):
             e.tensor_reduce(out=o_, in_=i_, axis=AX.X, op=ALU.add), [bax], [bssq])
        k.act(ssq[:, 4:8], ssq[:, 0:4], AF.Sqrt, [bssq, self.cB], [bssq], bias=self.epsT, scale=1.0 / 512)
        k.recip(ssq[:, 4:8], ssq[:, 4:8], [bssq], [bssq])
        for g4 in range(4):
            k.stt(sbf[:, g4 * 512:(g4 + 1) * 512], ya[:, g4 * 512:(g4 + 1) * 512], ssq[:, 4 + g4:5 + g4],
                  normrep[:, g4 * 512:(g4 + 1) * 512], ALU.mult, ALU.mult, [bya, bssq, bcs], [bsbf])
        sT, bsT = sTs[j], bsTs[j]
        for hb in range(2):
            ps, bps = self.psum()
            psb = ps[:].bitcast(BF16)
            for q in range(8):
                ct = hb * 8 + q
                k.tr(psb[:, q * 128:(q + 1) * 128], sbf[:, ct * 128:(ct + 1) * 128], self.identB,
                     [bsbf, self.cB], [bps])
            k.copy("act", sT[:, hb * 8:(hb + 1) * 8, :], psb.rearrange("p (c t) -> p c t", c=8), [bps], [bsT])
        k.dma("pool", self.S_sT[:, oo:oo + 128].rearrange("(c p) t -> p c t", p=128), sT[:], [bsT],
              [self.bS(self.S_sT, oo)])

    def run_pass(order, d):
        stageA(order[0], d)
        stageA2(order[0], d)
        for n_, c in enumerate(order):
            nxt = order[n_ + 1] if n_ + 1 < len(order) else None
            if nxt is not None:
                stageA(nxt, d)
            stageB(c, d)
            if nxt is not None:
                stageA2(nxt, d)
            stageC(c, d)

    prep(0)
    combine(0)
    run_pass(list(range(NO)), 0)
    prep(1)
    combine(1)
    run_pass(list(range(NO - 1, -1, -1)), 1)
```

```python
import contextlib
import math
import numpy as np
import concourse.bass as bass
import concourse.mybir as mybir
from concourse.bass_utils import run_bass_kernel_spmd

F32 = mybir.dt.float32
BF16 = mybir.dt.bfloat16
AF = mybir.ActivationFunctionType
ALU = mybir.AluOpType
AX = mybir.AxisListType

D = 1024
DFF = 2816
NFT = DFF // 128
EPS = 1e-6
D_INNER = 2048
NH = 32
HP = 64
NG = 4
DS = 128
CONV_DIM = 3072
GRID_W = 64
IN_SIZES = (2048, 3072, 32, 32, 1024, 256, 256, 2048)
IN_OFF = [0]
for _s in IN_SIZES:
    IN_OFF.append(IN_OFF[-1] + _s)
OFF_Z, OFF_XBC, OFF_DTF, OFF_DTB, OFF_Q, OFF_K, OFF_V, OFF_G = IN_OFF[:8]
NEG = -30000.0


class Buf:
    __slots__ = ("w", "rc", "rd")

    def __init__(self, w=None):
        self.w = w
        self.rc = {}
        self.rd = []


class Op:
    __slots__ = ("eng", "fn", "dma", "deps", "flag", "sem", "val", "idx", "waits", "cc")


ENGS = ("pe", "act", "dve", "pool", "sp")


class KB:
    def __init__(self, nc, es):
        self.nc = nc
        self.ops = {e: [] for e in ENGS}
        self.csem = {e: es.enter_context(nc.semaphore("c_" + e)) for e in ENGS}
        self.npool = {"sp": 10, "pool": 6, "act": 4}
        self.dsem = {q: [es.enter_context(nc.semaphore(f"d_{q}{i}")) for i in range(n)]
                     for q, n in self.npool.items()}
        self.ccsems = [es.enter_context(nc.semaphore(f"cc{i}")) for i in range(16)]
        self.dcnt = {q: 0 for q in self.npool}
        self.dlast = {q: [None] * n for q, n in self.npool.items()}
        self.barrier_op = None
        self.bufs = []

    def buf(self):
        b = Buf(self.barrier_op)
        self.bufs.append(b)
        return b

    def op(self, eng, fn, reads=(), writes=(), dma=False):
        o = Op()
        o.eng, o.fn, o.dma, o.flag, o.deps = eng, fn, dma, False, []
        o.sem = None
        o.val = 0
        o.cc = False

        def dep(p, raw):
            if p is None:
                return
            if (not p.dma) and (not dma) and p.eng == eng:
                if not (raw and eng != "pe"):
                    return
            o.deps.append(p)

        for b in reads:
            dep(b.w, True)
        for b in writes:
            dep(b.w, False)
            for r in b.rc.values():
                dep(r, False)
            for r in b.rd:
                dep(r, False)
        for b in reads:
            if dma:
                b.rd.append(o)
            else:
                b.rc[eng] = o
        for b in writes:
            b.w = o
            b.rc = {}
            b.rd = []
        self.ops[eng].append(o)
        o.idx = len(self.ops[eng])
        if dma:
            n = self.dcnt[eng]
            kq = self.npool[eng]
            slot = n % kq
            o.sem = self.dsem[eng][slot]
            o.val = 16 * (n // kq + 1)
            prev = self.dlast[eng][slot]
            if prev is not None:
                o.deps.append(prev)
            self.dlast[eng][slot] = o
            self.dcnt[eng] = n + 1
        return o

    def cc(self, in_h, out_h, R, W):
        if not hasattr(self, "ccops"):
            self.ccops = []
        i = len(self.ccops)
        sem = self.ccsems[i]
        groups = [[0, 1, 2, 3], [4, 5, 6, 7]]
        o = self.op("pool", lambda e: e.collective_compute("AllGather", ALU.bypass, replica_groups=groups,
                                                           ins=[in_h.ap().opt()], outs=[out_h.ap().opt()]), R, W)
        o.dma = True
        o.cc = True
        o.sem = sem
        o.val = 1
        for b in R:
            if b.rc.get("pool") is o:
                del b.rc["pool"]
                b.rd.append(o)
        self.ccops.append(o)
        return o

    def barrier(self):
        allb = self.bufs
        sc = self._bar_scratch
        o = self.op("dve", lambda e: e.memset(sc[:, 0:1], 0.0), reads=(), writes=allb)
        for q in self.dlast:
            for p in self.dlast[q]:
                if p is not None:
                    o.deps.append(p)
        for p in getattr(self, "ccops", []):
            o.deps.append(p)
        for e in ENGS:
            if e != "dve" and self.ops[e]:
                for last in reversed(self.ops[e]):
                    if not last.dma:
                        o.deps.append(last)
                        break
        self.barrier_op = o
        self.bufs = []
        return o

    def finalize(self):
        for e in ENGS:
            seen = {}
            for o in self.ops[e]:
                waits = []
                for p in o.deps:
                    if p.dma:
                        key, val = id(p.sem), p.val
                    else:
                        key, val = p.eng, p.idx
                    if seen.get(key, 0) >= val:
                        continue
                    seen[key] = val
                    waits.append(p)
                    p.flag = True
                o.waits = waits
        for e in ENGS:
            c = 0
            for o in self.ops[e]:
                if not o.dma and o.flag:
                    c += 1
                    o.val = c
                    o.sem = self.csem[e]

    def emit(self, eng, h):
        for o in self.ops[eng]:
            for p in o.waits:
                h.wait_ge(p.sem, p.val)
            ins = o.fn(h)
            if o.cc:
                ins.then_inc(o.sem)
            elif o.dma:
                ins.then_inc(o.sem, 16)
            elif o.flag:
                ins.then_inc(o.sem, 1)
        if eng == "sp":
            for q in self.dlast:
                for p in self.dlast[q]:
                    if p is not None:
                        h.wait_ge(p.sem, p.val)

    def mm(self, out, lhsT, rhs, start, stop, R, W):
        return self.op("pe", lambda e: e.matmul(out, lhsT=lhsT, rhs=rhs, start=start, stop=stop), R, W)

    def tr(self, out, in_, ident, R, W):
        return self.op("pe", lambda e: e.transpose(out, in_, ident), R, W)

    def act(self, out, in_, func, R, W, bias=None, scale=None, eng="act"):
        kw = {}
        if bias is not None:
            kw["bias"] = bias
        if scale is not None:
            kw["scale"] = scale
        return self.op(eng, lambda e: e.activation(out=out, in_=in_, func=func, **kw), R, W)

    def tt(self, eng, out, in0, in1, op, R, W):
        return self.op(eng, lambda e: e.tensor_tensor(out=out, in0=in0, in1=in1, op=op), R, W)

    def ts(self, eng, out, in0, s1, s2, op0, op1, R, W):
        if op1 is None:
            return self.op(eng, lambda e: e.tensor_scalar(out=out, in0=in0, scalar1=s1, scalar2=None, op0=op0), R, W)
        return self.op(eng, lambda e: e.tensor_scalar(out=out, in0=in0, scalar1=s1, scalar2=s2, op0=op0, op1=op1), R, W)

    def stt(self, out, in0, scalar, in1, op0, op1, R, W):
        return self.op("dve", lambda e: e.scalar_tensor_tensor(out=out, in0=in0, scalar=scalar, in1=in1,
                                                                op0=op0, op1=op1), R, W)

    def copy(self, eng, out, in_, R, W):
        if eng == "act":
            return self.op("act", lambda e: e.activation(out=out, in_=in_, func=AF.Copy), R, W)
        return self.op(eng, lambda e: e.tensor_copy(out=out, in_=in_), R, W)

    def memset(self, eng, ap, val, W):
        return self.op(eng, lambda e: e.memset(ap, val), (), W)

    def recip(self, out, in_, R, W):
        return self.op("dve", lambda e: e.reciprocal(out=out, in_=in_), R, W)

    def dma(self, q, out, in_, R, W):
        return self.op(q, lambda e: e.dma_start(out=out, in_=in_), R, W, dma=True)


class Prog:
    def __init__(self, jobs, debug=None, cc=True):
        self.jobs = jobs
        self.cc = cc
        self.debug = debug or ()
        self.LT = sum(L for L, _ in jobs)
        self.LOT = sum(LO for _, LO in jobs)
        if cc:
            self.LT = self.LOT

    def dram(self, name, shape, dt, kind="Internal"):
        if name in self.debug:
            kind = "ExternalOutput"
        return self.nc.dram_tensor(name, list(shape), dt, kind=kind).ap()

    def build(self):
        nc = bass.Bass("TRN2", target_bir_lowering=False)
        self.nc = nc
        LT, LOT = self.LT, self.LOT
        I = {}

        def inp(name, shape, dt=F32):
            I[name] = nc.dram_tensor(name, list(shape), dt, kind="ExternalInput").ap()

        inp("x", [LT, D])
        inp("w1gu", [D, 2 * DFF]); inp("w1d", [DFF, D]); inp("g1", [128, 8])
        inp("w2gu", [D, 2 * DFF]); inp("w2d", [DFF, D]); inp("g2", [128, 8])
        inp("gfin", [128, 8])
        inp("win", [D, 8768]); inp("gmix", [128, 8]); inp("gq", [128, 1]); inp("gk", [128, 1])
        self.TB = min(1024, min(LO for _, LO in self.jobs))
        self.nblk_total = sum(L // self.TB for L, _ in self.jobs)
        inp("convw", [128, 24, 5]); inp("convb", [128, 24]); (None if self.cc else inp("cmask", [128, 2 * self.nblk_total]))
        inp("dtbias", [128, 64]); inp("alog", [128, 64]); inp("dskip", [128, 32]); inp("ssmnorm", [128, 2048])
        (None if self.cc else inp("smask", [128, 8 * len(self.jobs)])); inp("tri", [128, 256]); inp("negm", [128, 1024])
        inp("wsb", [2048, D]); inp("wab", [D, D]); inp("wout", [D, D])
        inp("bgate", [128, 16]); inp("cosT", [128, LT]); inp("sinT", [128, LT]); inp("pm", [128, 128])
        self.I = I
        self.y = nc.dram_tensor("y", [LOT, D], F32, kind="ExternalOutput").ap()
        self.S_h = self.dram("S_h", [D, LT], F32)
        self.S_xpre_l = [self.dram(f"S_xpre{i}", [CONV_DIM, LO if self.cc else L], F32)
                         for i, (L, LO) in enumerate(self.jobs)]
        if self.cc:
            inp("ccmask", [128, 64])
            nj = len(self.jobs)
            dt_ = lambda n, sh, d: nc.dram_tensor(n, list(sh), d)
            self.E_in = dt_("E_in", [128, 24 * nj * 4], F32)
            self.E_out = dt_("E_out", [4 * 128, 24 * nj * 4], F32)
            self.T_in = dt_("T_in", [128, 64 * nj], F32)
            self.T_out = dt_("T_out", [4 * 128, 64 * nj], F32)
            self.K_in = [[dt_(f"K_in{j}_{kv}", [128, LO], BF16) for kv in range(2)] for j, (_, LO) in enumerate(self.jobs)]
            self.K_out = [[dt_(f"K_out{j}_{kv}", [4 * 128, LO], BF16) for kv in range(2)] for j, (_, LO) in enumerate(self.jobs)]
            self.V_in = [[dt_(f"V_in{j}_{kv}", [LO, 128], BF16) for kv in range(2)] for j, (_, LO) in enumerate(self.jobs)]
            self.V_out = [[dt_(f"V_out{j}_{kv}", [4 * LO, 128], BF16) for kv in range(2)] for j, (_, LO) in enumerate(self.jobs)]
            self.St_in = [[dt_(f"St_in{j}_{d}", [128, 2048], F32) for d in range(2)] for j in range(nj)]
            self.St_out = [[dt_(f"St_out{j}_{d}", [4 * 128, 2048], F32) for d in range(2)] for j in range(nj)]
            self.bcc = {}
        self.S_dt = self.dram("S_dt", [LT, 64], F32)
        self.S_v = self.dram("S_v", [LT, 256], BF16)
        self.S_kT = self.dram("S_kT", [256, LT], BF16)
        self.S_sz = self.dram("S_sz", [LOT, 2048], F32)
        self.S_bcT = self.dram("S_bcT", [1024, LT], BF16)
        self.S_y = self.dram("S_y", [LOT, 2048], F32)
        self.S_aT = self.dram("S_aT", [1024, LOT], BF16)
        self.S_h2 = self.dram("S_h2", [D, LOT], F32)
        self.S_sT = self.dram("S_sT", [2048, LOT], BF16)
        self.S_xtok = self.dram("S_xtok", [LT, 2048], BF16)
        self.S_Btok = self.dram("S_Btok", [LT, 512], BF16)
        self.S_qT = self.dram("S_qT", [1024, LOT], BF16)
        self.S_gT = self.dram("S_gT", [2048, LOT], F32)
        with contextlib.ExitStack() as es:
            self.es = es
            k = KB(nc, es)
            self.k = k
            self.psall = es.enter_context(nc.psum_tensor("psall", [128, 8, 512], F32))
            self.ps = [self.psall[:, i, :] for i in range(8)]
            self.psb = [k.buf() for _ in range(8)]
            self.psi = 0
            cst = es.enter_context(nc.sbuf_tensor("cst", [128, 704], F32))
            cstb = es.enter_context(nc.sbuf_tensor("cstb", [128, 512], BF16))
            k._bar_scratch = cst[:, 700:701]
            self.cB = k.buf()
            self.identF = cst[:, 0:128]
            self.epsT = cst[:, 128:129]
            self.onesB = cstb[:, 0:128]
            self.identB = cstb[:, 128:256]
            k.memset("dve", cst[:, 0:128], 0.0, [self.cB])
            k.memset("dve", cst[:, 128:129], EPS, [self.cB])
            k.memset("dve", cstb[:, 0:128], 1.0, [self.cB])
            k.memset("pool", cst[:, 256:384], 1.0, [self.cB])
            k.op("pool", lambda e: e.affine_select(out=cst[:, 0:128], in_=cst[:, 256:384], pattern=[[-1, 128]],
                                                   compare_op=ALU.is_equal, fill=0.0, base=0,
                                                   channel_multiplier=1), [self.cB], [self.cB])
            k.copy("dve", cstb[:, 128:256], cst[:, 0:128], [self.cB], [self.cB])
            self.PmF = cst[:, 384:512]
            self.onesF = cst[:, 512:640]
            self.oneT = cst[:, 512:513]
            k.memset("dve", cst[:, 512:640], 1.0, [self.cB])
            k.dma("sp", cst[:, 384:512], I["pm"][:, :], [], [self.cB])
            k.barrier()
            if self.cc:
                self.body2()
            else:
                self.body()
            k.barrier()
            k.finalize()
            with nc.Block() as block:
                @block.tensor
                def _(h):
                    k.emit("pe", h)

                @block.scalar
                def _(h):
                    k.emit("act", h)

                @block.vector
                def _(h):
                    k.emit("dve", h)

                @block.gpsimd
                def _(h):
                    k.emit("pool", h)

                @block.sync
                def _(h):
                    k.emit("sp", h)
        return nc

    def sbt(self, name, shape, dt):
        return self.nc.sbuf_tensor(f"{name}_{self.uid()}", shape, dt)

    def psum(self):
        i = self.psi
        self.psi = (i + 1) % 8
        return self.ps[i], self.psb[i]

    def jobof(self, g):
        o = 0
        for j, (L, LO) in enumerate(self.jobs):
            if g < o + LO:
                return j, g - o
            o += LO
        raise ValueError(g)

    def ccb(self, h):
        if id(h) not in self.bcc:
            self.bcc[id(h)] = Buf(None)
        return self.bcc[id(h)]

    def body2(self):
        k = self.k
        offs = []
        o = 0
        for (L, LO) in self.jobs:
            offs.append(o)
            o += LO
        nj = len(self.jobs)
        LOT = self.LOT
        with contextlib.ExitStack() as ph:
            self.ffn_phase(ph, "w1gu", "w1d", "g1", LOT, src=("tok", self.I["x"], 0), dst=("feat", self.S_h, 0))
        k.barrier()
        with contextlib.ExitStack() as ph:
            self.inproj_phase(ph, "a", LOT, 0, 0)
        k.barrier()
        with contextlib.ExitStack() as ph:
            self.inproj_phase(ph, "b", LOT, 0, 0)
        k.barrier()
        for j, (L, LO) in enumerate(self.jobs):
            xp = self.S_xpre_l[j]
            ev = self.E_in.ap().rearrange("p (c j e) -> p c j e", c=24, j=nj)
            rows = xp[:, :].rearrange("(c p) t -> p c t", p=128)
            deps = [self.bS(xp, t) for t in range(0, LO, 512)]
            k.dma("sp", ev[:, :, j, 0:2], rows[:, :, 0:2], deps, [self.ccb(self.E_in)])
            k.dma("sp", ev[:, :, j, 2:4], rows[:, :, LO - 2:LO], deps, [self.ccb(self.E_in)])
        k.cc(self.E_in, self.E_out, [self.ccb(self.E_in)], [self.ccb(self.E_out)])
        for j in range(nj):
            for kv in range(2):
                k.cc(self.K_in[j][kv], self.K_out[j][kv], [self.ccb(self.K_in[j][kv])], [self.ccb(self.K_out[j][kv])])
                k.cc(self.V_in[j][kv], self.V_out[j][kv], [self.ccb(self.V_in[j][kv])], [self.ccb(self.V_out[j][kv])])
        blk0 = 0
        for j, (L, LO) in enumerate(self.jobs):
            self.S_xpre = self.S_xpre_l[j]
            self.job = j
            with contextlib.ExitStack() as ph:
                self.conv_phase(ph, LO, LO, offs[j], blk0)
            k.barrier()
            with contextlib.ExitStack() as ph:
                self.ssd2_phase(ph, LO, offs[j], j, "local")
            k.barrier()
            blk0 += LO // self.TB
        k.cc(self.T_in, self.T_out, [self.ccb(self.T_in)], [self.ccb(self.T_out)])
        for j in range(nj):
            for d in range(2):
                k.cc(self.St_in[j][d], self.St_out[j][d], [self.ccb(self.St_in[j][d])], [self.ccb(self.St_out[j][d])])
        for j, (L, LO) in enumerate(self.jobs):
            self.job = j
            with contextlib.ExitStack() as ph:
                self.ssd2_phase(ph, LO, offs[j], j, "own")
            k.barrier()
            with contextlib.ExitStack() as ph:
                self.attn_phase(ph, L, LO, offs[j], offs[j])
            k.barrier()
        with contextlib.ExitStack() as ph:
            self.merge_phase(ph, LOT, 0, 0)
        k.barrier()
        with contextlib.ExitStack() as ph:
            self.ffn_phase(ph, "w2gu", "w2d", "g2", LOT, src=("feat", self.S_h2, 0), dst=("final", self.y, 0))
        k.barrier()

    def body(self):
        off = 0
        ooff = 0
        blk0 = 0
        job = 0
        for (L, LO) in self.jobs:
            self.S_xpre = self.S_xpre_l[job]
            with contextlib.ExitStack() as ph:
                self.ffn_phase(ph, "w1gu", "w1d", "g1", L, src=("tok", self.I["x"], off),
                               dst=("feat", self.S_h, off))
            self.k.barrier()
            with contextlib.ExitStack() as ph:
                self.inproj_phase(ph, "a", L, off, ooff)
            self.k.barrier()
            with contextlib.ExitStack() as ph:
                self.inproj_phase(ph, "b", LO, off, ooff)
            self.k.barrier()
            with contextlib.ExitStack() as ph:
                self.conv_phase(ph, L, LO, off, blk0)
            self.k.barrier()
            with contextlib.ExitStack() as ph:
                self.ssd_phase(ph, L, LO, off, ooff, job)
            self.k.barrier()
            with contextlib.ExitStack() as ph:
                self.attn_phase(ph, L, LO, off, ooff)
            self.k.barrier()
            with contextlib.ExitStack() as ph:
                self.merge_phase(ph, LO, off, ooff)
            self.k.barrier()
            with contextlib.ExitStack() as ph:
                self.ffn_phase(ph, "w2gu", "w2d", "g2", LO, src=("feat", self.S_h2, ooff), dst=("final", self.y, ooff))
            self.k.barrier()
            off += L
            ooff += LO
            job += 1
            blk0 += L // self.TB

    def ffn_phase(self, ph, wgu_name, wd_name, g_name, ntok, src, dst):
        nc, k, I = self.nc, self.k, self.I
        T = 512
        Wgu = ph.enter_context(self.sbt("Wgu", [128, 8, 2 * DFF], BF16))
        Wd = ph.enter_context(self.sbt("Wd", [128, NFT, D], BF16))
        gT = ph.enter_context(self.sbt("gT", [128, 8], F32))
        xT = ph.enter_context(self.sbt("xT", [128, 8, T], F32))
        xn = ph.enter_context(self.sbt("xn", [128, 8, T], BF16))
        actb = ph.enter_context(self.sbt("actb", [128, NFT, T], BF16))
        xin = [ph.enter_context(self.sbt(f"xin{i}", [128, 512], F32)) for i in range(2)]
        sg = [ph.enter_context(self.sbt(f"sg{i}", [128, T], F32)) for i in range(2)]
        rstd = ph.enter_context(self.sbt("rstd", [128, T], F32))
        hout = [ph.enter_context(self.sbt(f"hout{i}", [128, T], F32)) for i in range(2)]
        bWgu, bWd, bg, bxT, bxn, bact, brstd = (k.buf() for _ in range(7))
        bxin = [k.buf(), k.buf()]
        bsg = [k.buf(), k.buf()]
        bhout = [k.buf(), k.buf()]
        stgt = ph.enter_context(self.sbt("ffnstg", [128, 2 * 1408], F32))
        stg = stgt[:]
        bstg = [k.buf(), k.buf()]
        k.dma("sp", gT[:], I[g_name][:, :], [], [bg])
        gF = ph.enter_context(self.sbt("gF", [128, 8], F32))
        k.dma("sp", gF[:], I["gfin"][:, :], [], [bg])
        wgu = I[wgu_name].rearrange("(kt p) f -> p kt f", p=128)
        wd = I[wd_name].rearrange("(ft p) m -> p ft m", p=128)
        CH = 1408
        n = 0
        bWc = [k.buf() for _ in range(4)]
        bWdc = [k.buf() for _ in range(2)]
        SW = 1408
        for ci, c0 in enumerate((0, DFF, CH, DFF + CH)):
            for kt in range(8):
                eng = ("dve", "act", "dve", "act", "dve")[n % 5]
                sv_ = stg[:, (n % 2) * SW:(n % 2) * SW + CH]
                k.dma("sp", sv_, wgu[:, kt, c0:c0 + CH], [], [bstg[n % 2]])
                bw_ = bWc[(0, 2, 1, 3)[ci]]
                if eng == "act":
                    k.act(Wgu[:, kt, c0:c0 + CH], sv_, AF.Copy, [bstg[n % 2], bg], [bw_], scale=gT[:, kt:kt + 1])
                else:
                    k.ts(eng, Wgu[:, kt, c0:c0 + CH], sv_, gT[:, kt:kt + 1], None, ALU.mult, None,
                         [bstg[n % 2], bg], [bw_])
                n += 1
        for ft in range(NFT):
            eng = ("dve", "act", "dve", "act", "dve")[n % 5]
            sv_ = stg[:, (n % 2) * SW:(n % 2) * SW + D]
            k.dma("sp", sv_, wd[:, ft, :], [], [bstg[n % 2]])
            k.copy(eng, Wd[:, ft, :], sv_, [bstg[n % 2]], [bWdc[0 if ft < 12 else 1]])
            n += 1
        bWgu_of = lambda j, u: bWc[(2 if u else 0) + (0 if j < 11 else 1)]
        bWd_of = lambda j: bWdc[0 if j < 12 else 1]
        for t0 in range(0, ntok, T):
            if src[0] == "tok":
                xsrc, xoff = src[1], src[2]
                for s in range(4):
                    r0 = xoff + t0 + s * 128
                    for half in range(2):
                        xi, bxi = xin[half], bxin[half]
                        k.dma("sp", xi[:], xsrc[r0:r0 + 128, half * 512:(half + 1) * 512], [], [bxi])
                        ps, bps = self.psum()
                        for j in range(4):
                            k.tr(ps[:, j * 128:(j + 1) * 128], xi[:, j * 128:(j + 1) * 128], self.identF,
                                 [bxi, self.cB], [bps])
                        k.copy("act" if half == 0 else "dve", xT[:, half * 4:half * 4 + 4, s * 128:(s + 1) * 128],
                               ps[:].rearrange("p (a t) -> p a t", a=4), [bps], [bxT])
            else:
                hsrc, xoff = src[1], src[2]
                k.dma("sp", xT[:], hsrc[:, xoff + t0:xoff + t0 + T].rearrange("(kt p) t -> p kt t", p=128),
                      [self.bS(hsrc, xoff + t0)], [bxT])
            sq = actb[:, 0:8, :]
            k.act(sq, xT[:], AF.Square, [bxT], [bact])
            pst, bpst = self.psum()
            for kt in range(8):
                k.mm(pst[:], self.onesB, sq[:, kt, :], kt == 0, kt == 7, [bact, self.cB], [bpst])
            k.act(rstd[:], pst[:], AF.Sqrt, [bpst, self.cB], [brstd], bias=self.epsT, scale=1.0 / D)
            k.recip(rstd[:], rstd[:], [brstd], [brstd])
            for kt in range(8):
                k.tt("dve" if kt % 2 == 0 else "pool", xn[:, kt, :], xT[:, kt, :], rstd[:], ALU.mult,
                     [bxT, brstd], [bxn])
            for j in range(NFT):
                psg, bpsg = self.psum()
                psu, bpsu = self.psum()
                for kt in range(8):
                    k.mm(psg[:], Wgu[:, kt, j * 128:(j + 1) * 128], xn[:, kt, :], kt == 0, kt == 7,
                         [bWgu_of(j, 0), bxn], [bpsg])
                for kt in range(8):
                    k.mm(psu[:], Wgu[:, kt, DFF + j * 128:DFF + (j + 1) * 128], xn[:, kt, :], kt == 0, kt == 7,
                         [bWgu_of(j, 1), bxn], [bpsu])
                k.act(sg[j % 2][:], psg[:], AF.Silu, [bpsg], [bsg[j % 2]])
                k.tt("dve", actb[:, j, :], sg[j % 2][:], psu[:], ALU.mult, [bsg[j % 2], bpsu], [bact])
            for m in range(8):
                ps, bps = self.psum()
                for j in range(NFT):
                    k.mm(ps[:], Wd[:, j, m * 128:(m + 1) * 128], actb[:, j, :], j == 0, j == NFT - 1,
                         [bWd_of(j), bact], [bps])
                ho, bho = hout[m % 2], bhout[m % 2]
                if dst[0] == "final":
                    k.stt(xT[:, m, :], ps[:], 0.5, xT[:, m, :], ALU.mult, ALU.add, [bps, bxT], [bxT])
                    continue
                k.stt(ho[:], ps[:], 0.5, xT[:, m, :], ALU.mult, ALU.add, [bps, bxT], [bho])
                if dst[0] == "feat":
                    hd, doff = dst[1], dst[2]
                    k.dma("pool", hd[m * 128:(m + 1) * 128, doff + t0:doff + t0 + T], ho[:], [bho],
                          [self.bS(hd, doff + t0)])

            if dst[0] == "final":
                yd, doff = dst[1], dst[2]
                sq = actb[:, 0:8, :]
                k.act(sq, xT[:], AF.Square, [bxT], [bact])
                pst, bpst = self.psum()
                for kt in range(8):
                    k.mm(pst[:], self.onesB, sq[:, kt, :], kt == 0, kt == 7, [bact, self.cB], [bpst])
                k.act(rstd[:], pst[:], AF.Sqrt, [bpst, self.cB], [brstd], bias=self.epsT, scale=1.0 / D)
                k.recip(rstd[:], rstd[:], [brstd], [brstd])
                for kt in range(8):
                    k.stt(xT[:, kt, :], xT[:, kt, :], gF[:, kt:kt + 1], rstd[:], ALU.mult, ALU.mult,
                          [bxT, brstd, bg], [bxT])
                for s in range(4):
                    r0 = doff + t0 + s * 128
                    for half in range(2):
                        xi, bxi = xin[half], bxin[half]
                        ps, bps = self.psum()
                        for j in range(4):
                            kt = half * 4 + j
                            k.tr(ps[:, j * 128:(j + 1) * 128], xT[:, kt, s * 128:(s + 1) * 128], self.identF,
                                 [bxT, self.cB], [bps])
                        k.copy("act" if half == 0 else "dve", xi[:], ps[:], [bps], [bxi])
                        k.dma("pool", yd[r0:r0 + 128, half * 512:(half + 1) * 512], xi[:], [bxi], [])

    def bS(self, ap, t0):
        d = self.__dict__.setdefault("_sb", {})
        key = (id(ap), t0 // 512)
        if key not in d:
            d[key] = Buf(None)
        return d[key]


def _add_methods(cls):
    def deco(f):
        setattr(cls, f.__name__, f)
        return f
    return deco


@_add_methods(Prog)
def load_weight_cols(self, ph, W, bW, src, cols, gT, bg, chunked=False):
    nc, k = self.nc, self.k
    cache = self.__dict__.setdefault("_stgc", {})
    if cache.get("ph") is not ph:
        cache["ph"] = ph
        cache["stg"] = [ph.enter_context(self.sbt(f"wstg{i}", [128, 2048], F32)) for i in range(3)]
        cache["bst"] = [k.buf() for _ in range(3)]
    stg, bst = cache["stg"], cache["bst"]
    nst = len(stg)
    srcv = src.rearrange("(kt p) f -> p kt f", p=128)
    nkt = srcv.shape[1]
    n = 0
    o = 0
    if chunked:
        self._wch = []
    for (c0, cn) in cols:
        for cc in range(0, cn, 2048):
            w = min(2048, cn - cc)
            for kt in range(nkt):
                s, bs = stg[n % nst], bst[n % nst]
                if chunked and kt == 0:
                    bW = k.buf()
                    self._wch.append((o + cc, o + cc + w, bW))
                k.dma("sp", s[:, 0:w], srcv[:, kt, c0 + cc:c0 + cc + w], [], [bs])
                eng = ("dve", "act", "dve", "act", "dve")[n % 5]
                if gT is not None and eng == "act":
                    k.act(W[:, kt, o + cc:o + cc + w], s[:, 0:w], AF.Copy, [bs, bg], [bW], scale=gT[:, kt:kt + 1])
                elif gT is not None:
                    k.ts(eng, W[:, kt, o + cc:o + cc + w], s[:, 0:w], gT[:, kt:kt + 1], None, ALU.mult, None,
                         [bs, bg], [bW])
                else:
                    k.copy(eng, W[:, kt, o + cc:o + cc + w], s[:, 0:w], [bs], [bW])
                n += 1
        o += cn


@_add_methods(Prog)
def uid(self):
    self._uid = getattr(self, "_uid", 0) + 1
    return self._uid


@_add_methods(Prog)
def norm_tile(self, hT, bhT, un, bun, sq, bsq, rstd, brstd, T=512, nkt=8, dim=D):
    k = self.k
    k.act(sq, hT, AF.Square, [bhT], [bsq])
    pst, bpst = self.psum()
    for kt in range(nkt):
        k.mm(pst[:, 0:T], self.onesB, sq[:, kt, :], kt == 0, kt == nkt - 1, [bsq, self.cB], [bpst])
    k.act(rstd, pst[:, 0:T], AF.Sqrt, [bpst, self.cB], [brstd], bias=self.epsT, scale=1.0 / dim)
    k.recip(rstd, rstd, [brstd], [brstd])
    for kt in range(nkt):
        k.tt("dve" if kt % 2 == 0 else "pool", un[:, kt, :], hT[:, kt, :], rstd, ALU.mult, [bhT, brstd], [bun])


@_add_methods(Prog)
def qk_head(self, ps, bps, gcol, bgc, cosT, sinT, btab, tmp, btmp, outbf, bout, T=512, fill=None):
    k = self.k
    sqh, rs, xn, t1 = (t[:] for t in tmp)
    k.act(sqh, ps[:, 0:T], AF.Square, [bps], [btmp])
    if fill:
        fill()
    p2, bp2 = self.psum()
    k.mm(p2[:, 0:T], self.onesB, sqh, True, True, [btmp, self.cB], [bp2])
    k.act(rs, p2[:, 0:T], AF.Sqrt, [bp2, self.cB], [btmp], bias=self.epsT, scale=1.0 / 128)
    k.recip(rs, rs, [btmp], [btmp])
    k.stt(xn, ps[:, 0:T], gcol, rs, ALU.mult, ALU.mult, [bps, bgc, btmp], [btmp])
    if fill:
        fill()
    p3, bp3 = self.psum()
    k.mm(p3[:, 0:T], self.PmF, xn, True, True, [btmp, self.cB], [bp3])
    k.tt("pool", t1, xn, cosT, ALU.mult, [btmp, btab], [btmp])
    k.tt("dve", xn, p3[:, 0:T], sinT, ALU.mult, [bp3, btab], [btmp])
    k.tt("pool", outbf, t1, xn, ALU.add, [btmp], [bout])


@_add_methods(Prog)
def inproj_phase(self, ph, part, ntok, off, ooff):
    nc, k, I = self.nc, self.k, self.I
    T = 512
    if part == "a":
        cols = [(OFF_XBC, 3072), (OFF_DTF, 64), (OFF_K, 256), (OFF_V, 256)]
    else:
        cols = [(OFF_Z, 2048), (OFF_Q, 1024), (OFF_G, 2048)]
    ncol = sum(c for _, c in cols)
    W = ph.enter_context(self.sbt("Win", [128, 8, ncol], BF16))
    gT = ph.enter_context(self.sbt("gmixT", [128, 8], F32))
    sv = ph.enter_context(self.sbt("smallv", [128, 32], F32))
    bW, bg, bsv = k.buf(), k.buf(), k.buf()
    k.dma("sp", gT[:], I["gmix"][:, :], [], [bg])
    k.dma("sp", sv[:, 0:1], I["gq"][:, :], [], [bsv])
    k.dma("sp", sv[:, 1:2], I["gk"][:, :], [], [bsv])
    k.dma("sp", sv[:, 2:18], I["bgate"][:, :], [], [bsv])
    k.ts("dve", sv[:, 0:1], sv[:, 0:1], 128.0 ** -0.5, None, ALU.mult, None, [bsv], [bsv])
    self.load_weight_cols(ph, W, bW, I["win"], cols, gT, bg, chunked=True)
    wch = list(self._wch)

    def wb(col):
        for a_, b_, buf_ in wch:
            if a_ <= col < b_:
                return buf_
        raise ValueError(col)
    hT = ph.enter_context(self.sbt("hT", [128, 8, T], F32))
    un = ph.enter_context(self.sbt("un", [128, 8, T], BF16))
    sq = ph.enter_context(self.sbt("sq", [128, 8, T], BF16))
    rstd = ph.enter_context(self.sbt("rstd", [128, T], F32))
    bhT, bun, bsq, brstd = (k.buf() for _ in range(4))
    tabs = ph.enter_context(self.sbt("tabs", [128, 2, T], F32))
    btab = k.buf()
    tmpq = [(ph.enter_context(self.sbt(f"sqh{i}", [128, T], BF16)),
             ph.enter_context(self.sbt(f"rs{i}", [128, T], F32)),
             ph.enter_context(self.sbt(f"xnq{i}", [128, T], F32)),
             ph.enter_context(self.sbt(f"t1q{i}", [128, T], F32))) for i in range(2)]
    btmpq = [k.buf(), k.buf()]
    qo = [ph.enter_context(self.sbt(f"qo{i}", [128, T], BF16)) for i in range(2)]
    bqo = [k.buf(), k.buf()]
    if part == "a":
        xst = [ph.enter_context(self.sbt(f"xst{i}", [128, 4, T], F32)) for i in range(2)]
        bxst = [k.buf(), k.buf()]
        dst_ = ph.enter_context(self.sbt("dtst", [128, 4, 64], F32))
        vst = ph.enter_context(self.sbt("vst", [128, 4, 256], BF16))
        bdst, bvst = k.buf(), k.buf()
    else:
        zst = [ph.enter_context(self.sbt(f"zst{i}", [128, 2048], F32)) for i in range(2)]
        bzst = [k.buf(), k.buf()]
        gst = [ph.enter_context(self.sbt(f"gst{i}", [128, 4, T], F32)) for i in range(2)]
        bgst = [k.buf(), k.buf()]
    nq = 0
    for t0 in range(0, ntok, T):
        g0 = off + t0
        jb, tl = self.jobof(g0) if self.cc else (None, t0)
        xpre = self.S_xpre_l[jb] if self.cc else self.S_xpre
        k.dma("sp", hT[:], self.S_h[:, g0:g0 + T].rearrange("(kt p) t -> p kt t", p=128),
              [self.bS(self.S_h, g0)], [bhT])
        k.dma("sp", tabs[:, 0, :], I["cosT"][:, g0:g0 + T], [], [btab])
        k.dma("sp", tabs[:, 1, :], I["sinT"][:, g0:g0 + T], [], [btab])
        self.norm_tile(hT[:], bhT, un, bun, sq[:], bsq, rstd[:], brstd)
        fillers = []

        def fill(n=2):
            for _ in range(n):
                if fillers:
                    fillers.pop(0)()

        if part == "a":
            def xbc_group(ct):
                def f():
                    c4, j = ct // 4, ct % 4
                    st, bst = xst[c4 % 2], bxst[c4 % 2]
                    ps, bps = self.psum()
                    for kt in range(8):
                        k.mm(ps[:], W[:, kt, ct * 128:(ct + 1) * 128], un[:, kt, :], kt == 0, kt == 7,
                             [wb(ct * 128), bun], [bps])
                    k.copy("act" if j % 2 == 0 else "dve", st[:, j, :], ps[:], [bps], [bst])
                    if j == 3:
                        k.dma("pool", xpre[c4 * 512:(c4 + 1) * 512, tl:tl + T].rearrange("(c p) t -> p c t", p=128),
                              st[:], [bst], [self.bS(xpre, tl)])
                return f

            def dtv_group(s_):
                def f():
                    ps, bps = self.psum()
                    for kt in range(8):
                        k.mm(ps[:, 0:64], un[:, kt, s_ * 128:(s_ + 1) * 128], W[:, kt, 3072:3136], kt == 0, kt == 7,
                             [wb(3072), bun], [bps])
                    k.copy("act", dst_[:, s_, :], ps[:, 0:64], [bps], [bdst])
                    ps, bps = self.psum()
                    for kt in range(8):
                        k.mm(ps[:, 0:256], un[:, kt, s_ * 128:(s_ + 1) * 128], W[:, kt, 3392:3648], kt == 0, kt == 7,
                             [wb(3392), bun], [bps])
                    k.copy("dve", vst[:, s_, :], ps[:, 0:256], [bps], [bvst])
                    if s_ == 3:
                        k.dma("pool", self.S_dt[g0:g0 + T, :].rearrange("(s p) c -> p s c", p=128), dst_[:], [bdst],
                              [self.bS(self.S_dt, g0)])
                        if self.cc:
                            for kv in range(2):
                                vh = self.V_in[jb][kv]
                                k.dma("pool", vh.ap()[tl:tl + T, :].rearrange("(s p) c -> p s c", p=128),
                                      vst[:, :, kv * 128:(kv + 1) * 128], [bvst], [self.ccb(vh)])
                        else:
                            k.dma("pool", self.S_v[g0:g0 + T, :].rearrange("(s p) c -> p s c", p=128), vst[:], [bvst],
                                  [self.bS(self.S_v, g0)])
                return f

            fillers.extend(xbc_group(ct) for ct in range(24))
            fillers.extend(dtv_group(s_) for s_ in range(4))
            fill(4)
            for hd in range(2):
                ps, bps = self.psum()
                for kt in range(8):
                    k.mm(ps[:], W[:, kt, 3136 + hd * 128:3136 + (hd + 1) * 128], un[:, kt, :], kt == 0, kt == 7,
                         [wb(3136 + hd * 128), bun], [bps])
                self.qk_head(ps, bps, sv[:, 1:2], bsv, tabs[:, 0, :], tabs[:, 1, :], btab,
                             tmpq[nq % 2], btmpq[nq % 2], qo[nq % 2][:], bqo[nq % 2], fill=fill)
                if self.cc:
                    kh = self.K_in[jb][hd]
                    k.dma("pool", kh.ap()[:, tl:tl + T], qo[nq % 2][:], [bqo[nq % 2]], [self.ccb(kh)])
                else:
                    k.dma("pool", self.S_kT[hd * 128:(hd + 1) * 128, g0:g0 + T], qo[nq % 2][:], [bqo[nq % 2]],
                          [self.bS(self.S_kT, g0)])
                nq += 1
                fill(2)
            fill(len(fillers))
        else:
            o0 = ooff + t0

            def z_group(s_, c):
                def f():
                    zs, bzs = zst[s_ % 2], bzst[s_ % 2]
                    ps, bps = self.psum()
                    for kt in range(8):
                        k.mm(ps[:], un[:, kt, s_ * 128:(s_ + 1) * 128], W[:, kt, c * 512:(c + 1) * 512], kt == 0,
                             kt == 7, [wb(c * 512), bun], [bps])
                    k.act(zs[:, c * 512:(c + 1) * 512], ps[:], AF.Silu, [bps], [bzs])
                    if c == 3:
                        k.dma("pool", self.S_sz[o0 + s_ * 128:o0 + (s_ + 1) * 128, :], zs[:], [bzs],
                              [self.bS(self.S_sz, o0)])
                return f

            def g_group(ct):
                def f():
                    c4, j = ct // 4, ct % 4
                    st, bst = gst[c4 % 2], bgst[c4 % 2]
                    ps, bps = self.psum()
                    for kt in range(8):
                        k.mm(ps[:], W[:, kt, 3072 + ct * 128:3072 + (ct + 1) * 128], un[:, kt, :], kt == 0,
                             kt == 7, [wb(3072 + ct * 128), bun], [bps])
                    k.act(st[:, j, :], ps[:], AF.Sigmoid, [bps, bsv], [bst], bias=sv[:, 2 + ct:3 + ct])
                    if j == 3:
                        k.dma("pool", self.S_gT[c4 * 512:(c4 + 1) * 512, o0:o0 + T].rearrange("(c p) t -> p c t", p=128),
                              st[:], [bst], [self.bS(self.S_gT, o0)])
                return f

            fillers.extend(z_group(s_, c) for s_ in range(4) for c in range(4))
            fillers.extend(g_group(ct) for ct in range(16))
            for hd in range(8):
                ps, bps = self.psum()
                for kt in range(8):
                    k.mm(ps[:], W[:, kt, 2048 + hd * 128:2048 + (hd + 1) * 128], un[:, kt, :], kt == 0, kt == 7,
                         [wb(2048 + hd * 128), bun], [bps])
                self.qk_head(ps, bps, sv[:, 0:1], bsv, tabs[:, 0, :], tabs[:, 1, :], btab,
                             tmpq[nq % 2], btmpq[nq % 2], qo[nq % 2][:], bqo[nq % 2], fill=lambda: fill(2))
                k.dma("pool", self.S_qT[hd * 128:(hd + 1) * 128, o0:o0 + T], qo[nq % 2][:], [bqo[nq % 2]],
                      [self.bS(self.S_qT, o0)])
                nq += 1
            fill(len(fillers))


@_add_methods(Prog)
def conv_phase(self, ph, L, LO, off, blk0):
    nc, k, I = self.nc, self.k, self.I
    TB = self.TB
    NS = TB // 128
    cw = ph.enter_context(self.sbt("convw", [128, 24, 5], F32))
    cb = ph.enter_context(self.sbt("convb", [128, 24], F32))
    cm = ph.enter_context(self.sbt("cmask", [128, 2 * self.nblk_total], F32))
    bcw = k.buf()
    k.dma("sp", cw[:], I["convw"][:, :, :], [], [bcw])
    k.dma("sp", cb[:], I["convb"][:, :], [], [bcw])
    if self.cc:
        nj = len(self.jobs)
        Eg = ph.enter_context(self.sbt("Eg", [128, 4, 24 * nj * 4], F32))
        c2 = ph.enter_context(self.sbt("ccm", [128, 64], F32))
        HLR = ph.enter_context(self.sbt("HLR", [128, 2, 24, 2], F32))
        bEg, bH = k.buf(), k.buf()
        k.dma("sp", Eg[:], self.E_out.ap().rearrange("(r p) f -> p r f", p=128), [self.ccb(self.E_out)], [bEg])
        k.dma("sp", c2[:], I["ccmask"][:, :], [], [bEg])
        for side in range(2):
            for r in range(4):
                ev = Eg[:, r, :].rearrange("p (c j e) -> p c j e", c=24, j=nj)
                src = ev[:, :, self.job, 2:4] if side == 0 else ev[:, :, self.job, 0:2]
                sc = c2[:, side * 4 + r:side * 4 + r + 1]
                if r == 0:
                    k.ts("dve", HLR[:, side, :, :], src, sc, None, ALU.mult, None, [bEg], [bH])
                else:
                    k.stt(HLR[:, side, :, :], src, sc, HLR[:, side, :, :], ALU.mult, ALU.add, [bEg, bH], [bH])
    else:
        k.dma("sp", cm[:], I["cmask"][:, :], [], [bcw])
    XB = [ph.enter_context(self.sbt(f"XB{i}", [128, 12, TB + 4], F32)) for i in range(2)]
    bXB = [k.buf(), k.buf()]
    acc = [ph.enter_context(self.sbt(f"cacc{i}", [128, TB], F32)) for i in range(3)]
    bacc = [k.buf() for _ in range(3)]
    xc = [ph.enter_context(self.sbt(f"xc{i}", [128, TB], BF16)) for i in range(3)]
    bxc = [k.buf() for _ in range(3)]
    xtok = ph.enter_context(self.sbt("xtok", [128, NS, 2048], BF16))
    btok = ph.enter_context(self.sbt("btok", [128, NS, 512], BF16))
    bxtok, bbtok = k.buf(), k.buf()
    n = 0
    for bi, t0 in enumerate(range(0, L, TB)):
        g0 = off + t0
        gl = off + (t0 - 2) % L
        gr = off + (t0 + TB) % L
        bidx = blk0 + bi
        for half in range(2):
            xb, bxb = XB[half], bXB[half]
            rows = self.S_xpre[half * 1536:(half + 1) * 1536, :].rearrange("(c p) t -> p c t", p=128)
            ll, lr = (t0 - 2) % L, (t0 + TB) % L
            deps = [self.bS(self.S_xpre, g) for g in {t0 + j for j in range(0, TB, 512)} | {ll, lr} | {max(t0 - 2, 0), min(t0 + TB, L - 1)}]
            lq = "sp" if half == 0 else "pool"
            k.dma(lq, xb[:, :, 2:TB + 2], rows[:, :, t0:t0 + TB], deps, [bxb])
            if self.cc:
                if t0 == 0:
                    k.copy("pool", xb[:, :, 0:2], HLR[:, 0, half * 12:(half + 1) * 12, :], [bH], [bxb])
                else:
                    k.dma("sp", xb[:, :, 0:2], rows[:, :, t0 - 2:t0], deps, [bxb])
                if t0 + TB == L:
                    k.copy("pool", xb[:, :, TB + 2:TB + 4], HLR[:, 1, half * 12:(half + 1) * 12, :], [bH], [bxb])
                else:
                    k.dma("sp", xb[:, :, TB + 2:TB + 4], rows[:, :, t0 + TB:t0 + TB + 2], deps, [bxb])
            else:
                k.dma("sp", xb[:, :, 0:2], rows[:, :, ll:ll + 2], deps, [bxb])
                k.dma("sp", xb[:, :, TB + 2:TB + 4], rows[:, :, lr:lr + 2], deps, [bxb])
                k.ts("pool", xb[:, :, 0:2], xb[:, :, 0:2], cm[:, 2 * bidx:2 * bidx + 1], None, ALU.mult, None,
                     [bxb, bcw], [bxb])
                k.ts("pool", xb[:, :, TB + 2:TB + 4], xb[:, :, TB + 2:TB + 4], cm[:, 2 * bidx + 1:2 * bidx + 2], None,
                     ALU.mult, None, [bxb, bcw], [bxb])
            def taps(c, n_):
                ct = half * 12 + c
                a, ba = acc[n_ % 3], bacc[n_ % 3]
                k.act(a[:], xb[:, c, 0:TB], AF.Identity, [bxb, bcw], [ba], bias=cb[:, ct:ct + 1], scale=cw[:, ct, 0:1])
                for kk in range(1, 5):
                    k.stt(a[:], xb[:, c, kk:kk + TB], cw[:, ct, kk:kk + 1], a[:], ALU.mult, ALU.add,
                          [bxb, bcw, ba], [ba])

            def finish(c, n_):
                ct = half * 12 + c
                a, ba = acc[n_ % 3], bacc[n_ % 3]
                o, bo = xc[n_ % 3], bxc[n_ % 3]
                k.act(o[:], a[:], AF.Silu, [ba], [bo])
                if ct >= 16:
                    r0 = (ct - 16) * 128
                    k.dma("act", self.S_bcT[r0:r0 + 128, g0:g0 + TB], o[:], [bo], [self.bS(self.S_bcT, g0)])
                if ct < 20:
                    ps, bps = self.psum()
                    psb = ps[:].bitcast(BF16)
                    for s_ in range(NS):
                        k.tr(psb[:, s_ * 128:(s_ + 1) * 128], o[:, s_ * 128:(s_ + 1) * 128], self.identB,
                             [bo, self.cB], [bps])
                    src = psb[:, 0:NS * 128].rearrange("p (s c) -> p s c", s=NS)
                    if ct < 16:
                        k.copy("act", xtok[:, :, ct * 128:(ct + 1) * 128], src, [bps], [bxtok])
                    else:
                        k.copy("act", btok[:, :, (ct - 16) * 128:(ct - 15) * 128], src, [bps], [bbtok])

            taps(0, n)
            for c in range(12):
                if c + 1 < 12:
                    taps(c + 1, n + c + 1)
                finish(c, n + c)
            n += 12
        k.dma("act", self.S_xtok[g0:g0 + TB, :].rearrange("(s p) c -> p s c", p=128), xtok[:], [bxtok],
              [self.bS(self.S_xtok, g0)])
        k.dma("act", self.S_Btok[g0:g0 + TB, :].rearrange("(s p) c -> p s c", p=128), btok[:], [bbtok],
              [self.bS(self.S_Btok, g0)])


def bc(ap, shape):
    return ap.to_broadcast(list(shape))


@_add_methods(Prog)
def ssd_phase(self, ph, L, LO, off, ooff, job):
    nc, k, I = self.nc, self.k, self.I
    NC, NO = L // 128, LO // 128
    SEG = NO
    cs = ph.enter_context(self.sbt("ssdc", [128, 176], F32))
    normrep = ph.enter_context(self.sbt("normrep", [128, 2048], F32))
    tri = ph.enter_context(self.sbt("tri", [128, 2, 128], F32))
    negf = ph.enter_context(self.sbt("negf", [128, 2, 512], F32))
    negm = ph.enter_context(self.sbt("negm", [128, 2, 512], BF16))
    bcs = k.buf()
    k.dma("sp", cs[:, 0:64], I["dtbias"][:, :], [], [bcs])
    k.dma("sp", cs[:, 64:128], I["alog"][:, :], [], [bcs])
    k.dma("sp", cs[:, 128:160], I["dskip"][:, :], [], [bcs])
    k.dma("sp", cs[:, 160:168], I["smask"][:, job * 8:job * 8 + 8], [], [bcs])
    k.dma("sp", normrep[:], I["ssmnorm"][:, :], [], [bcs])
    k.dma("sp", tri[:], I["tri"].rearrange("p (a l) -> p a l", a=2), [], [bcs])
    k.dma("sp", negf[:], I["negm"].rearrange("p (a l) -> p a l", a=2), [], [bcs])
    k.copy("dve", negm[:], negf[:], [bcs], [bcs])
    k.act(cs[:, 64:128], cs[:, 64:128], AF.Exp, [bcs], [bcs])
    k.ts("dve", cs[:, 64:128], cs[:, 64:128], -1.0, None, ALU.mult, None, [bcs], [bcs])
    dtb, Aneg, Drep, sm = cs[:, 0:64], cs[:, 64:128], cs[:, 128:160], cs[:, 160:168]
    hst = ph.enter_context(self.sbt("hstate", [128, 2048], F32))
    hpb = ph.enter_context(self.sbt("hprevb", [128, 2048], BF16))
    bhst, bhpb = k.buf(), k.buf()
    xts = [ph.enter_context(self.sbt(f"xt{i}", [128, 32, 64], BF16)) for i in range(2)]
    Bts = [ph.enter_context(self.sbt(f"Bt{i}", [128, 512], BF16)) for i in range(2)]
    dtrs = [ph.enter_context(self.sbt(f"dtr{i}", [128, 32], F32)) for i in range(2)]
    bcTs = [ph.enter_context(self.sbt(f"bcT{i}", [128, 8, 128], BF16)) for i in range(2)]
    smt = [ph.enter_context(self.sbt(f"smt{i}", [128, 8, 32], F32)) for i in range(2)]
    bin_ = [k.buf(), k.buf()]
    bsm = [k.buf(), k.buf()]
    xdt = ph.enter_context(self.sbt("xdt", [128, 32, 64], BF16))
    xdtd = [ph.enter_context(self.sbt(f"xdtd{i}", [128, 32, 64], BF16)) for i in range(2)]
    bxdt = k.buf()
    bxdtd = [k.buf(), k.buf()]
    Rt = ph.enter_context(self.sbt("Rt", [128, 32, 128], F32))
    Lt = ph.enter_context(self.sbt("Lt", [128, 32, 128], F32))
    MT = ph.enter_context(self.sbt("MT", [128, 32, 128], BF16))
    bR, bLt, bMT = k.buf(), k.buf(), k.buf()
    yacc = [ph.enter_context(self.sbt(f"yacc{i}", [128, 2048], F32)) for i in range(2)]
    byacc = [k.buf(), k.buf()]
    aux = [ph.enter_context(self.sbt(f"yaux{i}", [128, 2048], F32)) for i in range(2)]
    baux = [k.buf(), k.buf()]
    tmp2 = [ph.enter_context(self.sbt(f"ytmp{i}", [128, 512], F32)) for i in range(2)]
    btmp2 = [k.buf(), k.buf()]
    ssq = ph.enter_context(self.sbt("ssq", [128, 8], F32))
    bssq = k.buf()
    sbf = ph.enter_context(self.sbt("sbf", [128, 2048], BF16))
    sTs = [ph.enter_context(self.sbt(f"sTs{i}", [128, 16, 128], BF16)) for i in range(2)]
    bsbf = k.buf()
    bsTs = [k.buf(), k.buf()]
    st = {"n": 0, "nf": 0}

    def step(c, d, full):
        i = st["n"] % 2
        st["n"] += 1
        g = off + c * 128
        xt, Bt, dtr, bcT, s8 = xts[i], Bts[i], dtrs[i], bcTs[i], smt[i]
        bi, bs = bin_[i], bsm[i]
        deps = [self.bS(self.S_xtok, g), self.bS(self.S_Btok, g), self.bS(self.S_dt, g), self.bS(self.S_bcT, g)]
        k.dma("sp", xt[:].rearrange("p h q -> p (h q)"), self.S_xtok[g:g + 128, :], deps, [bi])
        k.dma("sp", Bt[:], self.S_Btok[g:g + 128, :], deps, [bi])
        k.dma("sp", dtr[:], self.S_dt[g:g + 128, d * 32:(d + 1) * 32], deps, [bi])
        if full:
            k.dma("sp", bcT[:], self.S_bcT[:, g:g + 128].rearrange("(r p) t -> p r t", p=128), deps, [bi])
        dt_, a_, ac_, nac_, eA_, wd_, cd_, tm_ = (s8[:, j, :] for j in range(8))
        k.tt("dve", tm_, dtr[:], dtb[:, d * 32:(d + 1) * 32], ALU.add, [bi, bcs], [bs])
        k.act(tm_, tm_, AF.Exp, [bs], [bs])
        k.act(dt_, tm_, AF.Ln, [bs], [bs], bias=self.oneT)
        k.tt("dve", a_, dt_, Aneg[:, d * 32:(d + 1) * 32], ALU.mult, [bs, bcs], [bs])
        ps1, bps1 = self.psum()
        k.mm(ps1[:, 0:32], tri[:, d, :], a_, True, True, [bcs, bs], [bps1])
        k.mm(ps1[:, 32:64], self.onesF, a_, True, True, [self.cB, bs], [bps1])
        k.copy("act", ac_, ps1[:, 0:32], [bps1], [bs])
        k.tt("dve", tm_, ps1[:, 32:64], ac_, ALU.subtract, [bps1, bs], [bs])
        k.act(wd_, tm_, AF.Exp, [bs], [bs])
        k.act(cd_, ps1[:, 32:64], AF.Exp, [bps1], [bs])
        k.tt("dve", wd_, wd_, dt_, ALU.mult, [bs], [bs])
        xd, bxd = xdtd[i], bxdtd[i]
        k.tt("pool", xd[:], xt[:], bc(wd_.unsqueeze(2), [128, 32, 64]), ALU.mult, [bi, bs], [bxd])
        if full:
            j = st["nf"] % 2
            st["nf"] += 1
            ya, bya = yacc[j], byacc[j]
            ax, bax = aux[j], baux[j]
            oo = ooff + c * 128
            k.act(eA_, ac_, AF.Exp, [bs], [bs])
            k.ts("dve", nac_, ac_, -1.0, None, ALU.mult, None, [bs], [bs])
            k.tt("pool", Rt[:], bc(a_.unsqueeze(2), [128, 32, 128]), bc(tri[:, d, :].unsqueeze(1), [128, 32, 128]),
                 ALU.mult, [bs, bcs], [bR])
            k.tt("dve", xdt[:], xt[:], bc(dt_.unsqueeze(2), [128, 32, 64]), ALU.mult, [bi, bs], [bxdt])
            k.copy("act", hpb[:], hst[:], [bhst], [bhpb])
            if d == 0:
                k.tt("pool", ax[:].rearrange("p (h q) -> p h q", q=64), xt[:], bc(Drep.unsqueeze(2), [128, 32, 64]),
                     ALU.mult, [bi, bcs], [bax])
            else:
                k.dma("sp", ax[:], self.S_y[oo:oo + 128, :], [self.bS(self.S_y, oo)], [bax])
            for hg in range(8):
                ps, bps = self.psum()
                k.mm(ps[:], self.identB, negm[:, d, :], True, False, [self.cB, bcs], [bps])
                k.mm(ps[:], self.onesF, Rt[:, hg * 4:(hg + 1) * 4, :].rearrange("p h l -> p (h l)"), False, True,
                     [self.cB, bR], [bps])
                for hh in range(4):
                    h = hg * 4 + hh
                    k.act(Lt[:, h, :], ps[:, hh * 128:(hh + 1) * 128], AF.Exp, [bps, bs], [bLt],
                          bias=nac_[:, h:h + 1])
            pcb, bpcb = self.psum()
            for g4 in range(4):
                k.mm(pcb[:, g4 * 128:(g4 + 1) * 128], bcT[:, g4, :], bcT[:, 4 + g4, :], True, True, [bi], [bpcb])
            for g4 in range(4):
                k.tt("dve", MT[:, g4 * 8:(g4 + 1) * 8, :], Lt[:, g4 * 8:(g4 + 1) * 8, :],
                     bc(pcb[:, g4 * 128:(g4 + 1) * 128].unsqueeze(1), [128, 8, 128]), ALU.mult, [bLt, bpcb], [bMT])
            for g4 in range(4):
                psy, bpsy = self.psum()
                for e in range(8):
                    h = g4 * 8 + e
                    k.mm(psy[:, e * 64:(e + 1) * 64], MT[:, h, :], xdt[:, h, :], True, True, [bMT, bxdt], [bpsy])
                pso, bpso = self.psum()
                k.mm(pso[:], bcT[:, 4 + g4, :], hpb[:, g4 * 512:(g4 + 1) * 512], True, True, [bi, bhpb], [bpso])
                t2, bt2 = tmp2[g4 % 2], btmp2[g4 % 2]
                k.tt("dve", t2[:].rearrange("p (e q) -> p e q", q=64), pso[:].rearrange("p (e q) -> p e q", q=64),
                     bc(eA_[:, g4 * 8:(g4 + 1) * 8].unsqueeze(2), [128, 8, 64]), ALU.mult, [bpso, bs], [bt2])
                k.tt("dve", ya[:, g4 * 512:(g4 + 1) * 512], psy[:], t2[:], ALU.add, [bpsy, bt2], [bya])
            if "D_s8" in self.debug and c == 0 and d == 0:
                k.dma("pool", self.dram("D_s8", [128, 256], F32), s8[:].rearrange("p a b -> p (a b)"), [bs], [])
                k.dma("pool", self.dram("D_Lt", [128, 4096], F32), Lt[:].rearrange("p a b -> p (a b)"), [bLt], [])
                k.dma("pool", self.dram("D_MT", [128, 4096], BF16), MT[:].rearrange("p a b -> p (a b)"), [bMT], [])
                k.dma("pool", self.dram("D_R", [128, 4096], F32), Rt[:].rearrange("p a b -> p (a b)"), [bR], [])
                k.dma("pool", self.dram("D_ya", [128, 2048], F32), ya[:], [bya], [])
                k.dma("pool", self.dram("D_xdt", [128, 2048], BF16), xdt[:].rearrange("p a b -> p (a b)"), [bxdt], [])
            k.tt("pool", ya[:], ya[:], ax[:], ALU.add, [bya, bax], [bya])
            if d == 0:
                k.dma("pool", self.S_y[oo:oo + 128, :], ya[:], [bya], [self.bS(self.S_y, oo)])
            else:
                k.dma("sp", ax[:], self.S_sz[oo:oo + 128, :], [self.bS(self.S_sz, oo)], [bax])
                k.tt("pool", ya[:], ya[:], ax[:], ALU.mult, [bya, bax], [bya])
                k.act(ax[:], ya[:], AF.Square, [bya], [bax])
                k.op("dve", lambda e, o_=ssq[:, 0:4], i_=ax[:].rearrange("p (g c) -> p g c", g=4):
                     e.tensor_reduce(out=o_, in_=i_, axis=AX.X, op=ALU.add), [bax], [bssq])
                k.act(ssq[:, 4:8], ssq[:, 0:4], AF.Sqrt, [bssq, self.cB], [bssq], bias=self.epsT, scale=1.0 / 512)
                k.recip(ssq[:, 4:8], ssq[:, 4:8], [bssq], [bssq])
                for g4 in range(4):
                    k.stt(sbf[:, g4 * 512:(g4 + 1) * 512], ya[:, g4 * 512:(g4 + 1) * 512], ssq[:, 4 + g4:5 + g4],
                          normrep[:, g4 * 512:(g4 + 1) * 512], ALU.mult, ALU.mult, [bya, bssq, bcs], [bsbf])
                sT, bsT = sTs[j], bsTs[j]
                for hb in range(2):
                    ps, bps = self.psum()
                    psb = ps[:].bitcast(BF16)
                    for q in range(8):
                        ct = hb * 8 + q
                        k.tr(psb[:, q * 128:(q + 1) * 128], sbf[:, ct * 128:(ct + 1) * 128], self.identB,
                             [bsbf, self.cB], [bps])
                    k.copy("act", sT[:, hb * 8:(hb + 1) * 8, :], psb.rearrange("p (c t) -> p c t", c=8), [bps], [bsT])
                k.dma("pool", self.S_sT[:, oo:oo + 128].rearrange("(c p) t -> p c t", p=128), sT[:], [bsT],
                      [self.bS(self.S_sT, oo)])
        for g4 in range(4):
            psS, bpsS = self.psum()
            k.mm(psS[:], Bt[:, g4 * 128:(g4 + 1) * 128], xd[:, g4 * 8:(g4 + 1) * 8, :].rearrange("p e q -> p (e q)"),
                 True, True, [bi, bxd], [bpsS])
            hv = hst[:, g4 * 512:(g4 + 1) * 512].rearrange("p (e q) -> p e q", q=64)
            k.tt("dve", hv, hv, bc(cd_[:, g4 * 8:(g4 + 1) * 8].unsqueeze(2), [128, 8, 64]), ALU.mult, [bhst, bs], [bhst])
            k.tt("dve", hst[:, g4 * 512:(g4 + 1) * 512], hst[:, g4 * 512:(g4 + 1) * 512], psS[:], ALU.add,
                 [bhst, bpsS], [bhst])

    def scale_state(col):
        k.ts("dve", hst[:], hst[:], sm[:, col:col + 1], None, ALU.mult, None, [bhst, bcs], [bhst])

    k.memset("dve", hst[:], 0.0, [bhst])
    for j in (1, 2, 3):
        scale_state(j - 1)
        for c in range(j * SEG, (j + 1) * SEG):
            step(c, 0, False)
    scale_state(3)
    for c in range(NO):
        step(c, 0, True)
    k.memset("dve", hst[:], 0.0, [bhst])
    for j in (3, 2, 1):
        scale_state(4 + (3 - j))
        for c in range((j + 1) * SEG - 1, j * SEG - 1, -1):
            step(c, 1, False)
    scale_state(7)
    for c in range(NO - 1, -1, -1):
        step(c, 1, True)


def _lay(v):
    return np.ascontiguousarray(np.asarray(v, np.float32).reshape(-1, 128).T)


def _rep(v):
    return np.ascontiguousarray(np.tile(np.asarray(v, np.float32).reshape(1, -1), (128, 1)))


def _rope_tables(L, shift):
    pos = (np.arange(L) + shift) % L
    row = (pos // GRID_W).astype(np.float32)
    col = (pos % GRID_W).astype(np.float32)
    inv = (np.float32(10000.0) ** (-np.arange(0, 64, 2, dtype=np.float32) / np.float32(64))).astype(np.float32)
    ang = np.concatenate([row[:, None] * inv, col[:, None] * inv], -1).astype(np.float32)
    c = np.repeat(np.cos(ang), 2, axis=1).T
    s = np.repeat(np.sin(ang), 2, axis=1).T
    return np.ascontiguousarray(c.astype(np.float32)), np.ascontiguousarray(s.astype(np.float32))


def const_inputs(Wt, jobs, TB):
    kk = np.arange(128)
    triF = (kk[:, None] <= kk[None, :]).astype(np.float32)
    triB = (kk[:, None] >= kk[None, :]).astype(np.float32)
    negF = np.where(kk[None, :] < kk[:, None], NEG, 0.0).astype(np.float32)
    negB = np.where(kk[None, :] > kk[:, None], NEG, 0.0).astype(np.float32)
    pm = np.zeros((128, 128), np.float32)
    for i in range(64):
        pm[2 * i + 1, 2 * i] = -1.0
        pm[2 * i, 2 * i + 1] = 1.0
    f = lambda a: np.ascontiguousarray(np.asarray(a, np.float32))
    cw = f(Wt["conv_w"])[0]
    return dict(
        w1gu=f(Wt["w_ffn1_gu"])[0], w1d=f(Wt["w_ffn1_down"])[0], g1=_lay(Wt["norm_ffn1"]),
        w2gu=f(Wt["w_ffn2_gu"])[0], w2d=f(Wt["w_ffn2_down"])[0], g2=_lay(Wt["norm_ffn2"]),
        gfin=_lay(Wt["norm_final"]), win=f(Wt["w_in"])[0], gmix=_lay(Wt["norm_mix"]),
        gq=f(Wt["q_norm"]).reshape(128, 1), gk=f(Wt["k_norm"]).reshape(128, 1), bgate=_lay(Wt["b_gate"]),
        convw=np.ascontiguousarray(cw.T.reshape(24, 128, 5).transpose(1, 0, 2)), convb=_lay(Wt["conv_b"]),
        dtbias=_rep(np.concatenate([f(Wt["dt_bias_f"]).ravel(), f(Wt["dt_bias_b"]).ravel()])),
        alog=_rep(np.concatenate([f(Wt["A_log_f"]).ravel(), f(Wt["A_log_b"]).ravel()])),
        dskip=_rep(Wt["D_skip"]), ssmnorm=_rep(Wt["ssm_norm"]),
        tri=np.concatenate([triF, triB], 1),
        negm=np.concatenate([np.tile(negF, (1, 4)), np.tile(negB, (1, 4))], 1), pm=pm,
        wsb=f(Wt["w_ssm_branch"])[0], wab=f(Wt["w_attn_branch"])[0], wout=f(Wt["w_out"])[0],
    )


def core_inputs(xs, qi, jobs, TB):
    xr, cos, sin, cms, sms = [], [], [], [], []
    for x, (L, LO) in zip(xs, jobs):
        xr.append(np.roll(x, -qi * LO, axis=0))
        c, s = _rope_tables(L, qi * LO)
        cos.append(c)
        sin.append(s)
        nb = L // TB
        cm = np.ones(2 * nb, np.float32)
        bpos = ((4 - qi) % 4) * LO
        for b in range(nb):
            if b * TB == bpos:
                cm[2 * b] = 0
            if ((b + 1) * TB) % L == bpos:
                cm[2 * b + 1] = 0
        cms.append(cm)
        m = np.ones(8, np.float32)
        for j in (1, 2, 3):
            if (qi + j) % 4 == 0:
                m[j - 1] = 0
            if (qi + j) % 4 == 3:
                m[4 + (3 - j)] = 0
        if qi == 0:
            m[3] = 0
        if qi == 3:
            m[7] = 0
        sms.append(m)
    return dict(x=np.ascontiguousarray(np.concatenate(xr, 0)), cosT=np.concatenate(cos, 1), sinT=np.concatenate(sin, 1),
                cmask=_rep(np.concatenate(cms)), smask=_rep(np.concatenate(sms)))


@_add_methods(Prog)
def attn_phase(self, ph, L, LO, off, ooff):
    nc, k, I = self.nc, self.k, self.I
    NK = L // 128
    T = 512
    KT = ph.enter_context(self.sbt("KT", [128, 2, L], BF16))
    V = ph.enter_context(self.sbt("Vsb", [128, NK, 256], BF16))
    bKT, bV = k.buf(), k.buf()
    if self.cc:
        for kv in range(2):
            ko, vo = self.K_out[self.job][kv], self.V_out[self.job][kv]
            k.dma("sp", KT[:, kv, :].rearrange("p (r t) -> p r t", r=4), ko.ap().rearrange("(r p) t -> p r t", p=128),
                  [self.ccb(ko)], [bKT])
            VS = min(32, NK)
            for j0 in range(0, NK, VS):
                k.dma("sp", V[:, j0:j0 + VS, kv * 128:(kv + 1) * 128],
                      vo.ap()[j0 * 128:(j0 + VS) * 128, :].rearrange("(j p) c -> p j c", p=128), [self.ccb(vo)], [bV])
    else:
        alldeps = [self.bS(self.S_kT, off + t) for t in range(0, L, 512)]
        for kv in range(2):
            k.dma("sp", KT[:, kv, :], self.S_kT[kv * 128:(kv + 1) * 128, off:off + L], alldeps, [bKT])
        vdeps = [self.bS(self.S_v, off + t) for t in range(0, L, 512)]
        for j0 in range(0, NK, 16):
            k.dma("sp", V[:, j0:j0 + 16, :], self.S_v[off + j0 * 128:off + (j0 + 16) * 128, :].rearrange("(j p) c -> p j c", p=128),
                  vdeps, [bV])
    QT = [ph.enter_context(self.sbt(f"QT{i}", [128, 8, T], BF16)) for i in range(2)]
    bQT = [k.buf(), k.buf()]
    NP = NK // 2
    NPT = 6
    pt = [ph.enter_context(self.sbt(f"pt{i}", [128, 2, T], BF16)) for i in range(NPT)]
    bpt = [k.buf() for _ in range(NPT)]
    rc = [ph.enter_context(self.sbt(f"rc{i}", [128, T], F32)) for i in range(2)]
    ao = [ph.enter_context(self.sbt(f"ao{i}", [128, T], BF16)) for i in range(2)]
    brc = [k.buf(), k.buf()]
    bao = [k.buf(), k.buf()]
    NA = 6
    NCY = 7
    accs = [[ph.enter_context(self.sbt(f"lacc{i}_{e}", [128, 2, T], F32)) for e in range(NA)] for i in range(1)]
    baccs = [[k.buf() for _ in range(NA)] for _ in range(1)]
    accb = [[ph.enter_context(self.sbt(f"laccb{i}_{e}", [128, 2, T], BF16)) for e in range(NA)] for i in range(1)]
    baccb = [[k.buf() for _ in range(NA)] for _ in range(1)]
    bS2 = [k.buf(), k.buf()]
    nh = 0
    for qb in range(LO // T):
        o0 = ooff + qb * T
        qt, bqt = QT[qb % 2], bQT[qb % 2]
        k.dma("sp", qt[:], self.S_qT[:, o0:o0 + T].rearrange("(h p) t -> p h t", p=128), [self.bS(self.S_qT, o0)], [bqt])
        for h in range(8):
            kv = h // 4
            pso, bpso = self.ps[4 + 2 * (nh % 2)], self.psb[4 + 2 * (nh % 2)]
            psl, bpsl = self.ps[5 + 2 * (nh % 2)], self.psb[5 + 2 * (nh % 2)]
            acc, bacc = accs[0], baccs[0]
            acb, bacb = accb[0], baccb[0]
            seen = [False] * NA
            lastjj = [max([jj for jj in range(NP) if jj % NCY == e], default=-1) for e in range(NA)]
            pe_first = [True]

            def mm_s(jj):
                p = jj % 2
                for u in range(2):
                    j = 2 * jj + u
                    k.mm(self.ps[2 * p + u][:], KT[:, kv, j * 128:(j + 1) * 128], qt[:, h, :], True, True,
                         [bKT, bqt], [bS2[p]])
            mm_s(0)
            if NP > 1:
                mm_s(1)
            for jj in range(NP):
                p = jj % 2
                p_, bp_ = pt[jj % NPT], bpt[jj % NPT]
                k.act(p_[:], self.psall[:, 2 * p:2 * p + 2, :], AF.Exp, [bS2[p]], [bp_])
                for u in range(2):
                    j = 2 * jj + u
                    k.mm(pso[:], V[:, j, kv * 128:(kv + 1) * 128], p_[:, u, :], j == 0, j == NK - 1, [bV, bp_], [bpso])
                e = jj % NCY
                if e >= NA:
                    for u in range(2):
                        k.mm(psl[:], self.onesB, p_[:, u, :], pe_first[0], False, [self.cB, bp_], [bpsl])
                        pe_first[0] = False
                else:
                    eng = "pool" if e >= 4 else "dve"
                    fin = (jj == lastjj[e])
                    dst, bdst = (acb[e], bacb[e]) if fin else (acc[e], bacc[e])
                    if not seen[e]:
                        k.copy(eng, dst[:], p_[:], [bp_], [bdst])
                        seen[e] = True
                    else:
                        k.tt(eng, dst[:], acc[e][:], p_[:], ALU.add, [bacc[e], bp_], [bdst])
                if jj + 2 < NP:
                    mm_s(jj + 2)
            parts = [(e, u) for e in range(NA) if seen[e] for u in range(2)]
            for i_, (e, u) in enumerate(parts):
                k.mm(psl[:], self.onesB, acb[e][:, u, :], pe_first[0], i_ == len(parts) - 1, [self.cB, bacb[e]], [bpsl])
                pe_first[0] = False
            r, br = rc[nh % 2], brc[nh % 2]
            a_, ba_ = ao[nh % 2], bao[nh % 2]
            k.recip(r[:], psl[:], [bpsl], [br])
            k.tt("dve", a_[:], pso[:], r[:], ALU.mult, [bpso, br], [ba_])
            k.dma("pool", self.S_aT[h * 128:(h + 1) * 128, o0:o0 + T], a_[:], [ba_], [self.bS(self.S_aT, o0)])
            nh += 1


@_add_methods(Prog)
def merge_phase(self, ph, LO, off, ooff):
    nc, k, I = self.nc, self.k, self.I
    T = 512
    Ws = ph.enter_context(self.sbt("Ws", [128, 16, D], BF16))
    Wa = ph.enter_context(self.sbt("Wa", [128, 8, D], BF16))
    Wo = ph.enter_context(self.sbt("Wo", [128, 8, D], BF16))
    bWs, bWa, bWo = k.buf(), k.buf(), k.buf()
    self.load_weight_cols(ph, Ws, bWs, I["wsb"], [(0, D)], None, None)
    self.load_weight_cols(ph, Wa, bWa, I["wab"], [(0, D)], None, None)
    self.load_weight_cols(ph, Wo, bWo, I["wout"], [(0, D)], None, None)
    sT = ph.enter_context(self.sbt("sT", [128, 16, T], BF16))
    aT = ph.enter_context(self.sbt("aT", [128, 8, T], BF16))
    gT = ph.enter_context(self.sbt("gT", [128, 16, T], F32))
    hT = ph.enter_context(self.sbt("hT", [128, 8, T], F32))
    mT = ph.enter_context(self.sbt("mT", [128, 8, T], BF16))
    m1 = [ph.enter_context(self.sbt(f"m1{i}", [128, T], F32)) for i in range(2)]
    m2 = [ph.enter_context(self.sbt(f"m2{i}", [128, T], F32)) for i in range(2)]
    ho = [ph.enter_context(self.sbt(f"h2o{i}", [128, T], F32)) for i in range(2)]
    bsT, baT, bgT, bhT, bmT = (k.buf() for _ in range(5))
    bm1 = [k.buf(), k.buf()]
    bm2 = [k.buf(), k.buf()]
    bho = [k.buf(), k.buf()]
    for t0 in range(0, LO, T):
        o0 = ooff + t0
        g0 = off + t0
        k.dma("sp", sT[:], self.S_sT[:, o0:o0 + T].rearrange("(c p) t -> p c t", p=128), [self.bS(self.S_sT, o0)], [bsT])
        k.dma("sp", aT[:], self.S_aT[:, o0:o0 + T].rearrange("(c p) t -> p c t", p=128), [self.bS(self.S_aT, o0)], [baT])
        k.dma("sp", gT[:], self.S_gT[:, o0:o0 + T].rearrange("(c p) t -> p c t", p=128), [self.bS(self.S_gT, o0)], [bgT])
        k.dma("sp", hT[:], self.S_h[:, g0:g0 + T].rearrange("(c p) t -> p c t", p=128), [self.bS(self.S_h, g0)], [bhT])
        for m in range(8):
            p1, bp1 = self.psum()
            for kt in range(16):
                k.mm(p1[:], Ws[:, kt, m * 128:(m + 1) * 128], sT[:, kt, :], kt == 0, kt == 15, [bWs, bsT], [bp1])
            p2, bp2 = self.psum()
            for kt in range(8):
                k.mm(p2[:], Wa[:, kt, m * 128:(m + 1) * 128], aT[:, kt, :], kt == 0, kt == 7, [bWa, baT], [bp2])
            a1, ba1 = m1[m % 2], bm1[m % 2]
            a2, ba2 = m2[m % 2], bm2[m % 2]
            k.tt("dve", a1[:], p1[:], gT[:, m, :], ALU.mult, [bp1, bgT], [ba1])
            k.tt("dve", a2[:], p2[:], gT[:, 8 + m, :], ALU.mult, [bp2, bgT], [ba2])
            k.tt("pool", mT[:, m, :], a1[:], a2[:], ALU.add, [ba1, ba2], [bmT])
        for m in range(8):
            p1, bp1 = self.psum()
            for kt in range(8):
                k.mm(p1[:], Wo[:, kt, m * 128:(m + 1) * 128], mT[:, kt, :], kt == 0, kt == 7, [bWo, bmT], [bp1])
            h_, bh_ = ho[m % 2], bho[m % 2]
            k.tt("dve", h_[:], p1[:], hT[:, m, :], ALU.add, [bp1, bhT], [bh_])
            k.dma("pool", self.S_h2[m * 128:(m + 1) * 128, o0:o0 + T], h_[:], [bh_], [self.bS(self.S_h2, o0)])


def core_inputs_cc(xs, qi, jobs):
    xo, cos, sin = [], [], []
    for x, (L, LO) in zip(xs, jobs):
        xo.append(x[qi * LO:(qi + 1) * LO])
        c, s = _rope_tables(L, 0)
        cos.append(c[:, qi * LO:(qi + 1) * LO])
        sin.append(s[:, qi * LO:(qi + 1) * LO])
    m = np.zeros(64, np.float32)
    for r in range(4):
        m[0 + r] = 1.0 if r == qi - 1 else 0.0
        m[4 + r] = 1.0 if r == qi + 1 else 0.0
        m[8 + r] = 1.0 if r < qi else 0.0
        m[12 + r] = 1.0 if r > qi else 0.0
        for r2 in range(4):
            m[16 + r * 4 + r2] = 1.0 if r < r2 < qi else 0.0
            m[32 + r * 4 + r2] = 1.0 if qi < r2 < r else 0.0
    return dict(x=np.ascontiguousarray(np.concatenate(xo, 0)), cosT=np.ascontiguousarray(np.concatenate(cos, 1)),
                sinT=np.ascontiguousarray(np.concatenate(sin, 1)), ccmask=_rep(m))


JOBS = [(16384, 4096), (8192, 2048)]
_CACHE = {}


def kernel(x_prompt, x_sample, **Wt):
    x_prompt = np.asarray(x_prompt, np.float32)
    x_sample = np.asarray(x_sample, np.float32)
    prog = Prog(JOBS)
    nc = prog.build()
    shared = const_inputs(Wt, prog.jobs, prog.TB)
    in_maps = []
    for c in range(8):
        p, qi = c // 4, c % 4
        m = dict(shared)
        if prog.cc:
            m.update(core_inputs_cc([x_prompt[p], x_sample[p]], qi, prog.jobs))
        else:
            m.update(core_inputs([x_prompt[p], x_sample[p]], qi, prog.jobs, prog.TB))
        in_maps.append({k_: v_ for k_, v_ in m.items() if k_ in prog.I})
    res = run_bass_kernel_spmd(nc, in_maps, core_ids=list(range(8)))
    y_p = np.empty(x_prompt.shape, np.float32)
    y_s = np.empty(x_sample.shape, np.float32)
    (L0, O0), (L1, O1) = JOBS
    for c in range(8):
        p, qi = c // 4, c % 4
        y = np.asarray(res.results[c]["y"], np.float32)
        y_p[p, qi * O0:(qi + 1) * O0] = y[0:O0]
        y_s[p, qi * O1:(qi + 1) * O1] = y[O0:O0 + O1]
    return (y_p, y_s)


@_add_methods(Prog)
def ssd2_phase(self, ph, LO, off, job, mode):
    nc, k, I = self.nc, self.k, self.I
    NO = LO // 128
    nj = len(self.jobs)
    cs = ph.enter_context(self.sbt("ssdc", [128, 240], F32))
    tri = ph.enter_context(self.sbt("tri", [128, 2, 128], F32))
    bcs = k.buf()
    k.dma("sp", cs[:, 0:64], I["dtbias"][:, :], [], [bcs])
    k.dma("sp", cs[:, 64:128], I["alog"][:, :], [], [bcs])
    k.dma("sp", cs[:, 128:160], I["dskip"][:, :], [], [bcs])
    k.dma("sp", cs[:, 160:224], I["ccmask"][:, :], [], [bcs])
    k.dma("sp", tri[:], I["tri"].rearrange("p (a l) -> p a l", a=2), [], [bcs])
    k.act(cs[:, 64:128], cs[:, 64:128], AF.Exp, [bcs], [bcs])
    k.ts("dve", cs[:, 64:128], cs[:, 64:128], -1.0, None, ALU.mult, None, [bcs], [bcs])
    dtb, Aneg, Drep, cmk = cs[:, 0:64], cs[:, 64:128], cs[:, 128:160], cs[:, 160:224]
    names = ("dtq", "aq", "nacq", "eAq", "cdq", "wdq", "wlq", "t1q", "t2q")
    Q = {n: ph.enter_context(self.sbt(n, [128, NO, 32], F32)) for n in names}
    bQ = k.buf()
    Tq = ph.enter_context(self.sbt("Tq", [128, 64], F32))
    bTq = k.buf()
    NB = 4 if mode == "local" else 2
    xts = [ph.enter_context(self.sbt(f"xt{i}", [128, 32, 64], BF16)) for i in range(NB)]
    Bts = [ph.enter_context(self.sbt(f"Bt{i}", [128, 512], BF16)) for i in range(NB)]
    bin_ = [k.buf() for _ in range(NB)]
    xdtd = [ph.enter_context(self.sbt(f"xdtd{i}", [128, 32, 64], BF16)) for i in range(NB)]
    bxdtd = [k.buf() for _ in range(NB)]
    hst = ph.enter_context(self.sbt("hstate", [128, 2048], F32))
    bhst = k.buf()

    def flat(t):
        return t[:].rearrange("p n h -> p (n h)")

    def prep(d):
        W = NO * 32
        dsl = slice(d * 32, (d + 1) * 32)
        deps = [self.bS(self.S_dt, off + t) for t in range(0, LO, 512)]
        k.dma("sp", Q["dtq"][:], self.S_dt[off:off + LO, dsl].rearrange("(n p) h -> p n h", p=128), deps, [bQ])
        k.tt("dve", Q["dtq"][:], Q["dtq"][:], bc(dtb[:, dsl].unsqueeze(1), [128, NO, 32]), ALU.add, [bQ, bcs], [bQ])
        k.act(flat(Q["t1q"]), flat(Q["dtq"]), AF.Exp, [bQ], [bQ])
        k.act(flat(Q["dtq"]), flat(Q["t1q"]), AF.Ln, [bQ], [bQ], bias=self.oneT)
        k.tt("dve", Q["aq"][:], Q["dtq"][:], bc(Aneg[:, dsl].unsqueeze(1), [128, NO, 32]), ALU.mult, [bQ, bcs], [bQ])
        for c0 in range(0, W, 512):
            w = min(512, W - c0)
            ps, bps = self.psum()
            k.mm(ps[:, 0:w], tri[:, d, :], flat(Q["aq"])[:, c0:c0 + w], True, True, [bcs, bQ], [bps])
            k.copy("act", flat(Q["t1q"])[:, c0:c0 + w], ps[:, 0:w], [bps], [bQ])
            ps, bps = self.psum()
            k.mm(ps[:, 0:w], self.onesF, flat(Q["aq"])[:, c0:c0 + w], True, True, [self.cB, bQ], [bps])
            k.copy("dve", flat(Q["t2q"])[:, c0:c0 + w], ps[:, 0:w], [bps], [bQ])
        k.act(flat(Q["nacq"]), flat(Q["dtq"]), AF.Ln, [bQ], [bQ])
        k.tt("dve", flat(Q["nacq"]), flat(Q["nacq"]), flat(Q["t1q"]), ALU.subtract, [bQ], [bQ])
        k.act(flat(Q["eAq"]), flat(Q["t1q"]), AF.Exp, [bQ], [bQ])
        k.act(flat(Q["cdq"]), flat(Q["t2q"]), AF.Exp, [bQ], [bQ])
        k.tt("dve", flat(Q["t1q"]), flat(Q["t2q"]), flat(Q["t1q"]), ALU.subtract, [bQ], [bQ])
        k.act(flat(Q["wdq"]), flat(Q["t1q"]), AF.Exp, [bQ], [bQ])
        k.tt("dve", flat(Q["wdq"]), flat(Q["wdq"]), flat(Q["dtq"]), ALU.mult, [bQ], [bQ])
        if mode == "local":
            order = list(range(NO - 1, -1, -1)) if d == 0 else list(range(NO))
            k.memset("dve", Q["t1q"][:, order[0], :], 0.0, [bQ])
            for a_, b_ in zip(order[:-1], order[1:]):
                k.tt("dve", Q["t1q"][:, b_, :], Q["t1q"][:, a_, :], Q["t2q"][:, a_, :], ALU.add, [bQ], [bQ])
            last = order[-1]
            k.tt("dve", Tq[:, dsl], Q["t1q"][:, last, :], Q["t2q"][:, last, :], ALU.add, [bQ], [bTq])
            k.act(flat(Q["wlq"]), flat(Q["t1q"]), AF.Exp, [bQ], [bQ])
            k.tt("dve", flat(Q["wlq"]), flat(Q["wlq"]), flat(Q["wdq"]), ALU.mult, [bQ], [bQ])

    st = {"n": 0, "nf": 0}

    def load_chunk(c, full, bufs=None):
        i = st["n"] % NB
        st["n"] += 1
        g = off + c * 128
        xt, Bt, bi = xts[i], Bts[i], bin_[i]
        deps = [self.bS(self.S_xtok, g), self.bS(self.S_Btok, g), self.bS(self.S_bcT, g)]
        k.dma("sp", xt[:].rearrange("p h q -> p (h q)"), self.S_xtok[g:g + 128, :], deps, [bi])
        k.dma("sp", Bt[:], self.S_Btok[g:g + 128, :], deps, [bi])
        if full:
            k.dma("sp", bufs[i][:], self.S_bcT[:, g:g + 128].rearrange("(r p) t -> p r t", p=128), deps, [bi])
        return i, xt, Bt, bi

    if mode == "local":
        Sloc = [ph.enter_context(self.sbt(f"Sloc{d}", [128, 2048], F32)) for d in range(2)]
        bSl = [k.buf(), k.buf()]
        wl2 = ph.enter_context(self.sbt("wlq2", [128, NO, 32], F32))
        prep(0)
        k.copy("pool", wl2[:], Q["wlq"][:], [bQ], [bQ])
        prep(1)
        wls = [wl2, Q["wlq"]]
        xd2 = [ph.enter_context(self.sbt(f"xdtdb{i}", [128, 32, 64], BF16)) for i in range(NB)]
        bxd2 = [k.buf() for _ in range(NB)]
        for c in range(NO):
            i, xt, Bt, bi = load_chunk(c, False)
            for d in range(2):
                xd, bxd = (xdtd[i], bxdtd[i]) if d == 0 else (xd2[i], bxd2[i])
                k.tt("pool" if (c + d) % 2 == 0 else "dve", xd[:], xt[:], bc(wls[d][:, c, :].unsqueeze(2), [128, 32, 64]),
                     ALU.mult, [bi, bQ], [bxd])
                for g4 in range(4):
                    k.mm(self.ps[4 * d + g4][:], Bt[:, g4 * 128:(g4 + 1) * 128],
                         xd[:, g4 * 8:(g4 + 1) * 8, :].rearrange("p e q -> p (e q)"), c == 0, c == NO - 1,
                         [bi, bxd], [self.psb[4 * d + g4]])
        for d in range(2):
            for g4 in range(4):
                k.copy("act" if g4 % 2 == 0 else "dve", Sloc[d][:, g4 * 512:(g4 + 1) * 512], self.ps[4 * d + g4][:],
                       [self.psb[4 * d + g4]], [bSl[d]])
            sh = self.St_in[job][d]
            k.dma("pool", sh.ap()[:, :], Sloc[d][:], [bSl[d]], [self.ccb(sh)])
        k.dma("pool", self.T_in.ap()[:, job * 64:(job + 1) * 64], Tq[:], [bTq], [self.ccb(self.T_in)])
        return

    normrep = ph.enter_context(self.sbt("normrep", [128, 2048], F32))
    negf = ph.enter_context(self.sbt("negf", [128, 2, 512], F32))
    negm = ph.enter_context(self.sbt("negm", [128, 2, 512], BF16))
    k.dma("sp", normrep[:], I["ssmnorm"][:, :], [], [bcs])
    k.dma("sp", negf[:], I["negm"].rearrange("p (a l) -> p a l", a=2), [], [bcs])
    k.copy("dve", negm[:], negf[:], [bcs], [bcs])
    Tg = ph.enter_context(self.sbt("Tg", [128, 4, 64 * nj], F32))
    cf = ph.enter_context(self.sbt("cf", [128, 4, 32], F32))
    bTg, bcf = k.buf(), k.buf()
    k.dma("sp", Tg[:], self.T_out.ap().rearrange("(r p) f -> p r f", p=128), [self.ccb(self.T_out)], [bTg])
    hpb = ph.enter_context(self.sbt("hprevb", [128, 2048], BF16))
    bhpb = k.buf()
    bcTs = [ph.enter_context(self.sbt(f"bcT{i}", [128, 8, 128], BF16)) for i in range(2)]
    Rt = ph.enter_context(self.sbt("Rt", [128, 32, 128], F32))
    Lt = ph.enter_context(self.sbt("Lt", [128, 32, 128], F32))
    MT = ph.enter_context(self.sbt("MT", [128, 32, 128], BF16))
    MT2 = ph.enter_context(self.sbt("MT2", [128, 32, 128], BF16))
    bR, bLt, bMT = k.buf(), k.buf(), k.buf()
    yacc = [ph.enter_context(self.sbt(f"yacc{i}", [128, 2048], F32)) for i in range(2)]
    byacc = [k.buf(), k.buf()]
    aux = [ph.enter_context(self.sbt(f"yaux{i}", [128, 2048], F32)) for i in range(2)]
    baux = [k.buf(), k.buf()]
    tmp2 = [ph.enter_context(self.sbt(f"ytmp{i}", [128, 512], F32)) for i in range(2)]
    btmp2 = [k.buf(), k.buf()]
    ssq = ph.enter_context(self.sbt("ssq", [128, 8], F32))
    bssq = k.buf()
    sbf = ph.enter_context(self.sbt("sbf", [128, 2048], BF16))
    sTs = [ph.enter_context(self.sbt(f"sTs{i}", [128, 16, 128], BF16)) for i in range(2)]
    bsbf = k.buf()
    bsTs = [k.buf(), k.buf()]

    def combine(d):
        dsl = slice(job * 64 + d * 32, job * 64 + (d + 1) * 32)
        for r in range(4):
            first = True
            for r2 in range(4):
                if (d == 0 and not (r < r2)) or (d == 1 and not (r2 < r)):
                    continue
                mcol = 16 + d * 16 + r * 4 + r2
                if first:
                    k.ts("dve", cf[:, r, :], Tg[:, r2, dsl], cmk[:, mcol:mcol + 1], None, ALU.mult, None,
                         [bTg, bcs], [bcf])
                    first = False
                else:
                    k.stt(cf[:, r, :], Tg[:, r2, dsl], cmk[:, mcol:mcol + 1], cf[:, r, :], ALU.mult, ALU.add,
                          [bTg, bcs, bcf], [bcf])
            if first:
                k.memset("dve", cf[:, r, :], 0.0, [bcf])
        k.act(cf[:].rearrange("p r h -> p (r h)"), cf[:].rearrange("p r h -> p (r h)"), AF.Exp, [bcf], [bcf])
        for r in range(4):
            k.ts("dve", cf[:, r, :], cf[:, r, :], cmk[:, 8 + d * 4 + r:9 + d * 4 + r], None, ALU.mult, None,
                 [bcf, bcs], [bcf])
        so = self.St_out[job][d]
        for r in range(4):
            ax, bax = aux[r % 2], baux[r % 2]
            k.dma("sp", ax[:], so.ap()[r * 128:(r + 1) * 128, :], [self.ccb(so)], [bax])
            cb_ = bc(cf[:, r, :].unsqueeze(2), [128, 32, 64])
            av = ax[:].rearrange("p (h q) -> p h q", q=64)
            hv = hst[:].rearrange("p (h q) -> p h q", q=64)
            if r == 0:
                k.tt("dve", hv, av, cb_, ALU.mult, [bax, bcf], [bhst])
            else:
                k.tt("pool", av, av, cb_, ALU.mult, [bax, bcf], [bax])
                k.tt("dve", hst[:], hst[:], ax[:], ALU.add, [bhst, bax], [bhst])

    MTs = [MT, MT2]
    bMTs = [bMT, k.buf()]
    ctx = {}

    def stageA(c, d):
        i, xt, Bt, bi = load_chunk(c, True, bcTs)
        bcT = bcTs[i]
        bs = bQ
        a_, nac_, wd_ = (Q[n][:, c, :] for n in ("aq", "nacq", "wdq"))
        xd, bxd = xdtd[i], bxdtd[i]
        k.tt("pool", xd[:], xt[:], bc(wd_.unsqueeze(2), [128, 32, 64]), ALU.mult, [bi, bs], [bxd])
        j = st["nf"] % 2
        st["nf"] += 1
        ax, bax = aux[j], baux[j]
        mt, bmt = MTs[j], bMTs[j]
        oo = off + c * 128
        k.tt("pool", Rt[:], bc(a_.unsqueeze(2), [128, 32, 128]), bc(tri[:, d, :].unsqueeze(1), [128, 32, 128]),
             ALU.mult, [bs, bcs], [bR])
        if d == 0:
            k.tt("pool", ax[:].rearrange("p (h q) -> p h q", q=64), xt[:], bc(Drep.unsqueeze(2), [128, 32, 64]),
                 ALU.mult, [bi, bcs], [bax])
        else:
            k.dma("sp", ax[:], self.S_y[oo:oo + 128, :], [self.bS(self.S_y, oo)], [bax])
        ctx[c] = (i, j)

    def stageA2(c, d):
        i, j = ctx[c]
        bi, bcT = bin_[i], bcTs[i]
        bs = bQ
        nac_ = Q["nacq"][:, c, :]
        mt, bmt = MTs[j], bMTs[j]
        for hg in range(8):
            ps, bps = self.psum()
            k.mm(ps[:], self.identB, negm[:, d, :], True, False, [self.cB, bcs], [bps])
            k.mm(ps[:], self.onesF, Rt[:, hg * 4:(hg + 1) * 4, :].rearrange("p h l -> p (h l)"), False, True,
                 [self.cB, bR], [bps])
            for hh in range(4):
                h = hg * 4 + hh
                k.act(Lt[:, h, :], ps[:, hh * 128:(hh + 1) * 128], AF.Exp, [bps, bs], [bLt], bias=nac_[:, h:h + 1])
        pcb, bpcb = self.psum()
        for g4 in range(4):
            k.mm(pcb[:, g4 * 128:(g4 + 1) * 128], bcT[:, g4, :], bcT[:, 4 + g4, :], True, True, [bi], [bpcb])
        for g4 in range(4):
            k.tt("dve", mt[:, g4 * 8:(g4 + 1) * 8, :], Lt[:, g4 * 8:(g4 + 1) * 8, :],
                 bc(pcb[:, g4 * 128:(g4 + 1) * 128].unsqueeze(1), [128, 8, 128]), ALU.mult, [bLt, bpcb], [bmt])

    def stageB(c, d):
        i, j = ctx[c]
        xt, Bt, bi, bcT = xts[i], Bts[i], bin_[i], bcTs[i]
        xd, bxd = xdtd[i], bxdtd[i]
        mt, bmt = MTs[j], bMTs[j]
        ya, bya = yacc[j], byacc[j]
        bs = bQ
        eA_, cd_ = Q["eAq"][:, c, :], Q["cdq"][:, c, :]
        k.copy("act", hpb[:], hst[:], [bhst], [bhpb])
        for g4 in range(4):
            psy, bpsy = self.psum()
            for e in range(8):
                h = g4 * 8 + e
                k.mm(psy[:, e * 64:(e + 1) * 64], mt[:, h, :], xt[:, h, :], True, True, [bmt, bi], [bpsy])
            pso, bpso = self.psum()
            k.mm(pso[:], bcT[:, 4 + g4, :], hpb[:, g4 * 512:(g4 + 1) * 512], True, True, [bi, bhpb], [bpso])
            t2, bt2 = tmp2[g4 % 2], btmp2[g4 % 2]
            k.tt("dve", t2[:].rearrange("p (e q) -> p e q", q=64), pso[:].rearrange("p (e q) -> p e q", q=64),
                 bc(eA_[:, g4 * 8:(g4 + 1) * 8].unsqueeze(2), [128, 8, 64]), ALU.mult, [bpso, bs], [bt2])
            k.tt("dve", ya[:, g4 * 512:(g4 + 1) * 512], psy[:], t2[:], ALU.add, [bpsy, bt2], [bya])
        for g4 in range(4):
            psS, bpsS = self.psum()
            k.mm(psS[:], Bt[:, g4 * 128:(g4 + 1) * 128], xd[:, g4 * 8:(g4 + 1) * 8, :].rearrange("p e q -> p (e q)"),
                 True, True, [bi, bxd], [bpsS])
            hv = hst[:, g4 * 512:(g4 + 1) * 512].rearrange("p (e q) -> p e q", q=64)
            k.tt("dve", hv, hv, bc(cd_[:, g4 * 8:(g4 + 1) * 8].unsqueeze(2), [128, 8, 64]), ALU.mult, [bhst, bs], [bhst])
            k.tt("dve", hst[:, g4 * 512:(g4 + 1) * 512], hst[:, g4 * 512:(g4 + 1) * 512], psS[:], ALU.add,
                 [bhst, bpsS], [bhst])

    def stageC(c, d):
        i, j = ctx.pop(c)
        ya, bya = yacc[j], byacc[j]
        ax, bax = aux[j], baux[j]
        oo = off + c * 128
        k.tt("pool", ya[:], ya[:], ax[:], ALU.add, [bya, bax], [bya])
        if d == 0:
            k.dma("pool", self.S_y[oo:oo + 128, :], ya[:], [bya], [self.bS(self.S_y, oo)])
            return
        k.dma("sp", ax[:], self.S_sz[oo:oo + 128, :], [self.bS(self.S_sz, oo)], [bax])
        k.tt("pool", ya[:], ya[:], ax[:], ALU.mult, [bya, bax], [bya])
        k.act(ax[:], ya[:], AF.Square, [bya], [bax])
        k.op("dve", lambda e, o_=ssq[:, 0:4], i_=ax[:].rearrange("p (g c) -> p g c", g=4):
             e.tensor_reduce(out=o_, in_=i_, axis=AX.X, op=ALU.add), [bax], [bssq])
        k.act(ssq[:, 4:8], ssq[:, 0:4], AF.Sqrt, [bssq, self.cB], [bssq], bias=self.epsT, scale=1.0 / 512)
        k.recip(ssq[:, 4:8], ssq[:, 4:8], [bssq], [bssq])
        for g4 in range(4):
            k.stt(sbf[:, g4 * 512:(g4 + 1) * 512], ya[:, g4 * 512:(g4 + 1) * 512], ssq[:, 4 + g4:5 + g4],
                  normrep[:, g4 * 512:(g4 + 1) * 512], ALU.mult, ALU.mult, [bya, bssq, bcs], [bsbf])
        sT, bsT = sTs[j], bsTs[j]
        for hb in range(2):
            ps, bps = self.psum()
            psb = ps[:].bitcast(BF16)
            for q in range(8):
                ct = hb * 8 + q
                k.tr(psb[:, q * 128:(q + 1) * 128], sbf[:, ct * 128:(ct + 1) * 128], self.identB,
                     [bsbf, self.cB], [bps])
            k.copy("act", sT[:, hb * 8:(hb + 1) * 8, :], psb.rearrange("p (c t) -> p c t", c=8), [bps], [bsT])
        k.dma("pool", self.S_sT[:, oo:oo + 128].rearrange("(c p) t -> p c t", p=128), sT[:], [bsT],
              [self.bS(self.S_sT, oo)])

    def run_pass(order, d):
        stageA(order[0], d)
        stageA2(order[0], d)
        for n_, c in enumerate(order):
            nxt = order[n_ + 1] if n_ + 1 < len(order) else None
            if nxt is not None:
                stageA(nxt, d)
            stageB(c, d)
            if nxt is not None:
                stageA2(nxt, d)
            stageC(c, d)

    prep(0)
    combine(0)
    run_pass(list(range(NO)), 0)
    prep(1)
    combine(1)
    run_pass(list(range(NO - 1, -1, -1)), 1)
```
